# Optimizing a Trainium2 kernel written in Bass

```python
import jax
import jax.numpy as jnp
from jax import lax
import numpy as np

D_MODEL = 2048
BATCH = 1
SEQ = 8192
DEPTH = 4

GRID_W = 64
RET_HEADS = 4
RET_HEAD_DIM = 256
RET_WIDTH = RET_HEADS * RET_HEAD_DIM
RET_CHUNK = 128
NA_HEADS = 8
NA_HEAD_DIM = 128
NA_WIDTH = NA_HEADS * NA_HEAD_DIM
NA_KH = 8
NA_KW = 16
LRU_WIDTH = 1024
LRU_BLOCKS = 8
LRU_BLOCK_DIM = LRU_WIDTH // LRU_BLOCKS
LRU_CONV = 4
LRU_C = 8.0
N_BRANCH = 3
BRANCH_WIDTH = 1024
IN_COLS = 4 * RET_WIDTH + 3 * NA_WIDTH + 2 * LRU_WIDTH + N_BRANCH * D_MODEL
D_FF = -(-8 * D_MODEL // (3 * 256)) * 256
DEEPNORM_ALPHA = (2 * DEPTH) ** 0.25
DEEPNORM_BETA = (8 * DEPTH) ** -0.25
LN_EPS = 1e-5
ROPE_BASE = 10000.0

kernel_name = 'hybrid_retention_natten_rglru_encoder'


def layer_norm(x, g, b):
    xf = x.astype(jnp.float32)
    mu = jnp.mean(xf, axis=-1, keepdims=True)
    var = jnp.mean(jnp.square(xf - mu), axis=-1, keepdims=True)
    y = (xf - mu) * lax.rsqrt(var + LN_EPS)
    return (y * g.astype(jnp.float32) + b.astype(jnp.float32)).astype(x.dtype)


def split_columns(proj):
    sizes = [RET_WIDTH] * 4 + [NA_WIDTH] * 3 + [LRU_WIDTH] * 2 + [N_BRANCH * D_MODEL]
    parts = []
    start = 0
    for w in sizes:
        parts.append(proj[..., start:start + w])
        start += w
    return parts


def rotary(t, pos):
    half = t.shape[-1] // 2
    inv = ROPE_BASE ** (-jnp.arange(half, dtype=jnp.float32) / half)
    ang = pos[:, None] * inv[None, :]
    cos = jnp.cos(ang)[None, :, None, :]
    sin = jnp.sin(ang)[None, :, None, :]
    t1 = t[..., :half].astype(jnp.float32)
    t2 = t[..., half:].astype(jnp.float32)
    return jnp.concatenate([t1 * cos - t2 * sin, t1 * sin + t2 * cos], axis=-1)


def decay_masks(log_g, strict):
    idx = jnp.arange(RET_CHUNK, dtype=jnp.float32)
    diff = idx[:, None] - idx[None, :]
    keep = (diff > 0) if strict else (diff >= 0)
    expo = jnp.where(keep, diff, 0.0)[None] * log_g[:, None, None]
    inner = jnp.where(keep[None], jnp.exp(expo), 0.0)
    q_dec = jnp.exp((idx + 1.0)[None, :] * log_g[:, None])
    k_dec = jnp.exp((RET_CHUNK - 1.0 - idx)[None, :] * log_g[:, None])
    c_dec = jnp.exp(RET_CHUNK * log_g)
    return inner, q_dec, k_dec, c_dec


def retention_one_direction(q, k, v, log_g, strict):
    b, h, s, d = q.shape
    nc = s // RET_CHUNK
    q = q.reshape(b, h, nc, RET_CHUNK, d)
    k = k.reshape(b, h, nc, RET_CHUNK, d)
    v = v.reshape(b, h, nc, RET_CHUNK, v.shape[-1])
    inner, q_dec, k_dec, c_dec = decay_masks(log_g, strict)
    scores = jnp.einsum('bhncd,bhnmd->bhncm', q, k) * inner[None, :, None]
    intra = jnp.einsum('bhncm,bhnme->bhnce', scores, v)
    kv = jnp.einsum('bhncd,bhnce->nbhde', k * k_dec[None, :, None, :, None], v)
    decay = c_dec[None, :, None, None]

    def step(state, kv_n):
        return decay * state + kv_n, state

    _, prev = lax.scan(step, jnp.zeros_like(kv[0]), kv)
    inter = jnp.einsum('bhncd,nbhde->bhnce', q * q_dec[None, :, None, :, None], prev)
    return (intra + inter).reshape(b, h, s, -1)


def retention_branch(q, k, v, g, decay_logits):
    b, s, _ = q.shape
    pos = jnp.arange(s, dtype=jnp.float32)
    heads = lambda t: t.reshape(b, s, RET_HEADS, RET_HEAD_DIM)
    to_bhsd = lambda t: jnp.transpose(t, (0, 2, 1, 3)).astype(jnp.float32)
    qh = to_bhsd(rotary(heads(q), pos))
    kh = to_bhsd(rotary(heads(k), pos) * (RET_HEAD_DIM ** -0.5))
    vh = to_bhsd(heads(v))
    log_g = jax.nn.log_sigmoid(decay_logits.astype(jnp.float32))
    flip = lambda t: jnp.flip(t, axis=2)
    fwd = retention_one_direction(qh, kh, vh, log_g[0], strict=False)
    bwd = flip(retention_one_direction(flip(qh), flip(kh), flip(vh), log_g[1], strict=True))
    y = fwd + bwd
    mu = jnp.mean(y, axis=-1, keepdims=True)
    var = jnp.mean(jnp.square(y - mu), axis=-1, keepdims=True)
    y = (y - mu) * lax.rsqrt(var + LN_EPS)
    y = jnp.transpose(y, (0, 2, 1, 3)).reshape(b, s, RET_WIDTH)
    return (jax.nn.silu(g.astype(jnp.float32)) * y).astype(g.dtype)


def neighborhood_attention_branch(q, k, v, rpb):
    b, s, _ = q.shape
    rows = s // GRID_W
    kh = min(NA_KH, rows)
    kw = NA_KW
    grid = lambda t: jnp.transpose(t.reshape(b, rows, GRID_W, NA_HEADS, NA_HEAD_DIM), (0, 3, 1, 2, 4))
    qg, kg, vg = grid(q), grid(k), grid(v)
    r = jnp.arange(rows)
    row_start = jnp.clip(r - kh // 2, 0, rows - kh)
    key_rows = row_start[:, None] + jnp.arange(kh)[None, :]
    k_strip = kg[:, :, key_rows]
    v_strip = vg[:, :, key_rows]
    c = jnp.arange(GRID_W)
    col_start = jnp.clip(c - kw // 2, 0, GRID_W - kw)
    in_win = (c[None, :] >= col_start[:, None]) & (c[None, :] < col_start[:, None] + kw)
    dr = key_rows - r[:, None] + (NA_KH - 1)
    dc = jnp.clip(c[None, :] - c[:, None], -(kw - 1), kw - 1) + (kw - 1)
    bias = rpb.astype(jnp.float32)[:, dr[:, None, :, None], dc[None, :, None, :]]
    scores = jnp.einsum('bhrcd,bhrkwd->bhrckw', qg, k_strip).astype(jnp.float32) * (NA_HEAD_DIM ** -0.5)
    scores = jnp.where(in_win[:, None, :], scores + bias[None], -jnp.inf)
    p = jax.nn.softmax(scores, axis=(-2, -1)).astype(v.dtype)
    out = jnp.einsum('bhrckw,bhrkwd->bhrcd', p, v_strip)
    return jnp.transpose(out, (0, 2, 3, 1, 4)).reshape(b, s, NA_WIDTH)


def linear_combine(left, right):
    a_l, b_l = left
    a_r, b_r = right
    return a_l * a_r, a_r * b_l + b_r


def rglru_branch(xb, yb, w_conv, b_conv, wa, ba, wi, bi, lam):
    b, s, _ = xb.shape
    xc = lax.conv_general_dilated(
        xb, w_conv.astype(xb.dtype)[:, None, :], window_strides=(1,),
        padding=[(LRU_CONV // 2, LRU_CONV - 1 - LRU_CONV // 2)],
        dimension_numbers=('NWC', 'WIO', 'NWC'), feature_group_count=LRU_WIDTH) + b_conv
    xf = xc.astype(jnp.float32)
    xblk = xf.reshape(b, s, LRU_BLOCKS, LRU_BLOCK_DIM)

    def direction(d, reverse):
        r_gate = jax.nn.sigmoid(jnp.einsum('bsni,nij->bsnj', xblk, wa[d].astype(jnp.float32)).reshape(b, s, LRU_WIDTH)
                                + ba[d].astype(jnp.float32))
        i_gate = jax.nn.sigmoid(jnp.einsum('bsni,nij->bsnj', xblk, wi[d].astype(jnp.float32)).reshape(b, s, LRU_WIDTH)
                                + bi[d].astype(jnp.float32))
        log_a = -LRU_C * r_gate * jax.nn.softplus(-lam[d].astype(jnp.float32))
        a = jnp.exp(log_a)
        inp = jnp.sqrt(-jnp.expm1(2.0 * log_a)) * (i_gate * xf)
        _, h = lax.associative_scan(linear_combine, (a, inp), axis=1, reverse=reverse)
        return h

    h = direction(0, False) + direction(1, True)
    return (h * jax.nn.gelu(yb.astype(jnp.float32))).astype(xb.dtype)


def hybrid_mixer(x, w_in, gate_b, ret_decay, w_conv, b_conv, lru_wa, lru_ba, lru_wi, lru_bi,
                 lru_lambda, na_rpb, w_branch, w_out):
    b, s, _ = x.shape
    proj = jnp.einsum('bsd,dc->bsc', x, w_in)
    rq, rk, rv, rg, nq, nk, nv, lx, ly, gate_pre = split_columns(proj)
    ret = retention_branch(rq, rk, rv, rg, ret_decay)
    na = neighborhood_attention_branch(nq, nk, nv, na_rpb)
    lru = rglru_branch(lx, ly, w_conv, b_conv, lru_wa, lru_ba, lru_wi, lru_bi, lru_lambda)
    branches = jnp.stack([ret, na, lru], axis=2)
    up = jnp.einsum('bsni,nid->bsnd', branches, w_branch)
    gates = jax.nn.sigmoid(gate_pre + gate_b).reshape(b, s, N_BRANCH, D_MODEL)
    merged = jnp.sum(gates * up, axis=2)
    return jnp.einsum('bsd,de->bse', merged, w_out)


def swiglu(x, w_ffn_in, w_ffn_out):
    h = jnp.einsum('bsd,df->bsf', x, w_ffn_in)
    gate, val = h[..., :D_FF], h[..., D_FF:]
    return jnp.einsum('bsf,fd->bsd', jax.nn.silu(gate) * val, w_ffn_out)


def setup_inputs(seed: int = 0) -> dict:
    key = jax.random.key(seed)
    ks = jax.random.split(key, 22)
    nrm = lambda k, shape, scale: scale * jax.random.normal(k, shape, jnp.float32)
    gamma0 = 1.0 - 2.0 ** (-5.0 - jnp.arange(RET_HEADS, dtype=jnp.float32))
    decay_logit = jnp.log(gamma0) - jnp.log1p(-gamma0)
    a8 = jax.random.uniform(ks[12], (DEPTH, 2, LRU_WIDTH), jnp.float32, 0.9, 0.999)
    a = a8 ** (1.0 / LRU_C)
    return {
        'x': nrm(ks[0], (BATCH, SEQ, D_MODEL), 1.0),
        'ln_in_g': 1.0 + nrm(ks[1], (D_MODEL,), 0.02),
        'ln_in_b': nrm(ks[2], (D_MODEL,), 0.02),
        'w_in': nrm(ks[3], (DEPTH, D_MODEL, IN_COLS), D_MODEL ** -0.5),
        'gate_b': nrm(ks[4], (DEPTH, N_BRANCH * D_MODEL), 0.1),
        'ret_decay': decay_logit + nrm(ks[5], (DEPTH, 2, RET_HEADS), 0.05),
        'w_conv': nrm(ks[6], (DEPTH, LRU_CONV, LRU_WIDTH), LRU_CONV ** -0.5),
        'b_conv': nrm(ks[7], (DEPTH, LRU_WIDTH), 0.02),
        'lru_wa': nrm(ks[8], (DEPTH, 2, LRU_BLOCKS, LRU_BLOCK_DIM, LRU_BLOCK_DIM), LRU_BLOCK_DIM ** -0.5),
        'lru_ba': nrm(ks[9], (DEPTH, 2, LRU_WIDTH), 0.02),
        'lru_wi': nrm(ks[10], (DEPTH, 2, LRU_BLOCKS, LRU_BLOCK_DIM, LRU_BLOCK_DIM), LRU_BLOCK_DIM ** -0.5),
        'lru_bi': nrm(ks[11], (DEPTH, 2, LRU_WIDTH), 0.02),
        'lru_lambda': jnp.log(a) - jnp.log1p(-a),
        'na_rpb': nrm(ks[13], (DEPTH, NA_HEADS, 2 * NA_KH - 1, 2 * NA_KW - 1), 0.1),
        'w_branch': nrm(ks[14], (DEPTH, N_BRANCH, BRANCH_WIDTH, D_MODEL), BRANCH_WIDTH ** -0.5),
        'w_out': nrm(ks[15], (DEPTH, D_MODEL, D_MODEL), DEEPNORM_BETA * D_MODEL ** -0.5),
        'ln1_g': 1.0 + nrm(ks[16], (DEPTH, D_MODEL), 0.02),
        'ln1_b': nrm(ks[17], (DEPTH, D_MODEL), 0.02),
        'w_ffn_in': nrm(ks[18], (DEPTH, D_MODEL, 2 * D_FF), D_MODEL ** -0.5),
        'w_ffn_out': nrm(ks[19], (DEPTH, D_FF, D_MODEL), DEEPNORM_BETA * D_FF ** -0.5),
        'ln2_g': 1.0 + nrm(ks[20], (DEPTH, D_MODEL), 0.02),
        'ln2_b': nrm(ks[21], (DEPTH, D_MODEL), 0.02),
    }


def reference(x, ln_in_g, ln_in_b, w_in, gate_b, ret_decay, w_conv, b_conv, lru_wa, lru_ba,
              lru_wi, lru_bi, lru_lambda, na_rpb, w_branch, w_out, ln1_g, ln1_b,
              w_ffn_in, w_ffn_out, ln2_g, ln2_b):
    h = layer_norm(x, ln_in_g, ln_in_b)
    for l in range(DEPTH):
        mix = hybrid_mixer(h, w_in[l], gate_b[l], ret_decay[l], w_conv[l], b_conv[l],
                           lru_wa[l], lru_ba[l], lru_wi[l], lru_bi[l], lru_lambda[l],
                           na_rpb[l], w_branch[l], w_out[l])
        h = layer_norm(DEEPNORM_ALPHA * h + mix, ln1_g[l], ln1_b[l])
        h = layer_norm(DEEPNORM_ALPHA * h + swiglu(h, w_ffn_in[l], w_ffn_out[l]), ln2_g[l], ln2_b[l])
    return h
```

```python
import contextlib
import numpy as np
import concourse.bass as bass
import concourse.mybir as mybir
from concourse.bass_utils import run_bass_kernel_spmd

F32 = mybir.dt.float32
BF16 = mybir.dt.bfloat16
AF = mybir.ActivationFunctionType
ALU = mybir.AluOpType
AX = mybir.AxisListType

NCORES = 8
D = 2048
SEQ = 8192
TPC = SEQ // NCORES
DEPTH = 4
IN_COLS = 15360
DFF = 5632
ALPHA = (2 * DEPTH) ** 0.25
EPS = 1e-5
GRID_W = 64


class Prog:
    ENGS = ("pe", "act", "dve", "pool", "sp")
    DMA_RING = 16

    def __init__(self, nc):
        self.nc = nc
        self.stack = contextlib.ExitStack()
        self.ops = {e: [] for e in self.ENGS}
        self.cnt = {e: 0 for e in self.ENGS}
        self.dcnt = {e: 0 for e in self.ENGS}
        self.known = {e: {} for e in self.ENGS}
        self.last_w = {}
        self.readers = {}
        self.sem = {e: self.stack.enter_context(nc.semaphore("s_" + e)) for e in self.ENGS}
        self.dsem = {}
        for q in ("sp", "pool", "act"):
            self.dsem[q] = [self.stack.enter_context(nc.semaphore(f"d_{q}{i}")) for i in range(self.DMA_RING)]
        self.out_tokens = []
        self._n = 0

    def sb(self, shape, dtype, name=None):
        self._n += 1
        return self.stack.enter_context(self.nc.sbuf_tensor("sb_" + (name or f"t{self._n}"), list(shape), dtype))

    def ps(self, name=None):
        self._n += 1
        return self.stack.enter_context(self.nc.psum_tensor(name or f"p{self._n}", [128, 512], F32))

    def _tok_sem(self, tok):
        if tok[0] == "e":
            return ("e", tok[1]), tok[2]
        return ("d", tok[1], tok[2] % self.DMA_RING), 16 * (tok[2] // self.DMA_RING + 1)

    def _waits(self, eng, reads, writes):
        toks = set()
        for r in reads:
            if r in self.last_w:
                toks.add(self.last_w[r])
        for w in writes:
            if w in self.last_w:
                toks.add(self.last_w[w])
            for t in self.readers.get(w, ()):
                toks.add(t)
        need = {}
        for t in toks:
            key, val = self._tok_sem(t)
            if self.known[eng].get(key, 0) >= val:
                continue
            need[key] = max(need.get(key, 0), val)
        for key, val in need.items():
            self.known[eng][key] = val
        return list(need.items())

    def _commit(self, tok, reads, writes):
        for r in reads:
            self.readers.setdefault(r, []).append(tok)
        for w in writes:
            self.last_w[w] = tok
            self.readers[w] = []

    def op(self, eng, fn, reads=(), writes=()):
        waits = self._waits(eng, reads, writes)
        self.cnt[eng] += 1
        tok = ("e", eng, self.cnt[eng])
        self.ops[eng].append(("op", fn, waits))
        self._commit(tok, reads, writes)
        return tok

    def dma(self, q, out, in_, reads=(), writes=(), **kw):
        idx = self.dcnt[q]
        waits = self._waits(q, reads, writes)
        if idx >= self.DMA_RING:
            key, val = self._tok_sem(("d", q, idx - self.DMA_RING))
            if self.known[q].get(key, 0) < val:
                self.known[q][key] = val
                waits.append((key, val))
        self.dcnt[q] += 1
        tok = ("d", q, idx)
        self.ops[q].append(("dma", (out, in_, kw, idx), waits))
        self._commit(tok, reads, writes)
        return tok

    def _semh(self, key):
        return self.sem[key[1]] if key[0] == "e" else self.dsem[key[1]][key[2]]

    def emit(self, final_tokens):
        nc = self.nc
        emap = {"pe": "tensor", "act": "scalar", "dve": "vector", "pool": "gpsimd", "sp": "sync"}
        fin = {}
        for t in final_tokens:
            key, val = self._tok_sem(t)
            fin[key] = max(fin.get(key, 0), val)
        with nc.Block() as block:
            for e in self.ENGS:
                def body(eng, e=e):
                    for kind, payload, waits in self.ops[e]:
                        for key, val in waits:
                            eng.wait_ge(self._semh(key), val)
                        if kind == "op":
                            ins = payload(eng)
                            ins.then_inc(self.sem[e], 1)
                        else:
                            out, in_, kw, idx = payload
                            eng.dma_start(out=out, in_=in_, **kw).then_inc(self.dsem[e][idx % self.DMA_RING], 16)
                    if e == "sp":
                        for key, val in fin.items():
                            eng.wait_ge(self._semh(key), val)
                getattr(block, emap[e])(body)
        self.stack.close()


class PsumRing:
    def __init__(self, P, n=8):
        self.P = P
        self.banks = [P.ps(f"bank{i}") for i in range(n)]
        self.i = 0

    def next(self):
        b = self.banks[self.i % len(self.banks)]
        k = ("psum", self.i % len(self.banks))
        self.i += 1
        return b, k


def load_consts(P, nc_in, name, shape_free, q="sp"):
    t = P.sb([128, shape_free], F32, name)
    P.dma(q, t[:, :], nc_in[:, :], writes=[name])
    return t


def layernorm_fm(P, PS, x, xkey, T, gam, bet, gkeys, ones, out32, out32key, out16, out16key, scr, scrkey, KC=16):
    nfeat = KC * 128
    xk = [xkey + (k,) for k in range(KC)]
    sk = [scrkey + (k,) for k in range(KC)]
    P.op("act", lambda e: e.activation(out=scr[:, 0:KC, 0:T], in_=x[:, 0:KC, 0:T], func=AF.Square),
         reads=xk, writes=sk)
    b_sum, k_sum = PS.next()
    b_sq, k_sq = PS.next()

    def mm_sum(e):
        for k in range(KC):
            ins = e.matmul(b_sum[:, 0:T], lhsT=ones[:, :], rhs=x[:, k, 0:T], start=(k == 0), stop=(k == KC - 1))
        return ins

    def mm_sq(e):
        for k in range(KC):
            ins = e.matmul(b_sq[:, 0:T], lhsT=ones[:, :], rhs=scr[:, k, 0:T], start=(k == 0), stop=(k == KC - 1))
        return ins
    P.op("pe", mm_sum, reads=xk + ["ones"], writes=[k_sum])
    P.op("pe", mm_sq, reads=sk + ["ones"], writes=[k_sq])
    if not hasattr(P, "_ln_tmp"):
        P._ln_tmp = (P.sb([128, 512], F32, "ln_mean"), P.sb([128, 512], F32, "ln_rstd"))
    mean, rstd = P._ln_tmp
    mk, rk = "ln_mean", "ln_rstd"
    P.op("act", lambda e: e.mul(out=mean[:, 0:T], in_=b_sum[:, 0:T], mul=1.0 / nfeat), reads=[k_sum], writes=[mk])
    P.op("dve", lambda e: e.tensor_tensor(out=rstd[:, 0:T], in0=mean[:, 0:T], in1=mean[:, 0:T], op=ALU.mult),
         reads=[mk], writes=[rk])
    P.op("dve", lambda e: e.scalar_tensor_tensor(out=rstd[:, 0:T], in0=b_sq[:, 0:T], scalar=1.0 / nfeat, in1=rstd[:, 0:T],
                                                 op0=ALU.mult, op1=ALU.subtract), reads=[k_sq, rk], writes=[rk])
    P.op("dve", lambda e: e.tensor_scalar(out=rstd[:, 0:T], in0=rstd[:, 0:T], scalar1=EPS, scalar2=None,
                                          op0=ALU.add), reads=[rk], writes=[rk])
    P.op("act", lambda e: e.activation(out=rstd[:, 0:T], in_=rstd[:, 0:T], func=AF.Sqrt), reads=[rk], writes=[rk])
    P.op("dve", lambda e: e.reciprocal(out=rstd[:, 0:T], in_=rstd[:, 0:T]), reads=[rk], writes=[rk])
    for k in range(KC):
        P.op("dve", lambda e, k=k: e.tensor_tensor(out=scr[:, k, 0:T], in0=x[:, k, 0:T], in1=mean[:, 0:T], op=ALU.subtract),
             reads=[xk[k], mk], writes=[sk[k]])
        P.op("dve", lambda e, k=k: e.tensor_tensor(out=scr[:, k, 0:T], in0=scr[:, k, 0:T], in1=rstd[:, 0:T], op=ALU.mult),
             reads=[sk[k], rk], writes=[sk[k]])
        P.op("act", lambda e, k=k: e.activation(out=out32[:, k, 0:T], in_=scr[:, k, 0:T], func=AF.Identity,
                                                scale=gam[:, k:k + 1], bias=bet[:, k:k + 1]),
             reads=[sk[k]] + list(gkeys), writes=[out32key + (k,)])
        P.op("pool", lambda e, k=k: e.tensor_copy(out=out16[:, k, 0:T], in_=out32[:, k, 0:T]),
             reads=[out32key + (k,)], writes=[out16key + (k,)])


def stream_matmul_fm(P, PS, w_dram, col0, ncols, KC, rhs16, rhs_keys, T, evac, wbufs, wname, WC=256, TH=512):
    assert ncols % 128 == 0
    c = 0
    gi = getattr(P, "_wcount", 0)
    while c < ncols:
        w = min(WC, ncols - c)
        wb = wbufs[gi % len(wbufs)]
        wk = (wname, gi % len(wbufs))
        gi += 1
        src = w_dram[:, col0 + c: col0 + c + w].rearrange("(k p) c -> p k c", p=128)
        P.dma("pool", wb[:, 0:KC, 0:w], src, writes=[wk])
        for cc in range(w // 128):
            for half in range(T // TH):
                bank, bk = PS.next()

                def mm(e, cc=cc, half=half, bank=bank, wb=wb):
                    for k in range(KC):
                        ins = e.matmul(bank[:, 0:TH], lhsT=wb[:, k, cc * 128:(cc + 1) * 128],
                                       rhs=rhs16[:, k, half * TH:(half + 1) * TH], start=(k == 0), stop=(k == KC - 1))
                    return ins
                P.op("pe", mm, reads=[wk] + list(rhs_keys), writes=[bk])
                evac((c // 128) + cc, half, bank, bk)
        c += w
    P._wcount = gi


def emit_inproj(P, PS, h16, h16keys, w_in, projT, wbufs, obufs, wname="w_in"):
    state = {"i": 0}
    toks = []

    def evac(ci, half, bank, bk):
        i = state["i"]
        state["i"] += 1
        ob = obufs[i % len(obufs)]
        ok = ("projo", i % len(obufs))
        eng = "act" if i % 2 == 0 else "dve"
        if eng == "act":
            P.op("act", lambda e: e.copy(out=ob[:, :], in_=bank[:, 0:512]), reads=[bk], writes=[ok])
        else:
            P.op("dve", lambda e: e.tensor_copy(out=ob[:, :], in_=bank[:, 0:512]), reads=[bk], writes=[ok])
        toks.append(P.dma("sp", projT[ci * 128:(ci + 1) * 128, half * 512:(half + 1) * 512], ob[:, :], reads=[ok]))
    stream_matmul_fm(P, PS, w_in, 0, IN_COLS, 16, h16, h16keys, TPC, evac, wbufs, wname)
    return toks


def build_A0():
    nc = bass.Bass("TRN2", target_bir_lowering=False)
    xT = nc.dram_tensor("xT", [D, TPC], F32, kind="ExternalInput").ap()
    lng = nc.dram_tensor("lng", [128, 16], F32, kind="ExternalInput").ap()
    lnb = nc.dram_tensor("lnb", [128, 16], F32, kind="ExternalInput").ap()
    w_in = nc.dram_tensor("w_in", [D, IN_COLS], F32, kind="ExternalInput").ap()
    projT = nc.dram_tensor("projT", [IN_COLS, TPC], F32, kind="ExternalOutput").ap()
    hT = nc.dram_tensor("hT", [D, TPC], F32, kind="ExternalOutput").ap()
    P = Prog(nc)
    PS = PsumRing(P)
    ones = P.sb([128, 128], F32, "ones")
    P.op("pool", lambda e: e.memset(ones[:, :], 1.0), writes=["ones"])
    gam = load_consts(P, lng, "gam", 16)
    bet = load_consts(P, lnb, "bet", 16)
    h16 = P.sb([128, 16, TPC], BF16, "h16")
    x32 = P.sb([128, 16, 512], F32, "x32")
    scr = P.sb([128, 16, 512], F32, "scr")
    toks = []
    x32k = [("x32", k) for k in range(16)]
    for half in range(2):
        P.dma("sp", x32[:, :, :], xT[:, half * 512:(half + 1) * 512].rearrange("(k p) t -> p k t", p=128), writes=x32k)
        h16v = h16[:, :, half * 512:(half + 1) * 512]
        layernorm_fm(P, PS, x32, ("x32",), 512, gam, bet, ["gam", "bet"], ones, x32, ("x32",), h16v, ("h16", half),
                     scr, ("scr",))
        toks.append(P.dma("sp", hT[:, half * 512:(half + 1) * 512].rearrange("(k p) t -> p k t", p=128), x32[:, :, :],
                          reads=x32k))
    wbufs = [P.sb([128, 16, 256], BF16, f"wb{i}") for i in range(2)]
    obufs = [P.sb([128, 512], F32, f"ob{i}") for i in range(4)]
    h16keys = [("h16", hf, k) for hf in range(2) for k in range(16)]
    toks += emit_inproj(P, PS, h16, h16keys, w_in, projT, wbufs, obufs)
    P.emit(toks)
    return nc


NEG = -30000.0
RB = 512
NCH = RB // 128
LB = 1024


def emit_retention(P, PS, nc, io):
    qT, kT, ktm, v, cosT, sinT, costm, sintm, logit, diffT, keepT, idx1, kidx, y = io
    cst = P.sb([128, 8], F32, "r_cst")
    dT = P.sb([128, 128], F32, "r_diffT")
    kpT = P.sb([128, 128], F32, "r_keepT")
    i1 = P.sb([128, 128], F32, "r_idx1")
    maskT = P.sb([128, 128], F32, "r_maskT")
    qdec = P.sb([128, RB], F32, "r_qdec")
    P.dma("sp", cst[:, 0:1], logit[:, :], writes=["r_c0"])
    P.dma("sp", cst[:, 6:7], kidx[:, :], writes=["r_c6"])
    P.dma("sp", dT[:, :], diffT[:, :], writes=["r_dT"])
    P.dma("sp", kpT[:, :], keepT[:, :], writes=["r_kpT"])
    P.dma("sp", i1[:, :], idx1[:, :], writes=["r_i1"])
    P.op("act", lambda e: e.activation(out=cst[:, 1:2], in_=cst[:, 0:1], func=AF.Exp, scale=-1.0), reads=["r_c0"], writes=["r_c1"])
    P.op("act", lambda e: e.activation(out=cst[:, 2:3], in_=cst[:, 1:2], func=AF.Ln, bias=1.0), reads=["r_c1"], writes=["r_c2"])
    P.op("act", lambda e: e.mul(out=cst[:, 3:4], in_=cst[:, 2:3], mul=-1.0), reads=["r_c2"], writes=["r_logg"])
    P.op("act", lambda e: e.activation(out=maskT[:, :], in_=dT[:, :], func=AF.Exp, scale=cst[:, 3:4]),
         reads=["r_dT", "r_logg"], writes=["r_maskT"])
    P.op("dve", lambda e: e.tensor_tensor(out=maskT[:, :], in0=maskT[:, :], in1=kpT[:, :], op=ALU.mult),
         reads=["r_maskT", "r_kpT"], writes=["r_maskT"])
    for n in range(RB // 128):
        P.op("act", lambda e, n=n: e.activation(out=qdec[:, n * 128:(n + 1) * 128], in_=i1[:, :], func=AF.Exp, scale=cst[:, 3:4]),
             reads=["r_i1", "r_logg"], writes=[("r_qdec", n)])
    qdk = [("r_qdec", n) for n in range(RB // 128)]
    P.op("act", lambda e: e.activation(out=cst[:, 4:5], in_=cst[:, 6:7], func=AF.Exp, scale=cst[:, 3:4]),
         reads=["r_c6", "r_logg"], writes=["r_kdec"])
    P.op("act", lambda e: e.activation(out=cst[:, 5:6], in_=cst[:, 3:4], func=AF.Exp, scale=128.0),
         reads=["r_logg"], writes=["r_cdec"])
    S32 = P.sb([128, 512], F32, "r_S32")
    S16 = P.sb([128, 512], BF16, "r_S16")
    P.op("pool", lambda e: e.memset(S32[:, :], 0.0), writes=["r_S32"])
    P.op("pool", lambda e: e.memset(S16[:, :], 0.0), writes=["r_S16"])
    qr = P.sb([128, 2, RB], F32, "r_qr")
    kr = P.sb([128, 2, RB], F32, "r_kr")
    ktr = P.sb([128, NCH, 256], F32, "r_ktr")
    vr = P.sb([128, NCH, 256], F32, "r_vr")
    cs = P.sb([128, RB], F32, "r_cs")
    sn = P.sb([128, RB], F32, "r_sn")
    cst_ = P.sb([128, NCH, 128], F32, "r_cstm")
    snt = P.sb([128, NCH, 128], F32, "r_sntm")
    ta = P.sb([128, RB], F32, "r_ta")
    tb = P.sb([128, RB], F32, "r_tb")
    tc_ = P.sb([128, RB], F32, "r_tc")
    q16 = P.sb([128, 2, RB], BF16, "r_q16")
    qd16 = P.sb([128, 2, RB], BF16, "r_qd16")
    k16 = P.sb([128, 2, RB], BF16, "r_k16")
    kt16 = P.sb([128, NCH, 256], BF16, "r_kt16")
    v16 = P.sb([128, NCH, 256], BF16, "r_v16")
    vd16 = P.sb([128, NCH, 256], BF16, "r_vd16")
    sm16 = [P.sb([128, 128], BF16, f"r_sm16_{i}") for i in range(2)]
    yb = [P.sb([128, NCH, 256], F32, f"r_yb{i}") for i in range(2)]
    toks = []
    KS = 256 ** -0.5
    for b in range(SEQ // RB):
        t0 = b * RB
        P.dma("sp", qr[:, :, :], qT[:, t0:t0 + RB].rearrange("(j p) t -> p j t", p=128), writes=["r_qr"])
        P.dma("sp", kr[:, :, :], kT[:, t0:t0 + RB].rearrange("(j p) t -> p j t", p=128), writes=["r_kr"])
        P.dma("act", ktr[:, :, :], ktm[t0:t0 + RB, :].rearrange("(n p) d -> p n d", p=128), writes=["r_ktr"])
        P.dma("act", vr[:, :, :], v[t0:t0 + RB, :].rearrange("(n p) d -> p n d", p=128), writes=["r_vr"])
        P.dma("sp", cs[:, :], cosT[:, t0:t0 + RB], writes=["r_cs"])
        P.dma("sp", sn[:, :], sinT[:, t0:t0 + RB], writes=["r_sn"])
        P.dma("act", cst_[:, :, :], costm[t0:t0 + RB, :].rearrange("(n p) d -> p n d", p=128), writes=["r_cstm"])
        P.dma("act", snt[:, :, :], sintm[t0:t0 + RB, :].rearrange("(n p) d -> p n d", p=128), writes=["r_sntm"])

        def rot_fm(src, skey, outs, okeys, scale):
            t1, t2 = src[:, 0, :], src[:, 1, :]
            P.op("dve", lambda e: e.tensor_tensor(out=ta[:, :], in0=t1, in1=cs[:, :], op=ALU.mult), reads=[skey, "r_cs"], writes=["r_ta"])
            P.op("pool", lambda e: e.tensor_tensor(out=tb[:, :], in0=t2, in1=sn[:, :], op=ALU.mult), reads=[skey, "r_sn"], writes=["r_tb"])
            P.op("dve", lambda e: e.tensor_tensor(out=ta[:, :], in0=ta[:, :], in1=tb[:, :], op=ALU.subtract), reads=["r_ta", "r_tb"], writes=["r_ta"])
            P.op("pool", lambda e: e.tensor_tensor(out=tb[:, :], in0=t1, in1=sn[:, :], op=ALU.mult), reads=[skey, "r_sn", "r_ta"], writes=["r_tb"])
            P.op("dve", lambda e: e.tensor_tensor(out=tc_[:, :], in0=t2, in1=cs[:, :], op=ALU.mult), reads=[skey, "r_cs"], writes=["r_tc"])
            P.op("dve", lambda e: e.tensor_tensor(out=tb[:, :], in0=tb[:, :], in1=tc_[:, :], op=ALU.add), reads=["r_tb", "r_tc"], writes=["r_tb"])
            for (o, ok, extra) in outs:
                if extra is None:
                    P.op("act", lambda e, o=o: e.mul(out=o[:, 0, :], in_=ta[:, :], mul=scale), reads=["r_ta"], writes=[ok + "0"])
                    P.op("act", lambda e, o=o: e.mul(out=o[:, 1, :], in_=tb[:, :], mul=scale), reads=["r_tb"], writes=[ok + "1"])
                else:
                    P.op("dve", lambda e, o=o: e.tensor_tensor(out=o[:, 0, :], in0=ta[:, :], in1=qdec[:, :], op=ALU.mult), reads=["r_ta"] + qdk, writes=[ok + "0"])
                    P.op("pool", lambda e, o=o: e.tensor_tensor(out=o[:, 1, :], in0=tb[:, :], in1=qdec[:, :], op=ALU.mult), reads=["r_tb"] + qdk, writes=[ok + "1"])
        rot_fm(qr, "r_qr", [(q16, "r_q16", None), (qd16, "r_qd16", True)], None, 1.0)
        rot_fm(kr, "r_kr", [(k16, "r_k16", None)], None, KS)
        ta3 = ta[:, :].rearrange("p (n d) -> p n d", d=128)
        tb3 = tb[:, :].rearrange("p (n d) -> p n d", d=128)
        tc3 = tc_[:, :].rearrange("p (n d) -> p n d", d=128)
        t1, t2 = ktr[:, :, 0:128], ktr[:, :, 128:256]
        P.op("dve", lambda e: e.tensor_tensor(out=ta3, in0=t1, in1=cst_[:, :, :], op=ALU.mult), reads=["r_ktr", "r_cstm"], writes=["r_ta"])
        P.op("pool", lambda e: e.tensor_tensor(out=tb3, in0=t2, in1=snt[:, :, :], op=ALU.mult), reads=["r_ktr", "r_sntm"], writes=["r_tb"])
        P.op("dve", lambda e: e.tensor_tensor(out=ta3, in0=ta3, in1=tb3, op=ALU.subtract), reads=["r_ta", "r_tb"], writes=["r_ta"])
        P.op("act", lambda e: e.mul(out=kt16[:, :, 0:128], in_=ta3, mul=KS), reads=["r_ta"], writes=["r_kt16a"])
        P.op("pool", lambda e: e.tensor_tensor(out=tb3, in0=t1, in1=snt[:, :, :], op=ALU.mult), reads=["r_ktr", "r_sntm", "r_ta"], writes=["r_tb"])
        P.op("dve", lambda e: e.tensor_tensor(out=tc3, in0=t2, in1=cst_[:, :, :], op=ALU.mult), reads=["r_ktr", "r_cstm"], writes=["r_tc"])
        P.op("dve", lambda e: e.tensor_tensor(out=tb3, in0=tb3, in1=tc3, op=ALU.add), reads=["r_tb", "r_tc"], writes=["r_tb"])
        P.op("act", lambda e: e.mul(out=kt16[:, :, 128:256], in_=tb3, mul=KS), reads=["r_tb"], writes=["r_kt16b"])
        P.op("pool", lambda e: e.tensor_copy(out=v16[:, :, :], in_=vr[:, :, :]), reads=["r_vr"], writes=["r_v16"])
        P.op("act", lambda e: e.activation(out=vd16[:, :, :], in_=vr[:, :, :], func=AF.Copy, scale=cst[:, 4:5]),
             reads=["r_vr", "r_kdec"], writes=["r_vd16"])
        ybuf = yb[b % 2]
        ybk = ("r_yb", b % 2)
        for n in range(RB // 128):
            cs_ = slice(n * 128, (n + 1) * 128)
            bs, bsk = PS.next()

            def mm_s(e, cs_=cs_, bs=bs):
                for j in range(2):
                    ins = e.matmul(bs[:, 0:128], lhsT=k16[:, j, cs_], rhs=q16[:, j, cs_], start=(j == 0), stop=(j == 1))
                return ins
            P.op("pe", mm_s, reads=["r_k160", "r_k161", "r_q160", "r_q161"], writes=[bsk])
            sm = sm16[n % 2]
            smk = ("r_sm", n % 2)
            P.op("dve", lambda e, sm=sm, bs=bs: e.tensor_tensor(out=sm[:, :], in0=bs[:, 0:128], in1=maskT[:, :], op=ALU.mult),
                 reads=[bsk, "r_maskT"], writes=[smk])
            by, byk = PS.next()

            def mm_y(e, cs_=cs_, by=by, sm=sm, n=n):
                e.matmul(by[:, 0:256], lhsT=sm[:, :], rhs=v16[:, n, :], start=True, stop=False)
                for j in range(2):
                    ins = e.matmul(by[:, 0:256], lhsT=qd16[:, j, cs_], rhs=S16[:, j * 256:(j + 1) * 256], start=False, stop=(j == 1))
                return ins
            P.op("pe", mm_y, reads=[smk, "r_v16", "r_qd160", "r_qd161", "r_S16"], writes=[byk])
            P.op("act", lambda e, by=by, n=n, ybuf=ybuf: e.copy(out=ybuf[:, n, :], in_=by[:, 0:256]), reads=[byk], writes=[ybk + (n,)])
            bkv, bkvk = PS.next()

            def mm_kv(e, bkv=bkv, n=n):
                for j in range(2):
                    ins = e.matmul(bkv[:, j * 256:(j + 1) * 256], lhsT=kt16[:, n, j * 128:(j + 1) * 128], rhs=vd16[:, n, :],
                                   start=True, stop=True)
                return ins
            P.op("pe", mm_kv, reads=["r_kt16a", "r_kt16b", "r_vd16"], writes=[bkvk])
            P.op("dve", lambda e, bkv=bkv: e.scalar_tensor_tensor(out=S32[:, :], in0=S32[:, :], scalar=cst[:, 5:6], in1=bkv[:, :],
                                                                 op0=ALU.mult, op1=ALU.add),
                 reads=["r_S32", "r_cdec", bkvk], writes=["r_S32"])
            P.op("pool", lambda e: e.tensor_copy(out=S16[:, :], in_=S32[:, :]), reads=["r_S32"], writes=["r_S16"])
        toks.append(P.dma("sp", y[t0:t0 + RB, :].rearrange("(n p) e -> p n e", p=128), ybuf[:, :, :],
                          reads=[ybk + (n,) for n in range(NCH)]))
    return toks


def emit_na(P, PS, nc, io):
    qT, kT, v64, biasd, o = io
    q16 = P.sb([128, SEQ], BF16, "n_q16")
    k16 = P.sb([128, SEQ], BF16, "n_k16")
    va = P.sb([64, 128, 132], BF16, "n_va")
    bias = P.sb([64, 8, 512], F32, "n_bias")
    for i in range(4):
        sl = slice(i * 2048, (i + 1) * 2048)
        P.dma("pool", q16[:, sl], qT[:, sl], writes=[("n_q16", i)], max_dma_last_dim=4096)
        P.dma("pool", k16[:, sl], kT[:, sl], writes=[("n_k16", i)], max_dma_last_dim=4096)
    qk_keys = [("n_q16", i) for i in range(4)] + [("n_k16", i) for i in range(4)]
    P.op("pool", lambda e: e.memset(va[:, :, 128:132], 1.0), writes=["n_va1"])
    for i in range(4):
        P.dma("pool", va[:, i * 32:(i + 1) * 32, 0:128], v64[:, i * 32:(i + 1) * 32, :], writes=[("n_va", i)])
    va_keys = ["n_va1"] + [("n_va", i) for i in range(4)]
    P.dma("sp", bias[:, :, :], biasd[:, :, :], writes=["n_bias"])
    st = [P.sb([64, 512], F32, f"n_st{i}") for i in range(2)]
    e16 = [P.sb([64, 512], BF16, f"n_e16_{i}") for i in range(2)]
    rc = [P.sb([64, 1], F32, f"n_rc{i}") for i in range(2)]
    ob = [P.sb([64, 8, 128], F32, f"n_ob{i}") for i in range(2)]
    toks = []
    scale = 128 ** -0.5
    nrows = SEQ // GRID_W
    for r in range(nrows):
        rs = min(max(r - 4, 0), nrows - 8)
        cls = r - rs
        bs, bsk = PS.next()

        def mm_s(e, r=r, rs=rs, bs=bs):
            for i in range(8):
                ks = slice((rs + i) * 64, (rs + i + 1) * 64)
                ins = e.matmul(bs[0:64, i * 64:(i + 1) * 64], lhsT=k16[:, ks], rhs=q16[:, r * 64:(r + 1) * 64], start=True, stop=True)
            return ins
        P.op("pe", mm_s, reads=qk_keys, writes=[bsk])
        s_, sk = st[r % 2], ("n_st", r % 2)
        P.op("dve", lambda e, s_=s_, bs=bs, cls=cls: e.scalar_tensor_tensor(out=s_[:, :], in0=bs[0:64, :], scalar=scale, in1=bias[:, cls, :],
                                                                            op0=ALU.mult, op1=ALU.add),
             reads=[bsk, "n_bias"], writes=[sk])
        e_, ek = e16[r % 2], ("n_e16", r % 2)
        P.op("act", lambda e, s_=s_, e_=e_: e.activation(out=e_[:, :], in_=s_[:, :], func=AF.Exp), reads=[sk], writes=[ek])
        bo, bok = PS.next()

        def mm_o(e, rs=rs, bo=bo, e_=e_):
            for i in range(8):
                ins = e.matmul(bo[0:64, 0:129], lhsT=e_[:, i * 64:(i + 1) * 64], rhs=va[:, rs + i, 0:129], start=(i == 0), stop=(i == 7))
            return ins
        P.op("pe", mm_o, reads=[ek] + va_keys, writes=[bok])
        rc_, rck = rc[r % 2], ("n_rc", r % 2)
        P.op("dve", lambda e, rc_=rc_, bo=bo: e.reciprocal(out=rc_[:, :], in_=bo[0:64, 128:129]), reads=[bok], writes=[rck])
        obuf, obk = ob[(r // 8) % 2], ("n_ob", (r // 8) % 2)
        P.op("act", lambda e, obuf=obuf, bo=bo, rc_=rc_, r=r: e.activation(out=obuf[:, r % 8, :], in_=bo[0:64, 0:128], func=AF.Copy, scale=rc_[:, 0:1]),
             reads=[bok, rck], writes=[obk + (r % 8,)])
        if r % 8 == 7:
            g = r // 8
            toks.append(P.dma("sp", o[g * 512:(g + 1) * 512, :].rearrange("(r p) d -> p r d", p=64), obuf[:, :, :],
                              reads=[obk + (i,) for i in range(8)]))
    return toks


def emit_lru(P, PS, nc, io):
    xpf, xpb, wtap, bconv, wad, wid, gb, lam, hout = io
    tap = P.sb([128, 8], F32, "l_tap")
    bc = P.sb([128, 1], F32, "l_bc")
    gbt = P.sb([128, 4], F32, "l_gb")
    lm = P.sb([128, 8], F32, "l_lm")
    wa16 = P.sb([128, 2, 128], BF16, "l_wa16")
    wi16 = P.sb([128, 2, 128], BF16, "l_wi16")
    P.dma("sp", tap[:, :], wtap[:, :], writes=["l_tap"])
    P.dma("sp", bc[:, :], bconv[:, :], writes=["l_bc"])
    P.dma("sp", gbt[:, :], gb[:, :], writes=["l_gb"])
    P.dma("sp", lm[:, 0:2], lam[:, :], writes=["l_lm0"])
    P.dma("pool", wa16[:, :, :], wad[:, :, :], writes=["l_wa16"])
    P.dma("pool", wi16[:, :, :], wid[:, :, :], writes=["l_wi16"])
    P.op("act", lambda e: e.activation(out=lm[:, 2:4], in_=lm[:, 0:2], func=AF.Exp, scale=-1.0), reads=["l_lm0"], writes=["l_lm1"])
    P.op("act", lambda e: e.activation(out=lm[:, 4:6], in_=lm[:, 2:4], func=AF.Ln, bias=1.0), reads=["l_lm1"], writes=["l_lm2"])
    P.op("act", lambda e: e.mul(out=lm[:, 6:8], in_=lm[:, 4:6], mul=-8.0), reads=["l_lm2"], writes=["l_lm3"])
    xp = P.sb([128, LB + 3], F32, "l_xp")
    xc = P.sb([128, LB], F32, "l_xc")
    xc16 = P.sb([128, LB], BF16, "l_xc16")
    rg = P.sb([128, LB], F32, "l_rg")
    ig = P.sb([128, LB], F32, "l_ig")
    hb = [P.sb([128, LB], F32, f"l_h{i}") for i in range(2)]
    toks = []
    it = 0
    for d in range(2):
        src = xpf if d == 0 else xpb
        for b in range(SEQ // LB):
            t0 = b * LB
            P.dma("sp", xp[:, :], src[:, t0:t0 + LB + 3], writes=["l_xp"])
            P.op("dve", lambda e, d=d: e.tensor_scalar(out=xc[:, :], in0=xp[:, 0:LB], scalar1=tap[:, 4 * d:4 * d + 1], scalar2=bc[:, 0:1],
                                                       op0=ALU.mult, op1=ALU.add), reads=["l_xp", "l_tap", "l_bc"], writes=["l_xc"])
            for j in range(1, 4):
                P.op("dve", lambda e, d=d, j=j: e.scalar_tensor_tensor(out=xc[:, :], in0=xp[:, j:j + LB], scalar=tap[:, 4 * d + j:4 * d + j + 1],
                                                                       in1=xc[:, :], op0=ALU.mult, op1=ALU.add),
                     reads=["l_xp", "l_tap", "l_xc"], writes=["l_xc"])
            P.op("pool", lambda e: e.tensor_copy(out=xc16[:, :], in_=xc[:, :]), reads=["l_xc"], writes=["l_xc16"])
            for s in range(LB // 512):
                sl = slice(s * 512, (s + 1) * 512)
                br, brk = PS.next()
                bi_, bik = PS.next()
                P.op("pe", lambda e, d=d, sl=sl, br=br: e.matmul(br[:, :], lhsT=wa16[:, d, :], rhs=xc16[:, sl], start=True, stop=True),
                     reads=["l_wa16", "l_xc16"], writes=[brk])
                P.op("pe", lambda e, d=d, sl=sl, bi_=bi_: e.matmul(bi_[:, :], lhsT=wi16[:, d, :], rhs=xc16[:, sl], start=True, stop=True),
                     reads=["l_wi16", "l_xc16"], writes=[bik])
                P.op("act", lambda e, d=d, sl=sl, br=br: e.activation(out=rg[:, sl], in_=br[:, :], func=AF.Sigmoid, bias=gbt[:, 2 * d:2 * d + 1]),
                     reads=[brk, "l_gb"], writes=[("l_rg", s)])
                P.op("act", lambda e, d=d, sl=sl, bi_=bi_: e.activation(out=ig[:, sl], in_=bi_[:, :], func=AF.Sigmoid, bias=gbt[:, 2 * d + 1:2 * d + 2]),
                     reads=[bik, "l_gb"], writes=[("l_ig", s)])
            rgk = [("l_rg", s) for s in range(LB // 512)]
            igk = [("l_ig", s) for s in range(LB // 512)]
            P.op("act", lambda e, d=d: e.activation(out=rg[:, :], in_=rg[:, :], func=AF.Exp, scale=lm[:, 6 + d:7 + d]), reads=rgk + ["l_lm3"], writes=rgk)
            P.op("dve", lambda e: e.tensor_tensor(out=ig[:, :], in0=ig[:, :], in1=xc[:, :], op=ALU.mult), reads=igk + ["l_xc"], writes=igk)
            P.op("pool", lambda e: e.tensor_tensor(out=xc[:, :], in0=rg[:, :], in1=rg[:, :], op=ALU.mult), reads=rgk + igk, writes=["l_xc"])
            P.op("act", lambda e: e.activation(out=xc[:, :], in_=xc[:, :], func=AF.Sqrt, scale=-1.0, bias=1.0), reads=["l_xc"], writes=["l_xc"])
            P.op("dve", lambda e: e.tensor_tensor(out=ig[:, :], in0=ig[:, :], in1=xc[:, :], op=ALU.mult), reads=igk + ["l_xc"], writes=igk)
            h, hk = hb[it % 2], ("l_h", it % 2)
            hprev, hpk = hb[(it + 1) % 2], ("l_h", (it + 1) % 2)
            if b == 0:
                P.op("dve", lambda e, h=h: e.tensor_tensor_scan(out=h[:, :], data0=rg[:, :], data1=ig[:, :], initial=0.0,
                                                                op0=ALU.mult, op1=ALU.add), reads=rgk + igk, writes=[hk])
            else:
                P.op("dve", lambda e, h=h, hprev=hprev: e.tensor_tensor_scan(out=h[:, :], data0=rg[:, :], data1=ig[:, :],
                                                                             initial=hprev[:, LB - 1:LB], op0=ALU.mult, op1=ALU.add),
                     reads=rgk + igk + [hpk], writes=[hk])
            toks.append(P.dma("sp", hout[d, :, t0:t0 + LB], h[:, :], reads=[hk]))
            it += 1
    return toks


def build_B(parts=("ret", "na", "lru")):
    nc = bass.Bass("TRN2", target_bir_lowering=False)
    di = lambda n, s: nc.dram_tensor(n, s, F32, kind="ExternalInput").ap()
    do = lambda n, s: nc.dram_tensor(n, s, F32, kind="ExternalOutput").ap()
    P = Prog(nc)
    PS = PsumRing(P)
    toks = []
    if "ret" in parts:
        io = (di("r_qT", [256, SEQ]), di("r_kT", [256, SEQ]), di("r_ktm", [SEQ, 256]), di("r_v", [SEQ, 256]),
              di("r_cosT", [128, SEQ]), di("r_sinT", [128, SEQ]), di("r_costm", [SEQ, 128]), di("r_sintm", [SEQ, 128]),
              di("r_logit", [128, 1]), di("r_diffT", [128, 128]), di("r_keepT", [128, 128]), di("r_idx1", [128, 128]),
              di("r_kidx", [128, 1]), do("r_y", [SEQ, 256]))
        toks += emit_retention(P, PS, nc, io)
    if "na" in parts:
        io = (di("n_qT", [128, SEQ]), di("n_kT", [128, SEQ]), di("n_v64", [64, 128, 128]), di("n_bias", [64, 8, 512]),
              do("n_o", [SEQ, 128]))
        toks += emit_na(P, PS, nc, io)
    if "lru" in parts:
        io = (di("l_xpf", [128, SEQ + 3]), di("l_xpb", [128, SEQ + 3]), di("l_wtap", [128, 8]), di("l_bconv", [128, 1]),
              di("l_wa", [128, 2, 128]), di("l_wi", [128, 2, 128]), di("l_gb", [128, 4]), di("l_lam", [128, 2]),
              do("l_h", [2, 128, SEQ]))
        toks += emit_lru(P, PS, nc, io)
    P.emit(toks)
    return nc


def _rot_tables():
    half = 128
    inv = (10000.0 ** (-np.arange(half, dtype=np.float32) / np.float32(half))).astype(np.float32)
    pos = np.arange(SEQ, dtype=np.float32)
    ang = (pos[:, None] * inv[None, :]).astype(np.float32)
    return np.cos(ang).astype(np.float32), np.sin(ang).astype(np.float32)


_CONST = {}


def _consts():
    if _CONST:
        return _CONST
    cos, sin = _rot_tables()
    _CONST["cos_tm"] = [cos, np.ascontiguousarray(cos[::-1])]
    _CONST["sin_tm"] = [sin, np.ascontiguousarray(sin[::-1])]
    _CONST["cos_fm"] = [np.ascontiguousarray(c.T) for c in _CONST["cos_tm"]]
    _CONST["sin_fm"] = [np.ascontiguousarray(c.T) for c in _CONST["sin_tm"]]
    idx = np.arange(128, dtype=np.float32)
    diff = idx[None, :] - idx[:, None]
    keep = [(diff >= 0), (diff > 0)]
    _CONST["diffT"] = [np.where(k, diff, 0.0).astype(np.float32) for k in keep]
    _CONST["keepT"] = [k.astype(np.float32) for k in keep]
    _CONST["idx1"] = np.ascontiguousarray(np.broadcast_to((idx + 1.0)[None, :], (128, 128))).astype(np.float32)
    _CONST["kidx"] = (127.0 - idx)[:, None].astype(np.float32)
    kc = np.arange(64)[:, None, None, None]
    cl = np.arange(8)[None, :, None, None]
    ki = np.arange(8)[None, None, :, None]
    q = np.arange(64)[None, None, None, :]
    cstart = np.clip(q - 8, 0, 48)
    inwin = (kc >= cstart) & (kc < cstart + 16)
    dr = ki - cl + 7 + 0 * kc + 0 * q
    dc = np.clip(kc - q, -15, 15) + 15 + 0 * cl + 0 * ki
    _CONST["na_dr"] = np.broadcast_to(dr, (64, 8, 8, 64)).copy()
    _CONST["na_dc"] = np.broadcast_to(dc, (64, 8, 8, 64)).copy()
    _CONST["na_win"] = np.broadcast_to(inwin, (64, 8, 8, 64)).copy()
    return _CONST


def prep_B(projT, l, inp):
    C = _consts()
    ims = []
    for c in range(NCORES):
        hh, dd = c // 2, c % 2
        fl = (lambda a: a[:, ::-1]) if dd else (lambda a: a)
        m = {}
        qT = fl(projT[hh * 256:(hh + 1) * 256])
        kT = fl(projT[1024 + hh * 256:1024 + (hh + 1) * 256])
        vT = fl(projT[2048 + hh * 256:2048 + (hh + 1) * 256])
        m["r_qT"] = np.ascontiguousarray(qT)
        m["r_kT"] = np.ascontiguousarray(kT)
        m["r_ktm"] = np.ascontiguousarray(kT.T)
        m["r_v"] = np.ascontiguousarray(vT.T)
        m["r_cosT"], m["r_sinT"] = C["cos_fm"][dd], C["sin_fm"][dd]
        m["r_costm"], m["r_sintm"] = C["cos_tm"][dd], C["sin_tm"][dd]
        m["r_logit"] = np.full((128, 1), inp["ret_decay"][l, dd, hh], np.float32)
        m["r_diffT"], m["r_keepT"] = C["diffT"][dd], C["keepT"][dd]
        m["r_idx1"], m["r_kidx"] = C["idx1"], C["kidx"]
        m["n_qT"] = np.ascontiguousarray(projT[4096 + c * 128:4096 + (c + 1) * 128])
        m["n_kT"] = np.ascontiguousarray(projT[5120 + c * 128:5120 + (c + 1) * 128])
        vh = projT[6144 + c * 128:6144 + (c + 1) * 128]
        m["n_v64"] = np.ascontiguousarray(vh.T.reshape(128, 64, 128).transpose(1, 0, 2))
        rpb = inp["na_rpb"][l, c]
        drv = np.clip(C["na_dr"], 0, 14)
        b = rpb[drv, C["na_dc"]]
        valid = C["na_win"] & (C["na_dr"] >= 0) & (C["na_dr"] <= 14)
        b = np.where(valid, b, np.float32(NEG)).astype(np.float32)
        m["n_bias"] = np.ascontiguousarray(b.reshape(64, 8, 512))
        x = projT[7168 + c * 128:7168 + (c + 1) * 128]
        xpf = np.zeros((128, SEQ + 3), np.float32)
        xpf[:, 2:2 + SEQ] = x
        xpb = np.zeros((128, SEQ + 3), np.float32)
        xpb[:, 1:1 + SEQ] = x[:, ::-1]
        m["l_xpf"], m["l_xpb"] = xpf, xpb
        wc = inp["w_conv"][l][:, c * 128:(c + 1) * 128]
        m["l_wtap"] = np.ascontiguousarray(np.concatenate([wc.T, wc[::-1].T], axis=1))
        m["l_bconv"] = np.ascontiguousarray(inp["b_conv"][l][c * 128:(c + 1) * 128, None])
        m["l_wa"] = np.ascontiguousarray(inp["lru_wa"][l][:, c].transpose(1, 0, 2))
        m["l_wi"] = np.ascontiguousarray(inp["lru_wi"][l][:, c].transpose(1, 0, 2))
        sl = slice(c * 128, (c + 1) * 128)
        m["l_gb"] = np.ascontiguousarray(np.stack([inp["lru_ba"][l][0, sl], inp["lru_bi"][l][0, sl],
                                                   inp["lru_ba"][l][1, sl], inp["lru_bi"][l][1, sl]], axis=1))
        m["l_lam"] = np.ascontiguousarray(inp["lru_lambda"][l][:, sl].T)
        ims.append(m)
    return ims


def post_B(results):
    yf = np.concatenate([results[2 * h]["r_y"] for h in range(4)], axis=1)
    yb = np.concatenate([results[2 * h + 1]["r_y"][::-1] for h in range(4)], axis=1)
    na = np.concatenate([results[c]["n_o"] for c in range(8)], axis=1)
    hf = np.concatenate([results[c]["l_h"][0] for c in range(8)], axis=0)
    hb = np.concatenate([results[c]["l_h"][1][:, ::-1] for c in range(8)], axis=0)
    return yf, yb, na, hf, hb


def fm_stats(P, PS, x, xk, KC, T, ones, scr, sk):
    nfeat = KC * 128
    P.op("act", lambda e: e.activation(out=scr[:, 0:KC, 0:T], in_=x[:, 0:KC, 0:T], func=AF.Square), reads=xk, writes=sk)
    b_sum, k_sum = PS.next()
    b_sq, k_sq = PS.next()

    def mm_sum(e):
        for k in range(KC):
            ins = e.matmul(b_sum[:, 0:T], lhsT=ones[:, :], rhs=x[:, k, 0:T], start=(k == 0), stop=(k == KC - 1))
        return ins

    def mm_sq(e):
        for k in range(KC):
            ins = e.matmul(b_sq[:, 0:T], lhsT=ones[:, :], rhs=scr[:, k, 0:T], start=(k == 0), stop=(k == KC - 1))
        return ins
    P.op("pe", mm_sum, reads=xk + ["ones"], writes=[k_sum])
    P.op("pe", mm_sq, reads=sk + ["ones"], writes=[k_sq])
    if not hasattr(P, "_ln_tmp"):
        P._ln_tmp = (P.sb([128, 512], F32, "ln_mean"), P.sb([128, 512], F32, "ln_rstd"))
    mean, rstd = P._ln_tmp
    mk, rk = "ln_mean", "ln_rstd"
    P.op("act", lambda e: e.mul(out=mean[:, 0:T], in_=b_sum[:, 0:T], mul=1.0 / nfeat), reads=[k_sum], writes=[mk])
    P.op("dve", lambda e: e.tensor_tensor(out=rstd[:, 0:T], in0=mean[:, 0:T], in1=mean[:, 0:T], op=ALU.mult), reads=[mk], writes=[rk])
    P.op("dve", lambda e: e.scalar_tensor_tensor(out=rstd[:, 0:T], in0=b_sq[:, 0:T], scalar=1.0 / nfeat, in1=rstd[:, 0:T],
                                                 op0=ALU.mult, op1=ALU.subtract), reads=[k_sq, rk], writes=[rk])
    P.op("dve", lambda e: e.tensor_scalar(out=rstd[:, 0:T], in0=rstd[:, 0:T], scalar1=EPS, scalar2=None, op0=ALU.add),
         reads=[rk], writes=[rk])
    P.op("act", lambda e: e.activation(out=rstd[:, 0:T], in_=rstd[:, 0:T], func=AF.Sqrt), reads=[rk], writes=[rk])
    P.op("dve", lambda e: e.reciprocal(out=rstd[:, 0:T], in_=rstd[:, 0:T]), reads=[rk], writes=[rk])
    return mean, rstd, mk, rk


def build_C(with_next_proj):
    T = 512
    nc = bass.Bass("TRN2", target_bir_lowering=False)
    di = lambda n, s: nc.dram_tensor(n, s, F32, kind="ExternalInput").ap()
    do = lambda n, s: nc.dram_tensor(n, s, F32, kind="ExternalOutput").ap()
    c_yf, c_yb, c_g = di("c_yf", [1024, TPC]), di("c_yb", [1024, TPC]), di("c_g", [1024, TPC])
    c_na, c_hf, c_hb, c_ly = di("c_na", [1024, TPC]), di("c_hf", [1024, TPC]), di("c_hb", [1024, TPC]), di("c_ly", [1024, TPC])
    c_gp, c_gb, c_h = di("c_gp", [6144, TPC]), di("c_gb", [128, 48]), di("c_h", [D, TPC])
    w_br, w_out = di("w_br", [3072, D]), di("w_out", [D, D])
    ln1g, ln1b, ln2g, ln2b = di("ln1g", [128, 16]), di("ln1b", [128, 16]), di("ln2g", [128, 16]), di("ln2b", [128, 16])
    w_f1, w_f2 = di("w_f1", [D, 2 * DFF]), di("w_f2", [DFF, D])
    h2T = do("h2T", [D, TPC])
    h2s = nc.dram_tensor("h2s", [D, TPC], F32).ap()
    if with_next_proj:
        w_in = di("w_in", [D, IN_COLS])
        projT = do("projT", [IN_COLS, TPC])
    P = Prog(nc)
    PS = PsumRing(P)
    ones = P.sb([128, 128], F32, "ones")
    P.op("pool", lambda e: e.memset(ones[:, :], 1.0), writes=["ones"])
    g1, b1 = load_consts(P, ln1g, "g1", 16), load_consts(P, ln1b, "b1", 16)
    g2, b2 = load_consts(P, ln2g, "g2", 16), load_consts(P, ln2b, "b2", 16)
    gbt = load_consts(P, c_gb, "gbt", 48)
    A = P.sb([128, 16, T], F32, "arenaA")
    B = P.sb([128, 16, T], F32, "arenaB")
    U = P.sb([128, 44 * T], BF16, "arenaU")
    U3 = U[:, :].rearrange("p (k t) -> p k t", t=T)
    h1_16 = P.sb([128, 16, T], BF16, "h1_16")
    wflat = [P.sb([128, 5632], BF16, f"wf{i}") for i in range(2)]
    wv = lambda kc, wc: [w[:, 0:kc * wc].rearrange("p (k c) -> p k c", c=wc) for w in wflat]
    sc = [P.sb([128, T], F32, f"sc{i}") for i in range(8)]
    sck = [("sc", i) for i in range(8)]
    gpt = [P.sb([128, T], F32, f"gpt{i}") for i in range(2)]
    Ak = [("A", k) for k in range(16)]
    Bk = [("B", k) for k in range(16)]
    Uk = [("U", k) for k in range(44)]
    toks = []
    for half in range(2):
        ts = slice(half * T, (half + 1) * T)
        for hh in range(4):
            rows = slice(hh * 256, (hh + 1) * 256)
            y, yk = A[:, 0:2, :], Ak[0:2]
            y2, y2k = A[:, 2:4, :], Ak[2:4]
            gg, ggk = A[:, 4:6, :], Ak[4:6]
            sq, sqk = A[:, 6:8, :], Ak[6:8]
            P.dma("sp", y, c_yf[rows, ts].rearrange("(k p) t -> p k t", p=128), writes=yk)
            P.dma("act", y2, c_yb[rows, ts].rearrange("(k p) t -> p k t", p=128), writes=y2k)
            P.dma("sp", gg, c_g[rows, ts].rearrange("(k p) t -> p k t", p=128), writes=ggk)
            P.op("dve", lambda e, y=y, y2=y2: e.tensor_tensor(out=y, in0=y, in1=y2, op=ALU.add), reads=yk + y2k, writes=yk)
            mean, rstd, mk, rk = fm_stats(P, PS, A[:, 0:2, :], yk, 2, T, ones, A[:, 6:8, :], sqk)
            P.op("act", lambda e, gg=gg: e.activation(out=gg, in_=gg, func=AF.Silu), reads=ggk, writes=ggk)
            for j in range(2):
                P.op("dve", lambda e, j=j: e.tensor_tensor(out=A[:, j, :], in0=A[:, j, :], in1=mean[:, 0:T], op=ALU.subtract),
                     reads=[Ak[j], mk], writes=[Ak[j]])
                P.op("dve", lambda e, j=j: e.tensor_tensor(out=A[:, j, :], in0=A[:, j, :], in1=rstd[:, 0:T], op=ALU.mult),
                     reads=[Ak[j], rk], writes=[Ak[j]])
                P.op("pool", lambda e, j=j, hh=hh: e.tensor_tensor(out=U3[:, 2 * hh + j, :], in0=A[:, j, :], in1=A[:, 4 + j, :], op=ALU.mult),
                     reads=[Ak[j], Ak[4 + j]], writes=[Uk[2 * hh + j]])
        P.dma("pool", U3[:, 8:16, :], c_na[:, ts].rearrange("(k p) t -> p k t", p=128), writes=Uk[8:16])
        for k in range(8):
            rows = slice(k * 128, (k + 1) * 128)
            hf_, hb_, ly_, t_ = sc[0], sc[1], sc[2], sc[3]
            P.dma("sp", hf_[:, :], c_hf[rows, ts], writes=[sck[0]])
            P.dma("act", hb_[:, :], c_hb[rows, ts], writes=[sck[1]])
            P.dma("sp", ly_[:, :], c_ly[rows, ts], writes=[sck[2]])
            P.op("dve", lambda e: e.tensor_tensor(out=hf_[:, :], in0=hf_[:, :], in1=hb_[:, :], op=ALU.add), reads=[sck[0], sck[1]], writes=[sck[0]])
            P.op("pool", lambda e: e.tensor_tensor(out=t_[:, :], in0=ly_[:, :], in1=ly_[:, :], op=ALU.mult), reads=[sck[2]], writes=[sck[3]])
            P.op("dve", lambda e: e.tensor_scalar(out=t_[:, :], in0=t_[:, :], scalar1=0.044715, scalar2=1.0, op0=ALU.mult, op1=ALU.add),
                 reads=[sck[3]], writes=[sck[3]])
            P.op("pool", lambda e: e.tensor_tensor(out=t_[:, :], in0=t_[:, :], in1=ly_[:, :], op=ALU.mult), reads=[sck[3], sck[2]], writes=[sck[3]])
            P.op("act", lambda e: e.activation(out=t_[:, :], in_=t_[:, :], func=AF.Sigmoid, scale=1.5957691216057308), reads=[sck[3]], writes=[sck[3]])
            P.op("dve", lambda e: e.tensor_tensor(out=t_[:, :], in0=t_[:, :], in1=ly_[:, :], op=ALU.mult), reads=[sck[3], sck[2]], writes=[sck[3]])
            P.op("dve", lambda e, k=k: e.tensor_tensor(out=U3[:, 16 + k, :], in0=t_[:, :], in1=hf_[:, :], op=ALU.mult),
                 reads=[sck[3], sck[0]], writes=[Uk[16 + k]])
        for b in range(3):
            def evac(ci, hf, bank, bk, b=b):
                gp_, gpk = gpt[ci % 2], ("gpt", ci % 2)
                rows = slice(b * 2048 + ci * 128, b * 2048 + (ci + 1) * 128)
                P.dma("sp", gp_[:, :], c_gp[rows, ts], writes=[gpk])
                P.op("act", lambda e: e.activation(out=gp_[:, :], in_=gp_[:, :], func=AF.Sigmoid, bias=gbt[:, b * 16 + ci:b * 16 + ci + 1]),
                     reads=[gpk, "gbt"], writes=[gpk])
                if b == 0:
                    P.op("dve", lambda e: e.tensor_tensor(out=A[:, ci, :], in0=bank[:, 0:T], in1=gp_[:, :], op=ALU.mult),
                         reads=[bk, gpk], writes=[Ak[ci]])
                else:
                    P.op("dve", lambda e: e.tensor_tensor(out=gp_[:, :], in0=bank[:, 0:T], in1=gp_[:, :], op=ALU.mult),
                         reads=[bk, gpk], writes=[gpk])
                    if b == 1:
                        P.op("pool", lambda e: e.tensor_tensor(out=A[:, ci, :], in0=A[:, ci, :], in1=gp_[:, :], op=ALU.add),
                             reads=[Ak[ci], gpk], writes=[Ak[ci]])
                    else:
                        P.op("pool", lambda e: e.tensor_tensor(out=U3[:, 24 + ci, :], in0=A[:, ci, :], in1=gp_[:, :], op=ALU.add),
                             reads=[Ak[ci], gpk], writes=[Uk[24 + ci]])
            stream_matmul_fm(P, PS, w_br[b * 1024:(b + 1) * 1024, :], 0, D, 8, U3[:, 8 * b:8 * b + 8, :], Uk[8 * b:8 * b + 8], T, evac,
                             wv(8, 512), "W", WC=512)
        P.dma("sp", B[:, :, :], c_h[:, ts].rearrange("(k p) t -> p k t", p=128), writes=Bk)

        def evac_o(ci, hf, bank, bk):
            P.op("dve", lambda e: e.scalar_tensor_tensor(out=B[:, ci, :], in0=B[:, ci, :], scalar=ALPHA, in1=bank[:, 0:T],
                                                         op0=ALU.mult, op1=ALU.add), reads=[bk, Bk[ci]], writes=[Bk[ci]])
        stream_matmul_fm(P, PS, w_out, 0, D, 16, U3[:, 24:40, :], Uk[24:40], T, evac_o, wv(16, 256), "W", WC=256)
        layernorm_fm(P, PS, B, ("B",), T, g1, b1, ["g1", "b1"], ones, B, ("B",), h1_16, ("h1_16",), A, ("A",))
        h1k = [("h1_16", k) for k in range(16)]

        def evac_f(ci, hf, bank, bk):
            if ci < 44:
                P.op("act", lambda e: e.activation(out=U3[:, ci, :], in_=bank[:, 0:T], func=AF.Silu), reads=[bk], writes=[Uk[ci]])
            else:
                f = ci - 44
                P.op("dve", lambda e: e.tensor_tensor(out=U3[:, f, :], in0=bank[:, 0:T], in1=U3[:, f, :], op=ALU.mult),
                     reads=[bk, Uk[f]], writes=[Uk[f]])
        stream_matmul_fm(P, PS, w_f1, 0, 2 * DFF, 16, h1_16, h1k, T, evac_f, wv(16, 256), "W", WC=256)

        def evac_2(ci, hf, bank, bk):
            P.op("dve", lambda e: e.scalar_tensor_tensor(out=B[:, ci, :], in0=B[:, ci, :], scalar=ALPHA, in1=bank[:, 0:T],
                                                         op0=ALU.mult, op1=ALU.add), reads=[bk, Bk[ci]], writes=[Bk[ci]])
        stream_matmul_fm(P, PS, w_f2, 0, D, 44, U3, Uk, T, evac_2, wv(44, 128), "W", WC=128)
        layernorm_fm(P, PS, B, ("B",), T, g2, b2, ["g2", "b2"], ones, B, ("B",), h1_16, ("h1_16",), A, ("A",))
        toks.append(P.dma("sp", h2T[:, ts].rearrange("(k p) t -> p k t", p=128), B[:, :, :], reads=Bk))
        if with_next_proj:
            P.dma("act", h2s[:, ts].rearrange("(k p) t -> p k t", p=128), B[:, :, :], reads=Bk, writes=[("h2s", half)])
    if with_next_proj:
        h16 = U[:, 0:16 * TPC].rearrange("p (k t) -> p k t", t=TPC)
        for k in range(16):
            P.dma("pool", h16[:, k, :], h2s[k * 128:(k + 1) * 128, :], reads=[("h2s", 0), ("h2s", 1)], writes=Uk[2 * k:2 * k + 2])
        obufs = [P.sb([128, 512], F32, f"ob{i}") for i in range(2)]
        toks += emit_inproj(P, PS, h16, Uk[0:32], w_in, projT, wv(16, 256), obufs, wname="W")
    P.emit(toks)
    return nc


def prep_C(l, inp, projT, hT, yf, yb, na, hf, hb, with_next):
    yfT, ybT, naT = np.ascontiguousarray(yf.T), np.ascontiguousarray(yb.T), np.ascontiguousarray(na.T)
    fm16 = lambda v: np.ascontiguousarray(v.reshape(16, 128).T)
    shared = {
        "c_gb": np.ascontiguousarray(inp["gate_b"][l].reshape(48, 128).T),
        "w_br": np.ascontiguousarray(inp["w_branch"][l].reshape(3072, D)), "w_out": inp["w_out"][l],
        "ln1g": fm16(inp["ln1_g"][l]), "ln1b": fm16(inp["ln1_b"][l]), "ln2g": fm16(inp["ln2_g"][l]), "ln2b": fm16(inp["ln2_b"][l]),
        "w_f1": inp["w_ffn_in"][l], "w_f2": inp["w_ffn_out"][l],
    }
    if with_next:
        shared["w_in"] = inp["w_in"][l + 1]
    ims = []
    for c in range(NCORES):
        ts = slice(c * TPC, (c + 1) * TPC)
        cc = lambda a: np.ascontiguousarray(a[:, ts])
        m = dict(shared)
        m.update({"c_yf": cc(yfT), "c_yb": cc(ybT), "c_g": cc(projT[3072:4096]), "c_na": cc(naT), "c_hf": cc(hf), "c_hb": cc(hb),
                  "c_ly": cc(projT[8192:9216]), "c_gp": cc(projT[9216:15360]), "c_h": cc(hT)})
        ims.append(m)
    return ims


def _run(nc, ims):
    return run_bass_kernel_spmd(nc, ims, core_ids=list(range(NCORES))).results


def kernel(**inputs):
    inp = {k: np.asarray(v) for k, v in inputs.items()}
    x = inp["x"][0]
    fm16 = lambda v: np.ascontiguousarray(v.reshape(16, 128).T)
    ims = [{"xT": np.ascontiguousarray(x[c * TPC:(c + 1) * TPC].T), "lng": fm16(inp["ln_in_g"]), "lnb": fm16(inp["ln_in_b"]),
            "w_in": inp["w_in"][0]} for c in range(NCORES)]
    res = _run(build_A0(), ims)
    hT = np.concatenate([r["hT"] for r in res], axis=1)
    projT = np.concatenate([r["projT"] for r in res], axis=1)
    for l in range(DEPTH):
        resB = _run(build_B(), prep_B(projT, l, inp))
        yf, yb, na, hf, hb = post_B(resB)
        del resB
        nxt = l + 1 < DEPTH
        res = _run(build_C(nxt), prep_C(l, inp, projT, hT, yf, yb, na, hf, hb, nxt))
        del yf, yb, na, hf, hb
        hT = np.concatenate([r["h2T"] for r in res], axis=1)
        if nxt:
            projT = np.concatenate([r["projT"] for r in res], axis=1)
        del res
    return np.ascontiguousarray(hT.T)[None].astype(np.float32)
```

```python
import contextlib
import numpy as np
import concourse.bass as bass
import concourse.mybir as mybir
from concourse.bass_utils import run_bass_kernel_spmd

F32 = mybir.dt.float32
BF16 = mybir.dt.bfloat16
AF = mybir.ActivationFunctionType
ALU = mybir.AluOpType
AX = mybir.AxisListType

NCORES = 8
D = 2048
SEQ = 8192
TPC = SEQ // NCORES
DEPTH = 4
IN_COLS = 15360
DFF = 5632
ALPHA = (2 * DEPTH) ** 0.25
EPS = 1e-5
GRID_W = 64


class Prog:
    ENGS = ("pe", "act", "dve", "pool", "sp")
    DMA_RING = 16

    def __init__(self, nc):
        self.nc = nc
        self.stack = contextlib.ExitStack()
        self.ops = {e: [] for e in self.ENGS}
        self.cnt = {e: 0 for e in self.ENGS}
        self.dcnt = {e: 0 for e in self.ENGS}
        self.known = {e: {} for e in self.ENGS}
        self.last_w = {}
        self.readers = {}
        self.sem = {e: self.stack.enter_context(nc.semaphore("s_" + e)) for e in self.ENGS}
        self.dsem = {}
        for q in ("sp", "pool", "act"):
            self.dsem[q] = [self.stack.enter_context(nc.semaphore(f"d_{q}{i}")) for i in range(self.DMA_RING)]
        self.out_tokens = []
        self._n = 0

    def sb(self, shape, dtype, name=None):
        self._n += 1
        return self.stack.enter_context(self.nc.sbuf_tensor("sb_" + (name or f"t{self._n}"), list(shape), dtype))

    def ps(self, name=None):
        self._n += 1
        return self.stack.enter_context(self.nc.psum_tensor(name or f"p{self._n}", [128, 512], F32))

    def _tok_sem(self, tok):
        if tok[0] == "e":
            return ("e", tok[1]), tok[2]
        return ("d", tok[1], tok[2] % self.DMA_RING), 16 * (tok[2] // self.DMA_RING + 1)

    def _waits(self, eng, reads, writes):
        toks = set()
        for r in reads:
            if r in self.last_w:
                toks.add(self.last_w[r])
        for w in writes:
            if w in self.last_w:
                toks.add(self.last_w[w])
            for t in self.readers.get(w, ()):
                toks.add(t)
        need = {}
        for t in toks:
            key, val = self._tok_sem(t)
            if self.known[eng].get(key, 0) >= val:
                continue
            need[key] = max(need.get(key, 0), val)
        for key, val in need.items():
            self.known[eng][key] = val
        return list(need.items())

    def _commit(self, tok, reads, writes):
        for r in reads:
            self.readers.setdefault(r, []).append(tok)
        for w in writes:
            self.last_w[w] = tok
            self.readers[w] = []

    def op(self, eng, fn, reads=(), writes=()):
        waits = self._waits(eng, reads, writes)
        self.cnt[eng] += 1
        tok = ("e", eng, self.cnt[eng])
        self.ops[eng].append(("op", fn, waits))
        self._commit(tok, reads, writes)
        return tok

    def dma(self, q, out, in_, reads=(), writes=(), **kw):
        idx = self.dcnt[q]
        waits = self._waits(q, reads, writes)
        if idx >= self.DMA_RING:
            key, val = self._tok_sem(("d", q, idx - self.DMA_RING))
            if self.known[q].get(key, 0) < val:
                self.known[q][key] = val
                waits.append((key, val))
        self.dcnt[q] += 1
        tok = ("d", q, idx)
        self.ops[q].append(("dma", (out, in_, kw, idx), waits))
        self._commit(tok, reads, writes)
        return tok

    def _semh(self, key):
        return self.sem[key[1]] if key[0] == "e" else self.dsem[key[1]][key[2]]

    def emit(self, final_tokens):
        nc = self.nc
        emap = {"pe": "tensor", "act": "scalar", "dve": "vector", "pool": "gpsimd", "sp": "sync"}
        fin = {}
        for t in final_tokens:
            key, val = self._tok_sem(t)
            fin[key] = max(fin.get(key, 0), val)
        with nc.Block() as block:
            for e in self.ENGS:
                def body(eng, e=e):
                    for kind, payload, waits in self.ops[e]:
                        for key, val in waits:
                            eng.wait_ge(self._semh(key), val)
                        if kind == "op":
                            ins = payload(eng)
                            ins.then_inc(self.sem[e], 1)
                        else:
                            out, in_, kw, idx = payload
                            eng.dma_start(out=out, in_=in_, **kw).then_inc(self.dsem[e][idx % self.DMA_RING], 16)
                    if e == "sp":
                        for key, val in fin.items():
                            eng.wait_ge(self._semh(key), val)
                getattr(block, emap[e])(body)
        self.stack.close()


class PsumRing:
    def __init__(self, P, n=8):
        self.P = P
        self.banks = [P.ps(f"bank{i}") for i in range(n)]
        self.i = 0

    def next(self):
        b = self.banks[self.i % len(self.banks)]
        k = ("psum", self.i % len(self.banks))
        self.i += 1
        return b, k


def load_consts(P, nc_in, name, shape_free, q="sp"):
    t = P.sb([128, shape_free], F32, name)
    P.dma(q, t[:, :], nc_in[:, :], writes=[name])
    return t


def layernorm_fm(P, PS, x, xkey, T, gam, bet, gkeys, ones, out32, out32key, out16, out16key, scr, scrkey, KC=16):
    nfeat = KC * 128
    xk = [xkey + (k,) for k in range(KC)]
    sk = [scrkey + (k,) for k in range(KC)]
    P.op("act", lambda e: e.activation(out=scr[:, 0:KC, 0:T], in_=x[:, 0:KC, 0:T], func=AF.Square),
         reads=xk, writes=sk)
    b_sum, k_sum = PS.next()
    b_sq, k_sq = PS.next()

    def mm_sum(e):
        for k in range(KC):
            ins = e.matmul(b_sum[:, 0:T], lhsT=ones[:, :], rhs=x[:, k, 0:T], start=(k == 0), stop=(k == KC - 1))
        return ins

    def mm_sq(e):
        for k in range(KC):
            ins = e.matmul(b_sq[:, 0:T], lhsT=ones[:, :], rhs=scr[:, k, 0:T], start=(k == 0), stop=(k == KC - 1))
        return ins
    P.op("pe", mm_sum, reads=xk + ["ones"], writes=[k_sum])
    P.op("pe", mm_sq, reads=sk + ["ones"], writes=[k_sq])
    if not hasattr(P, "_ln_tmp"):
        P._ln_tmp = (P.sb([128, 512], F32, "ln_mean"), P.sb([128, 512], F32, "ln_rstd"))
    mean, rstd = P._ln_tmp
    mk, rk = "ln_mean", "ln_rstd"
    P.op("act", lambda e: e.mul(out=mean[:, 0:T], in_=b_sum[:, 0:T], mul=1.0 / nfeat), reads=[k_sum], writes=[mk])
    P.op("dve", lambda e: e.tensor_tensor(out=rstd[:, 0:T], in0=mean[:, 0:T], in1=mean[:, 0:T], op=ALU.mult),
         reads=[mk], writes=[rk])
    P.op("dve", lambda e: e.scalar_tensor_tensor(out=rstd[:, 0:T], in0=b_sq[:, 0:T], scalar=1.0 / nfeat, in1=rstd[:, 0:T],
                                                 op0=ALU.mult, op1=ALU.subtract), reads=[k_sq, rk], writes=[rk])
    P.op("dve", lambda e: e.tensor_scalar(out=rstd[:, 0:T], in0=rstd[:, 0:T], scalar1=EPS, scalar2=None,
                                          op0=ALU.add), reads=[rk], writes=[rk])
    P.op("act", lambda e: e.activation(out=rstd[:, 0:T], in_=rstd[:, 0:T], func=AF.Sqrt), reads=[rk], writes=[rk])
    P.op("dve", lambda e: e.reciprocal(out=rstd[:, 0:T], in_=rstd[:, 0:T]), reads=[rk], writes=[rk])
    for k in range(KC):
        P.op("dve", lambda e, k=k: e.tensor_tensor(out=scr[:, k, 0:T], in0=x[:, k, 0:T], in1=mean[:, 0:T], op=ALU.subtract),
             reads=[xk[k], mk], writes=[sk[k]])
        P.op("dve", lambda e, k=k: e.tensor_tensor(out=scr[:, k, 0:T], in0=scr[:, k, 0:T], in1=rstd[:, 0:T], op=ALU.mult),
             reads=[sk[k], rk], writes=[sk[k]])
        P.op("act", lambda e, k=k: e.activation(out=out32[:, k, 0:T], in_=scr[:, k, 0:T], func=AF.Identity,
                                                scale=gam[:, k:k + 1], bias=bet[:, k:k + 1]),
             reads=[sk[k]] + list(gkeys), writes=[out32key + (k,)])
        P.op("act", lambda e, k=k: e.activation(out=out16[:, k, 0:T], in_=scr[:, k, 0:T], func=AF.Identity,
                                                scale=gam[:, k:k + 1], bias=bet[:, k:k + 1]),
             reads=[sk[k]] + list(gkeys), writes=[out16key + (k,)])


def stream_matmul_fm(P, PS, w_dram, col0, ncols, KC, rhs16, rhs_keys, T, evac, wbufs, wname, WC=256, TH=512):
    assert ncols % 128 == 0
    c = 0
    gi = getattr(P, "_wcount", 0)
    while c < ncols:
        w = min(WC, ncols - c)
        wb = wbufs[gi % len(wbufs)]
        wk = (wname, gi % len(wbufs))
        gi += 1
        src = w_dram[:, col0 + c: col0 + c + w].rearrange("(k p) c -> p k c", p=128)
        P.dma("pool", wb[:, 0:KC, 0:w], src, writes=[wk])
        for cc in range(w // 128):
            for half in range(T // TH):
                bank, bk = PS.next()

                def mm(e, cc=cc, half=half, bank=bank, wb=wb):
                    for k in range(KC):
                        ins = e.matmul(bank[:, 0:TH], lhsT=wb[:, k, cc * 128:(cc + 1) * 128],
                                       rhs=rhs16[:, k, half * TH:(half + 1) * TH], start=(k == 0), stop=(k == KC - 1))
                    return ins
                P.op("pe", mm, reads=[wk] + list(rhs_keys), writes=[bk])
                evac((c // 128) + cc, half, bank, bk)
        c += w
    P._wcount = gi


def emit_inproj(P, PS, h16, h16keys, w_in, projT, wbufs, obufs, wname="w_in"):
    state = {"i": 0}
    toks = []

    def evac(ci, half, bank, bk):
        i = state["i"]
        state["i"] += 1
        ob = obufs[i % len(obufs)]
        ok = ("projo", i % len(obufs))
        eng = "act" if i % 2 == 0 else "dve"
        if eng == "act":
            P.op("act", lambda e: e.copy(out=ob[:, :], in_=bank[:, 0:512]), reads=[bk], writes=[ok])
        else:
            P.op("dve", lambda e: e.tensor_copy(out=ob[:, :], in_=bank[:, 0:512]), reads=[bk], writes=[ok])
        toks.append(P.dma("sp", projT[ci * 128:(ci + 1) * 128, half * 512:(half + 1) * 512], ob[:, :], reads=[ok]))
    stream_matmul_fm(P, PS, w_in, 0, IN_COLS, 16, h16, h16keys, TPC, evac, wbufs, wname)
    return toks


def build_A0():
    nc = bass.Bass("TRN2", target_bir_lowering=False)
    xT = nc.dram_tensor("xT", [D, TPC], F32, kind="ExternalInput").ap()
    lng = nc.dram_tensor("lng", [128, 16], F32, kind="ExternalInput").ap()
    lnb = nc.dram_tensor("lnb", [128, 16], F32, kind="ExternalInput").ap()
    w_in = nc.dram_tensor("w_in", [D, IN_COLS], F32, kind="ExternalInput").ap()
    projT = nc.dram_tensor("projT", [IN_COLS, TPC], F32, kind="ExternalOutput").ap()
    hT = nc.dram_tensor("hT", [D, TPC], F32, kind="ExternalOutput").ap()
    P = Prog(nc)
    PS = PsumRing(P)
    ones = P.sb([128, 128], F32, "ones")
    P.op("pool", lambda e: e.memset(ones[:, :], 1.0), writes=["ones"])
    gam = load_consts(P, lng, "gam", 16)
    bet = load_consts(P, lnb, "bet", 16)
    h16 = P.sb([128, 16, TPC], BF16, "h16")
    x32 = P.sb([128, 16, 512], F32, "x32")
    scr = P.sb([128, 16, 512], F32, "scr")
    toks = []
    x32k = [("x32", k) for k in range(16)]
    for half in range(2):
        P.dma("sp", x32[:, :, :], xT[:, half * 512:(half + 1) * 512].rearrange("(k p) t -> p k t", p=128), writes=x32k)
        h16v = h16[:, :, half * 512:(half + 1) * 512]
        layernorm_fm(P, PS, x32, ("x32",), 512, gam, bet, ["gam", "bet"], ones, x32, ("x32",), h16v, ("h16", half),
                     scr, ("scr",))
        toks.append(P.dma("sp", hT[:, half * 512:(half + 1) * 512].rearrange("(k p) t -> p k t", p=128), x32[:, :, :],
                          reads=x32k))
    wbufs = [P.sb([128, 16, 256], BF16, f"wb{i}") for i in range(2)]
    obufs = [P.sb([128, 512], F32, f"ob{i}") for i in range(4)]
    h16keys = [("h16", hf, k) for hf in range(2) for k in range(16)]
    toks += emit_inproj(P, PS, h16, h16keys, w_in, projT, wbufs, obufs)
    P.emit(toks)
    return nc


NEG = -30000.0
RB = 512
NCH = RB // 128
LB = 1024


def emit_retention(P, PS, nc, io):
    qT, kT, ktm, v, cosT, sinT, costm, sintm, logit, diffT, keepT, idx1, kidx, y = io
    cst = P.sb([128, 8], F32, "r_cst")
    dT = P.sb([128, 128], F32, "r_diffT")
    kpT = P.sb([128, 128], F32, "r_keepT")
    i1 = P.sb([128, 128], F32, "r_idx1")
    maskT = P.sb([128, 128], F32, "r_maskT")
    qdec = P.sb([128, RB], F32, "r_qdec")
    P.dma("sp", cst[:, 0:1], logit[:, :], writes=["r_c0"])
    P.dma("sp", cst[:, 6:7], kidx[:, :], writes=["r_c6"])
    P.dma("sp", dT[:, :], diffT[:, :], writes=["r_dT"])
    P.dma("sp", kpT[:, :], keepT[:, :], writes=["r_kpT"])
    P.dma("sp", i1[:, :], idx1[:, :], writes=["r_i1"])
    P.op("act", lambda e: e.activation(out=cst[:, 1:2], in_=cst[:, 0:1], func=AF.Exp, scale=-1.0), reads=["r_c0"], writes=["r_c1"])
    P.op("act", lambda e: e.activation(out=cst[:, 2:3], in_=cst[:, 1:2], func=AF.Ln, bias=1.0), reads=["r_c1"], writes=["r_c2"])
    P.op("act", lambda e: e.mul(out=cst[:, 3:4], in_=cst[:, 2:3], mul=-1.0), reads=["r_c2"], writes=["r_logg"])
    P.op("act", lambda e: e.activation(out=maskT[:, :], in_=dT[:, :], func=AF.Exp, scale=cst[:, 3:4]),
         reads=["r_dT", "r_logg"], writes=["r_maskT"])
    P.op("dve", lambda e: e.tensor_tensor(out=maskT[:, :], in0=maskT[:, :], in1=kpT[:, :], op=ALU.mult),
         reads=["r_maskT", "r_kpT"], writes=["r_maskT"])
    for n in range(RB // 128):
        P.op("act", lambda e, n=n: e.activation(out=qdec[:, n * 128:(n + 1) * 128], in_=i1[:, :], func=AF.Exp, scale=cst[:, 3:4]),
             reads=["r_i1", "r_logg"], writes=[("r_qdec", n)])
    qdk = [("r_qdec", n) for n in range(RB // 128)]
    P.op("act", lambda e: e.activation(out=cst[:, 4:5], in_=cst[:, 6:7], func=AF.Exp, scale=cst[:, 3:4]),
         reads=["r_c6", "r_logg"], writes=["r_kdec"])
    P.op("act", lambda e: e.activation(out=cst[:, 5:6], in_=cst[:, 3:4], func=AF.Exp, scale=128.0),
         reads=["r_logg"], writes=["r_cdec"])
    S32 = P.sb([128, 512], F32, "r_S32")
    S16 = P.sb([128, 512], BF16, "r_S16")
    P.op("pool", lambda e: e.memset(S32[:, :], 0.0), writes=["r_S32"])
    P.op("pool", lambda e: e.memset(S16[:, :], 0.0), writes=["r_S16"])
    qr = P.sb([128, 2, RB], F32, "r_qr")
    kr = P.sb([128, 2, RB], F32, "r_kr")
    ktr = P.sb([128, NCH, 256], F32, "r_ktr")
    vr = P.sb([128, NCH, 256], F32, "r_vr")
    cs = P.sb([128, RB], F32, "r_cs")
    sn = P.sb([128, RB], F32, "r_sn")
    cst_ = P.sb([128, NCH, 128], F32, "r_cstm")
    snt = P.sb([128, NCH, 128], F32, "r_sntm")
    ta = P.sb([128, RB], F32, "r_ta")
    tb = P.sb([128, RB], F32, "r_tb")
    tc_ = P.sb([128, RB], F32, "r_tc")
    q16 = P.sb([128, 2, RB], BF16, "r_q16")
    qd16 = P.sb([128, 2, RB], BF16, "r_qd16")
    k16 = P.sb([128, 2, RB], BF16, "r_k16")
    kt16 = P.sb([128, NCH, 256], BF16, "r_kt16")
    v16 = P.sb([128, NCH, 256], BF16, "r_v16")
    vd16 = P.sb([128, NCH, 256], BF16, "r_vd16")
    sm16 = [P.sb([128, 128], BF16, f"r_sm16_{i}") for i in range(2)]
    yb = [P.sb([128, NCH, 256], F32, f"r_yb{i}") for i in range(2)]
    toks = []
    KS = 256 ** -0.5
    for b in range(SEQ // RB):
        t0 = b * RB
        P.dma("sp", qr[:, :, :], qT[:, t0:t0 + RB].rearrange("(j p) t -> p j t", p=128), writes=["r_qr"])
        P.dma("sp", kr[:, :, :], kT[:, t0:t0 + RB].rearrange("(j p) t -> p j t", p=128), writes=["r_kr"])
        P.dma("act", ktr[:, :, :], ktm[t0:t0 + RB, :].rearrange("(n p) d -> p n d", p=128), writes=["r_ktr"])
        P.dma("act", vr[:, :, :], v[t0:t0 + RB, :].rearrange("(n p) d -> p n d", p=128), writes=["r_vr"])
        P.dma("sp", cs[:, :], cosT[:, t0:t0 + RB], writes=["r_cs"])
        P.dma("sp", sn[:, :], sinT[:, t0:t0 + RB], writes=["r_sn"])
        P.dma("act", cst_[:, :, :], costm[t0:t0 + RB, :].rearrange("(n p) d -> p n d", p=128), writes=["r_cstm"])
        P.dma("act", snt[:, :, :], sintm[t0:t0 + RB, :].rearrange("(n p) d -> p n d", p=128), writes=["r_sntm"])

        def rot_fm(src, skey, outs, okeys, scale):
            t1, t2 = src[:, 0, :], src[:, 1, :]
            P.op("dve", lambda e: e.tensor_tensor(out=ta[:, :], in0=t1, in1=cs[:, :], op=ALU.mult), reads=[skey, "r_cs"], writes=["r_ta"])
            P.op("pool", lambda e: e.tensor_tensor(out=tb[:, :], in0=t2, in1=sn[:, :], op=ALU.mult), reads=[skey, "r_sn"], writes=["r_tb"])
            P.op("dve", lambda e: e.tensor_tensor(out=ta[:, :], in0=ta[:, :], in1=tb[:, :], op=ALU.subtract), reads=["r_ta", "r_tb"], writes=["r_ta"])
            P.op("pool", lambda e: e.tensor_tensor(out=tb[:, :], in0=t1, in1=sn[:, :], op=ALU.mult), reads=[skey, "r_sn", "r_ta"], writes=["r_tb"])
            P.op("dve", lambda e: e.tensor_tensor(out=tc_[:, :], in0=t2, in1=cs[:, :], op=ALU.mult), reads=[skey, "r_cs"], writes=["r_tc"])
            P.op("dve", lambda e: e.tensor_tensor(out=tb[:, :], in0=tb[:, :], in1=tc_[:, :], op=ALU.add), reads=["r_tb", "r_tc"], writes=["r_tb"])
            for (o, ok, extra) in outs:
                if extra is None:
                    P.op("act", lambda e, o=o: e.mul(out=o[:, 0, :], in_=ta[:, :], mul=scale), reads=["r_ta"], writes=[ok + "0"])
                    P.op("act", lambda e, o=o: e.mul(out=o[:, 1, :], in_=tb[:, :], mul=scale), reads=["r_tb"], writes=[ok + "1"])
                else:
                    P.op("dve", lambda e, o=o: e.tensor_tensor(out=o[:, 0, :], in0=ta[:, :], in1=qdec[:, :], op=ALU.mult), reads=["r_ta"] + qdk, writes=[ok + "0"])
                    P.op("pool", lambda e, o=o: e.tensor_tensor(out=o[:, 1, :], in0=tb[:, :], in1=qdec[:, :], op=ALU.mult), reads=["r_tb"] + qdk, writes=[ok + "1"])
        rot_fm(qr, "r_qr", [(q16, "r_q16", None), (qd16, "r_qd16", True)], None, 1.0)
        rot_fm(kr, "r_kr", [(k16, "r_k16", None)], None, KS)
        ta3 = ta[:, :].rearrange("p (n d) -> p n d", d=128)
        tb3 = tb[:, :].rearrange("p (n d) -> p n d", d=128)
        tc3 = tc_[:, :].rearrange("p (n d) -> p n d", d=128)
        t1, t2 = ktr[:, :, 0:128], ktr[:, :, 128:256]
        P.op("dve", lambda e: e.tensor_tensor(out=ta3, in0=t1, in1=cst_[:, :, :], op=ALU.mult), reads=["r_ktr", "r_cstm"], writes=["r_ta"])
        P.op("pool", lambda e: e.tensor_tensor(out=tb3, in0=t2, in1=snt[:, :, :], op=ALU.mult), reads=["r_ktr", "r_sntm"], writes=["r_tb"])
        P.op("dve", lambda e: e.tensor_tensor(out=ta3, in0=ta3, in1=tb3, op=ALU.subtract), reads=["r_ta", "r_tb"], writes=["r_ta"])
        P.op("act", lambda e: e.mul(out=kt16[:, :, 0:128], in_=ta3, mul=KS), reads=["r_ta"], writes=["r_kt16a"])
        P.op("pool", lambda e: e.tensor_tensor(out=tb3, in0=t1, in1=snt[:, :, :], op=ALU.mult), reads=["r_ktr", "r_sntm", "r_ta"], writes=["r_tb"])
        P.op("dve", lambda e: e.tensor_tensor(out=tc3, in0=t2, in1=cst_[:, :, :], op=ALU.mult), reads=["r_ktr", "r_cstm"], writes=["r_tc"])
        P.op("dve", lambda e: e.tensor_tensor(out=tb3, in0=tb3, in1=tc3, op=ALU.add), reads=["r_tb", "r_tc"], writes=["r_tb"])
        P.op("act", lambda e: e.mul(out=kt16[:, :, 128:256], in_=tb3, mul=KS), reads=["r_tb"], writes=["r_kt16b"])
        P.op("pool", lambda e: e.tensor_copy(out=v16[:, :, :], in_=vr[:, :, :]), reads=["r_vr"], writes=["r_v16"])
        P.op("act", lambda e: e.activation(out=vd16[:, :, :], in_=vr[:, :, :], func=AF.Copy, scale=cst[:, 4:5]),
             reads=["r_vr", "r_kdec"], writes=["r_vd16"])
        ybuf = yb[b % 2]
        ybk = ("r_yb", b % 2)
        for n in range(RB // 128):
            cs_ = slice(n * 128, (n + 1) * 128)
            bs, bsk = PS.next()

            def mm_s(e, cs_=cs_, bs=bs):
                for j in range(2):
                    ins = e.matmul(bs[:, 0:128], lhsT=k16[:, j, cs_], rhs=q16[:, j, cs_], start=(j == 0), stop=(j == 1))
                return ins
            P.op("pe", mm_s, reads=["r_k160", "r_k161", "r_q160", "r_q161"], writes=[bsk])
            sm = sm16[n % 2]
            smk = ("r_sm", n % 2)
            P.op("dve", lambda e, sm=sm, bs=bs: e.tensor_tensor(out=sm[:, :], in0=bs[:, 0:128], in1=maskT[:, :], op=ALU.mult),
                 reads=[bsk, "r_maskT"], writes=[smk])
            by, byk = PS.next()

            def mm_y(e, cs_=cs_, by=by, sm=sm, n=n):
                e.matmul(by[:, 0:256], lhsT=sm[:, :], rhs=v16[:, n, :], start=True, stop=False)
                for j in range(2):
                    ins = e.matmul(by[:, 0:256], lhsT=qd16[:, j, cs_], rhs=S16[:, j * 256:(j + 1) * 256], start=False, stop=(j == 1))
                return ins
            P.op("pe", mm_y, reads=[smk, "r_v16", "r_qd160", "r_qd161", "r_S16"], writes=[byk])
            P.op("act", lambda e, by=by, n=n, ybuf=ybuf: e.copy(out=ybuf[:, n, :], in_=by[:, 0:256]), reads=[byk], writes=[ybk + (n,)])
            bkv, bkvk = PS.next()

            def mm_kv(e, bkv=bkv, n=n):
                for j in range(2):
                    ins = e.matmul(bkv[:, j * 256:(j + 1) * 256], lhsT=kt16[:, n, j * 128:(j + 1) * 128], rhs=vd16[:, n, :],
                                   start=True, stop=True)
                return ins
            P.op("pe", mm_kv, reads=["r_kt16a", "r_kt16b", "r_vd16"], writes=[bkvk])
            P.op("dve", lambda e, bkv=bkv: e.scalar_tensor_tensor(out=S32[:, :], in0=S32[:, :], scalar=cst[:, 5:6], in1=bkv[:, :],
                                                                 op0=ALU.mult, op1=ALU.add),
                 reads=["r_S32", "r_cdec", bkvk], writes=["r_S32"])
            P.op("pool", lambda e: e.tensor_copy(out=S16[:, :], in_=S32[:, :]), reads=["r_S32"], writes=["r_S16"])
            yield
        toks.append(P.dma("sp", y[t0:t0 + RB, :].rearrange("(n p) e -> p n e", p=128), ybuf[:, :, :],
                          reads=[ybk + (n,) for n in range(NCH)]))
    return toks


def emit_na(P, PS, nc, io):
    qT, kT, v64, biasd, o = io
    q16 = P.sb([128, SEQ], BF16, "n_q16")
    k16 = P.sb([128, SEQ], BF16, "n_k16")
    va = P.sb([64, 128, 132], BF16, "n_va")
    bias = P.sb([64, 8, 512], F32, "n_bias")
    for i in range(4):
        sl = slice(i * 2048, (i + 1) * 2048)
        P.dma("pool", q16[:, sl], qT[:, sl], writes=[("n_q16", i)], max_dma_last_dim=4096)
        P.dma("pool", k16[:, sl], kT[:, sl], writes=[("n_k16", i)], max_dma_last_dim=4096)
    qk_keys = [("n_q16", i) for i in range(4)] + [("n_k16", i) for i in range(4)]
    P.op("pool", lambda e: e.memset(va[:, :, 128:132], 1.0), writes=["n_va1"])
    for i in range(4):
        P.dma("pool", va[:, i * 32:(i + 1) * 32, 0:128], v64[:, i * 32:(i + 1) * 32, :], writes=[("n_va", i)])
    va_keys = ["n_va1"] + [("n_va", i) for i in range(4)]
    P.dma("sp", bias[:, :, :], biasd[:, :, :], writes=["n_bias"])
    st = [P.sb([64, 512], F32, f"n_st{i}") for i in range(2)]
    e16 = [P.sb([64, 512], BF16, f"n_e16_{i}") for i in range(2)]
    rc = [P.sb([64, 1], F32, f"n_rc{i}") for i in range(2)]
    ob = [P.sb([64, 8, 128], F32, f"n_ob{i}") for i in range(2)]
    toks = []
    scale = 128 ** -0.5
    nrows = SEQ // GRID_W
    for r in range(nrows):
        rs = min(max(r - 4, 0), nrows - 8)
        cls = r - rs
        bs, bsk = PS.next()

        def mm_s(e, r=r, rs=rs, bs=bs):
            for i in range(8):
                ks = slice((rs + i) * 64, (rs + i + 1) * 64)
                ins = e.matmul(bs[0:64, i * 64:(i + 1) * 64], lhsT=k16[:, ks], rhs=q16[:, r * 64:(r + 1) * 64], start=True, stop=True)
            return ins
        P.op("pe", mm_s, reads=qk_keys, writes=[bsk])
        s_, sk = st[r % 2], ("n_st", r % 2)
        P.op("dve", lambda e, s_=s_, bs=bs, cls=cls: e.scalar_tensor_tensor(out=s_[:, :], in0=bs[0:64, :], scalar=scale, in1=bias[:, cls, :],
                                                                            op0=ALU.mult, op1=ALU.add),
             reads=[bsk, "n_bias"], writes=[sk])
        e_, ek = e16[r % 2], ("n_e16", r % 2)
        P.op("act", lambda e, s_=s_, e_=e_: e.activation(out=e_[:, :], in_=s_[:, :], func=AF.Exp), reads=[sk], writes=[ek])
        bo, bok = PS.next()

        def mm_o(e, rs=rs, bo=bo, e_=e_):
            for i in range(8):
                ins = e.matmul(bo[0:64, 0:129], lhsT=e_[:, i * 64:(i + 1) * 64], rhs=va[:, rs + i, 0:129], start=(i == 0), stop=(i == 7))
            return ins
        P.op("pe", mm_o, reads=[ek] + va_keys, writes=[bok])
        rc_, rck = rc[r % 2], ("n_rc", r % 2)
        P.op("dve", lambda e, rc_=rc_, bo=bo: e.reciprocal(out=rc_[:, :], in_=bo[0:64, 128:129]), reads=[bok], writes=[rck])
        obuf, obk = ob[(r // 8) % 2], ("n_ob", (r // 8) % 2)
        P.op("act", lambda e, obuf=obuf, bo=bo, rc_=rc_, r=r: e.activation(out=obuf[:, r % 8, :], in_=bo[0:64, 0:128], func=AF.Copy, scale=rc_[:, 0:1]),
             reads=[bok, rck], writes=[obk + (r % 8,)])
        if r % 2 == 1:
            yield
        if r % 8 == 7:
            g = r // 8
            toks.append(P.dma("sp", o[g * 512:(g + 1) * 512, :].rearrange("(r p) d -> p r d", p=64), obuf[:, :, :],
                              reads=[obk + (i,) for i in range(8)]))
    return toks


def emit_lru(P, PS, nc, io):
    xpf, xpb, wtap, bconv, wad, wid, gb, lam, hout = io
    tap = P.sb([128, 8], F32, "l_tap")
    bc = P.sb([128, 1], F32, "l_bc")
    gbt = P.sb([128, 4], F32, "l_gb")
    lm = P.sb([128, 8], F32, "l_lm")
    wa16 = P.sb([128, 2, 128], BF16, "l_wa16")
    wi16 = P.sb([128, 2, 128], BF16, "l_wi16")
    P.dma("sp", tap[:, :], wtap[:, :], writes=["l_tap"])
    P.dma("sp", bc[:, :], bconv[:, :], writes=["l_bc"])
    P.dma("sp", gbt[:, :], gb[:, :], writes=["l_gb"])
    P.dma("sp", lm[:, 0:2], lam[:, :], writes=["l_lm0"])
    P.dma("pool", wa16[:, :, :], wad[:, :, :], writes=["l_wa16"])
    P.dma("pool", wi16[:, :, :], wid[:, :, :], writes=["l_wi16"])
    P.op("act", lambda e: e.activation(out=lm[:, 2:4], in_=lm[:, 0:2], func=AF.Exp, scale=-1.0), reads=["l_lm0"], writes=["l_lm1"])
    P.op("act", lambda e: e.activation(out=lm[:, 4:6], in_=lm[:, 2:4], func=AF.Ln, bias=1.0), reads=["l_lm1"], writes=["l_lm2"])
    P.op("act", lambda e: e.mul(out=lm[:, 6:8], in_=lm[:, 4:6], mul=-8.0), reads=["l_lm2"], writes=["l_lm3"])
    xp = P.sb([128, LB + 3], F32, "l_xp")
    xc = P.sb([128, LB], F32, "l_xc")
    xc16 = P.sb([128, LB], BF16, "l_xc16")
    rg = P.sb([128, LB], F32, "l_rg")
    ig = P.sb([128, LB], F32, "l_ig")
    hb = [P.sb([128, LB], F32, f"l_h{i}") for i in range(2)]
    toks = []
    it = 0
    for d in range(2):
        src = xpf if d == 0 else xpb
        for b in range(SEQ // LB):
            t0 = b * LB
            P.dma("sp", xp[:, :], src[:, t0:t0 + LB + 3], writes=["l_xp"])
            P.op("dve", lambda e, d=d: e.tensor_scalar(out=xc[:, :], in0=xp[:, 0:LB], scalar1=tap[:, 4 * d:4 * d + 1], scalar2=bc[:, 0:1],
                                                       op0=ALU.mult, op1=ALU.add), reads=["l_xp", "l_tap", "l_bc"], writes=["l_xc"])
            for j in range(1, 4):
                P.op("dve", lambda e, d=d, j=j: e.scalar_tensor_tensor(out=xc[:, :], in0=xp[:, j:j + LB], scalar=tap[:, 4 * d + j:4 * d + j + 1],
                                                                       in1=xc[:, :], op0=ALU.mult, op1=ALU.add),
                     reads=["l_xp", "l_tap", "l_xc"], writes=["l_xc"])
            P.op("pool", lambda e: e.tensor_copy(out=xc16[:, :], in_=xc[:, :]), reads=["l_xc"], writes=["l_xc16"])
            for s in range(LB // 512):
                sl = slice(s * 512, (s + 1) * 512)
                br, brk = PS.next()
                bi_, bik = PS.next()
                P.op("pe", lambda e, d=d, sl=sl, br=br: e.matmul(br[:, :], lhsT=wa16[:, d, :], rhs=xc16[:, sl], start=True, stop=True),
                     reads=["l_wa16", "l_xc16"], writes=[brk])
                P.op("pe", lambda e, d=d, sl=sl, bi_=bi_: e.matmul(bi_[:, :], lhsT=wi16[:, d, :], rhs=xc16[:, sl], start=True, stop=True),
                     reads=["l_wi16", "l_xc16"], writes=[bik])
                P.op("act", lambda e, d=d, sl=sl, br=br: e.activation(out=rg[:, sl], in_=br[:, :], func=AF.Sigmoid, bias=gbt[:, 2 * d:2 * d + 1]),
                     reads=[brk, "l_gb"], writes=[("l_rg", s)])
                P.op("act", lambda e, d=d, sl=sl, bi_=bi_: e.activation(out=ig[:, sl], in_=bi_[:, :], func=AF.Sigmoid, bias=gbt[:, 2 * d + 1:2 * d + 2]),
                     reads=[bik, "l_gb"], writes=[("l_ig", s)])
                yield
            rgk = [("l_rg", s) for s in range(LB // 512)]
            igk = [("l_ig", s) for s in range(LB // 512)]
            P.op("act", lambda e, d=d: e.activation(out=rg[:, :], in_=rg[:, :], func=AF.Exp, scale=lm[:, 6 + d:7 + d]), reads=rgk + ["l_lm3"], writes=rgk)
            P.op("dve", lambda e: e.tensor_tensor(out=ig[:, :], in0=ig[:, :], in1=xc[:, :], op=ALU.mult), reads=igk + ["l_xc"], writes=igk)
            P.op("pool", lambda e: e.tensor_tensor(out=xc[:, :], in0=rg[:, :], in1=rg[:, :], op=ALU.mult), reads=rgk + igk, writes=["l_xc"])
            P.op("act", lambda e: e.activation(out=xc[:, :], in_=xc[:, :], func=AF.Sqrt, scale=-1.0, bias=1.0), reads=["l_xc"], writes=["l_xc"])
            P.op("dve", lambda e: e.tensor_tensor(out=ig[:, :], in0=ig[:, :], in1=xc[:, :], op=ALU.mult), reads=igk + ["l_xc"], writes=igk)
            h, hk = hb[it % 2], ("l_h", it % 2)
            hprev, hpk = hb[(it + 1) % 2], ("l_h", (it + 1) % 2)
            if b == 0:
                P.op("dve", lambda e, h=h: e.tensor_tensor_scan(out=h[:, :], data0=rg[:, :], data1=ig[:, :], initial=0.0,
                                                                op0=ALU.mult, op1=ALU.add), reads=rgk + igk, writes=[hk])
            else:
                P.op("dve", lambda e, h=h, hprev=hprev: e.tensor_tensor_scan(out=h[:, :], data0=rg[:, :], data1=ig[:, :],
                                                                             initial=hprev[:, LB - 1:LB], op0=ALU.mult, op1=ALU.add),
                     reads=rgk + igk + [hpk], writes=[hk])
            toks.append(P.dma("sp", hout[d, :, t0:t0 + LB], h[:, :], reads=[hk]))
            it += 1
            yield
    return toks


def build_B(parts=("ret", "na", "lru")):
    nc = bass.Bass("TRN2", target_bir_lowering=False)
    di = lambda n, s: nc.dram_tensor(n, s, F32, kind="ExternalInput").ap()
    do = lambda n, s: nc.dram_tensor(n, s, F32, kind="ExternalOutput").ap()
    P = Prog(nc)
    PS = PsumRing(P)
    toks = []
    gens = []
    if "ret" in parts:
        io = (di("r_qT", [256, SEQ]), di("r_kT", [256, SEQ]), di("r_ktm", [SEQ, 256]), di("r_v", [SEQ, 256]),
              di("r_cosT", [128, SEQ]), di("r_sinT", [128, SEQ]), di("r_costm", [SEQ, 128]), di("r_sintm", [SEQ, 128]),
              di("r_logit", [128, 1]), di("r_diffT", [128, 128]), di("r_keepT", [128, 128]), di("r_idx1", [128, 128]),
              di("r_kidx", [128, 1]), do("r_y", [SEQ, 256]))
        gens.append(emit_retention(P, PS, nc, io))
    if "na" in parts:
        io = (di("n_qT", [128, SEQ]), di("n_kT", [128, SEQ]), di("n_v64", [64, 128, 128]), di("n_bias", [64, 8, 512]),
              do("n_o", [SEQ, 128]))
        gens.append(emit_na(P, PS, nc, io))
    if "lru" in parts:
        io = (di("l_xpf", [128, SEQ + 3]), di("l_xpb", [128, SEQ + 3]), di("l_wtap", [128, 8]), di("l_bconv", [128, 1]),
              di("l_wa", [128, 2, 128]), di("l_wi", [128, 2, 128]), di("l_gb", [128, 4]), di("l_lam", [128, 2]),
              do("l_h", [2, 128, SEQ]))
        gens.append(emit_lru(P, PS, nc, io))
    while gens:
        for g in list(gens):
            try:
                next(g)
            except StopIteration as stop:
                toks += stop.value
                gens.remove(g)
    P.emit(toks)
    return nc


def _rot_tables():
    half = 128
    inv = (10000.0 ** (-np.arange(half, dtype=np.float32) / np.float32(half))).astype(np.float32)
    pos = np.arange(SEQ, dtype=np.float32)
    ang = (pos[:, None] * inv[None, :]).astype(np.float32)
    return np.cos(ang).astype(np.float32), np.sin(ang).astype(np.float32)


_CONST = {}


def _consts():
    if _CONST:
        return _CONST
    cos, sin = _rot_tables()
    _CONST["cos_tm"] = [cos, np.ascontiguousarray(cos[::-1])]
    _CONST["sin_tm"] = [sin, np.ascontiguousarray(sin[::-1])]
    _CONST["cos_fm"] = [np.ascontiguousarray(c.T) for c in _CONST["cos_tm"]]
    _CONST["sin_fm"] = [np.ascontiguousarray(c.T) for c in _CONST["sin_tm"]]
    idx = np.arange(128, dtype=np.float32)
    diff = idx[None, :] - idx[:, None]
    keep = [(diff >= 0), (diff > 0)]
    _CONST["diffT"] = [np.where(k, diff, 0.0).astype(np.float32) for k in keep]
    _CONST["keepT"] = [k.astype(np.float32) for k in keep]
    _CONST["idx1"] = np.ascontiguousarray(np.broadcast_to((idx + 1.0)[None, :], (128, 128))).astype(np.float32)
    _CONST["kidx"] = (127.0 - idx)[:, None].astype(np.float32)
    kc = np.arange(64)[:, None, None, None]
    cl = np.arange(8)[None, :, None, None]
    ki = np.arange(8)[None, None, :, None]
    q = np.arange(64)[None, None, None, :]
    cstart = np.clip(q - 8, 0, 48)
    inwin = (kc >= cstart) & (kc < cstart + 16)
    dr = ki - cl + 7 + 0 * kc + 0 * q
    dc = np.clip(kc - q, -15, 15) + 15 + 0 * cl + 0 * ki
    _CONST["na_dr"] = np.broadcast_to(dr, (64, 8, 8, 64)).copy()
    _CONST["na_dc"] = np.broadcast_to(dc, (64, 8, 8, 64)).copy()
    _CONST["na_win"] = np.broadcast_to(inwin, (64, 8, 8, 64)).copy()
    return _CONST


def prep_B(projT, l, inp):
    C = _consts()
    ims = []
    for c in range(NCORES):
        hh, dd = c // 2, c % 2
        fl = (lambda a: a[:, ::-1]) if dd else (lambda a: a)
        m = {}
        qT = fl(projT[hh * 256:(hh + 1) * 256])
        kT = fl(projT[1024 + hh * 256:1024 + (hh + 1) * 256])
        vT = fl(projT[2048 + hh * 256:2048 + (hh + 1) * 256])
        m["r_qT"] = np.ascontiguousarray(qT)
        m["r_kT"] = np.ascontiguousarray(kT)
        m["r_ktm"] = np.ascontiguousarray(kT.T)
        m["r_v"] = np.ascontiguousarray(vT.T)
        m["r_cosT"], m["r_sinT"] = C["cos_fm"][dd], C["sin_fm"][dd]
        m["r_costm"], m["r_sintm"] = C["cos_tm"][dd], C["sin_tm"][dd]
        m["r_logit"] = np.full((128, 1), inp["ret_decay"][l, dd, hh], np.float32)
        m["r_diffT"], m["r_keepT"] = C["diffT"][dd], C["keepT"][dd]
        m["r_idx1"], m["r_kidx"] = C["idx1"], C["kidx"]
        m["n_qT"] = np.ascontiguousarray(projT[4096 + c * 128:4096 + (c + 1) * 128])
        m["n_kT"] = np.ascontiguousarray(projT[5120 + c * 128:5120 + (c + 1) * 128])
        vh = projT[6144 + c * 128:6144 + (c + 1) * 128]
        m["n_v64"] = np.ascontiguousarray(vh.T.reshape(128, 64, 128).transpose(1, 0, 2))
        rpb = inp["na_rpb"][l, c]
        drv = np.clip(C["na_dr"], 0, 14)
        b = rpb[drv, C["na_dc"]]
        valid = C["na_win"] & (C["na_dr"] >= 0) & (C["na_dr"] <= 14)
        b = np.where(valid, b, np.float32(NEG)).astype(np.float32)
        m["n_bias"] = np.ascontiguousarray(b.reshape(64, 8, 512))
        x = projT[7168 + c * 128:7168 + (c + 1) * 128]
        xpf = np.zeros((128, SEQ + 3), np.float32)
        xpf[:, 2:2 + SEQ] = x
        xpb = np.zeros((128, SEQ + 3), np.float32)
        xpb[:, 1:1 + SEQ] = x[:, ::-1]
        m["l_xpf"], m["l_xpb"] = xpf, xpb
        wc = inp["w_conv"][l][:, c * 128:(c + 1) * 128]
        m["l_wtap"] = np.ascontiguousarray(np.concatenate([wc.T, wc[::-1].T], axis=1))
        m["l_bconv"] = np.ascontiguousarray(inp["b_conv"][l][c * 128:(c + 1) * 128, None])
        m["l_wa"] = np.ascontiguousarray(inp["lru_wa"][l][:, c].transpose(1, 0, 2))
        m["l_wi"] = np.ascontiguousarray(inp["lru_wi"][l][:, c].transpose(1, 0, 2))
        sl = slice(c * 128, (c + 1) * 128)
        m["l_gb"] = np.ascontiguousarray(np.stack([inp["lru_ba"][l][0, sl], inp["lru_bi"][l][0, sl],
                                                   inp["lru_ba"][l][1, sl], inp["lru_bi"][l][1, sl]], axis=1))
        m["l_lam"] = np.ascontiguousarray(inp["lru_lambda"][l][:, sl].T)
        ims.append(m)
    return ims


def post_B(results):
    yf = np.concatenate([results[2 * h]["r_y"] for h in range(4)], axis=1)
    yb = np.concatenate([results[2 * h + 1]["r_y"][::-1] for h in range(4)], axis=1)
    na = np.concatenate([results[c]["n_o"] for c in range(8)], axis=1)
    hf = np.concatenate([results[c]["l_h"][0] for c in range(8)], axis=0)
    hb = np.concatenate([results[c]["l_h"][1][:, ::-1] for c in range(8)], axis=0)
    return yf, yb, na, hf, hb


def fm_stats(P, PS, x, xk, KC, T, ones, scr, sk):
    nfeat = KC * 128
    P.op("act", lambda e: e.activation(out=scr[:, 0:KC, 0:T], in_=x[:, 0:KC, 0:T], func=AF.Square), reads=xk, writes=sk)
    b_sum, k_sum = PS.next()
    b_sq, k_sq = PS.next()

    def mm_sum(e):
        for k in range(KC):
            ins = e.matmul(b_sum[:, 0:T], lhsT=ones[:, :], rhs=x[:, k, 0:T], start=(k == 0), stop=(k == KC - 1))
        return ins

    def mm_sq(e):
        for k in range(KC):
            ins = e.matmul(b_sq[:, 0:T], lhsT=ones[:, :], rhs=scr[:, k, 0:T], start=(k == 0), stop=(k == KC - 1))
        return ins
    P.op("pe", mm_sum, reads=xk + ["ones"], writes=[k_sum])
    P.op("pe", mm_sq, reads=sk + ["ones"], writes=[k_sq])
    if not hasattr(P, "_ln_tmp"):
        P._ln_tmp = (P.sb([128, 512], F32, "ln_mean"), P.sb([128, 512], F32, "ln_rstd"))
    mean, rstd = P._ln_tmp
    mk, rk = "ln_mean", "ln_rstd"
    P.op("act", lambda e: e.mul(out=mean[:, 0:T], in_=b_sum[:, 0:T], mul=1.0 / nfeat), reads=[k_sum], writes=[mk])
    P.op("dve", lambda e: e.tensor_tensor(out=rstd[:, 0:T], in0=mean[:, 0:T], in1=mean[:, 0:T], op=ALU.mult), reads=[mk], writes=[rk])
    P.op("dve", lambda e: e.scalar_tensor_tensor(out=rstd[:, 0:T], in0=b_sq[:, 0:T], scalar=1.0 / nfeat, in1=rstd[:, 0:T],
                                                 op0=ALU.mult, op1=ALU.subtract), reads=[k_sq, rk], writes=[rk])
    P.op("dve", lambda e: e.tensor_scalar(out=rstd[:, 0:T], in0=rstd[:, 0:T], scalar1=EPS, scalar2=None, op0=ALU.add),
         reads=[rk], writes=[rk])
    P.op("act", lambda e: e.activation(out=rstd[:, 0:T], in_=rstd[:, 0:T], func=AF.Sqrt), reads=[rk], writes=[rk])
    P.op("dve", lambda e: e.reciprocal(out=rstd[:, 0:T], in_=rstd[:, 0:T]), reads=[rk], writes=[rk])
    return mean, rstd, mk, rk


def build_C(with_next_proj):
    T = 512
    nc = bass.Bass("TRN2", target_bir_lowering=False)
    di = lambda n, s: nc.dram_tensor(n, s, F32, kind="ExternalInput").ap()
    do = lambda n, s: nc.dram_tensor(n, s, F32, kind="ExternalOutput").ap()
    c_yf, c_yb, c_g = di("c_yf", [1024, TPC]), di("c_yb", [1024, TPC]), di("c_g", [1024, TPC])
    c_na, c_hf, c_hb, c_ly = di("c_na", [1024, TPC]), di("c_hf", [1024, TPC]), di("c_hb", [1024, TPC]), di("c_ly", [1024, TPC])
    c_gp, c_gb, c_h = di("c_gp", [6144, TPC]), di("c_gb", [128, 48]), di("c_h", [D, TPC])
    w_br, w_out = di("w_br", [3072, D]), di("w_out", [D, D])
    ln1g, ln1b, ln2g, ln2b = di("ln1g", [128, 16]), di("ln1b", [128, 16]), di("ln2g", [128, 16]), di("ln2b", [128, 16])
    w_f1, w_f2 = di("w_f1", [D, 2 * DFF]), di("w_f2", [DFF, D])
    h2T = do("h2T", [D, TPC])
    h2s = nc.dram_tensor("h2s", [D, TPC], F32).ap()
    if with_next_proj:
        w_in = di("w_in", [D, IN_COLS])
        projT = do("projT", [IN_COLS, TPC])
    P = Prog(nc)
    PS = PsumRing(P)
    ones = P.sb([128, 128], F32, "ones")
    P.op("pool", lambda e: e.memset(ones[:, :], 1.0), writes=["ones"])
    g1, b1 = load_consts(P, ln1g, "g1", 16), load_consts(P, ln1b, "b1", 16)
    g2, b2 = load_consts(P, ln2g, "g2", 16), load_consts(P, ln2b, "b2", 16)
    gbt = load_consts(P, c_gb, "gbt", 48)
    A = P.sb([128, 16, T], F32, "arenaA")
    B = P.sb([128, 16, T], F32, "arenaB")
    U = P.sb([128, 44 * T], BF16, "arenaU")
    U3 = U[:, :].rearrange("p (k t) -> p k t", t=T)
    h1_16 = P.sb([128, 16, T], BF16, "h1_16")
    wflat = [P.sb([128, 5632], BF16, f"wf{i}") for i in range(4)]
    wv = lambda kc, wc: [w[:, 0:kc * wc].rearrange("p (k c) -> p k c", c=wc) for w in wflat]
    sc = [P.sb([128, T], F32, f"sc{i}") for i in range(8)]
    sck = [("sc", i) for i in range(8)]
    gpt = [P.sb([128, T], F32, f"gpt{i}") for i in range(2)]
    Ak = [("A", k) for k in range(16)]
    Bk = [("B", k) for k in range(16)]
    Uk = [("U", k) for k in range(44)]
    toks = []
    for half in range(2):
        ts = slice(half * T, (half + 1) * T)
        for hh in range(4):
            rows = slice(hh * 256, (hh + 1) * 256)
            y, yk = A[:, 0:2, :], Ak[0:2]
            y2, y2k = A[:, 2:4, :], Ak[2:4]
            gg, ggk = A[:, 4:6, :], Ak[4:6]
            sq, sqk = A[:, 6:8, :], Ak[6:8]
            P.dma("sp", y, c_yf[rows, ts].rearrange("(k p) t -> p k t", p=128), writes=yk)
            P.dma("act", y2, c_yb[rows, ts].rearrange("(k p) t -> p k t", p=128), writes=y2k)
            P.dma("sp", gg, c_g[rows, ts].rearrange("(k p) t -> p k t", p=128), writes=ggk)
            P.op("dve", lambda e, y=y, y2=y2: e.tensor_tensor(out=y, in0=y, in1=y2, op=ALU.add), reads=yk + y2k, writes=yk)
            mean, rstd, mk, rk = fm_stats(P, PS, A[:, 0:2, :], yk, 2, T, ones, A[:, 6:8, :], sqk)
            P.op("act", lambda e, gg=gg: e.activation(out=gg, in_=gg, func=AF.Silu), reads=ggk, writes=ggk)
            for j in range(2):
                P.op("dve", lambda e, j=j: e.tensor_tensor(out=A[:, j, :], in0=A[:, j, :], in1=mean[:, 0:T], op=ALU.subtract),
                     reads=[Ak[j], mk], writes=[Ak[j]])
                P.op("dve", lambda e, j=j: e.tensor_tensor(out=A[:, j, :], in0=A[:, j, :], in1=rstd[:, 0:T], op=ALU.mult),
                     reads=[Ak[j], rk], writes=[Ak[j]])
                P.op("dve", lambda e, j=j, hh=hh: e.tensor_tensor(out=U3[:, 2 * hh + j, :], in0=A[:, j, :], in1=A[:, 4 + j, :], op=ALU.mult),
                     reads=[Ak[j], Ak[4 + j]], writes=[Uk[2 * hh + j]])
        P.dma("pool", U3[:, 8:16, :], c_na[:, ts].rearrange("(k p) t -> p k t", p=128), writes=Uk[8:16])
        for k in range(8):
            rows = slice(k * 128, (k + 1) * 128)
            hf_, hb_, ly_, t_ = sc[0], sc[1], sc[2], sc[3]
            P.dma("sp", hf_[:, :], c_hf[rows, ts], writes=[sck[0]])
            P.dma("act", hb_[:, :], c_hb[rows, ts], writes=[sck[1]])
            P.dma("sp", ly_[:, :], c_ly[rows, ts], writes=[sck[2]])
            P.op("dve", lambda e: e.tensor_tensor(out=hf_[:, :], in0=hf_[:, :], in1=hb_[:, :], op=ALU.add), reads=[sck[0], sck[1]], writes=[sck[0]])
            P.op("dve", lambda e: e.tensor_tensor(out=t_[:, :], in0=ly_[:, :], in1=ly_[:, :], op=ALU.mult), reads=[sck[2]], writes=[sck[3]])
            P.op("dve", lambda e: e.tensor_scalar(out=t_[:, :], in0=t_[:, :], scalar1=0.044715, scalar2=1.0, op0=ALU.mult, op1=ALU.add),
                 reads=[sck[3]], writes=[sck[3]])
            P.op("dve", lambda e: e.tensor_tensor(out=t_[:, :], in0=t_[:, :], in1=ly_[:, :], op=ALU.mult), reads=[sck[3], sck[2]], writes=[sck[3]])
            P.op("act", lambda e: e.activation(out=t_[:, :], in_=t_[:, :], func=AF.Sigmoid, scale=1.5957691216057308), reads=[sck[3]], writes=[sck[3]])
            P.op("dve", lambda e: e.tensor_tensor(out=t_[:, :], in0=t_[:, :], in1=ly_[:, :], op=ALU.mult), reads=[sck[3], sck[2]], writes=[sck[3]])
            P.op("dve", lambda e, k=k: e.tensor_tensor(out=U3[:, 16 + k, :], in0=t_[:, :], in1=hf_[:, :], op=ALU.mult),
                 reads=[sck[3], sck[0]], writes=[Uk[16 + k]])
        for b in range(3):
            def evac(ci, hf, bank, bk, b=b):
                gp_, gpk = gpt[ci % 2], ("gpt", ci % 2)
                rows = slice(b * 2048 + ci * 128, b * 2048 + (ci + 1) * 128)
                P.dma("sp", gp_[:, :], c_gp[rows, ts], writes=[gpk])
                P.op("act", lambda e: e.activation(out=gp_[:, :], in_=gp_[:, :], func=AF.Sigmoid, bias=gbt[:, b * 16 + ci:b * 16 + ci + 1]),
                     reads=[gpk, "gbt"], writes=[gpk])
                if b == 0:
                    P.op("dve", lambda e: e.tensor_tensor(out=A[:, ci, :], in0=bank[:, 0:T], in1=gp_[:, :], op=ALU.mult),
                         reads=[bk, gpk], writes=[Ak[ci]])
                else:
                    P.op("dve", lambda e: e.tensor_tensor(out=gp_[:, :], in0=bank[:, 0:T], in1=gp_[:, :], op=ALU.mult),
                         reads=[bk, gpk], writes=[gpk])
                    if b == 1:
                        P.op("dve", lambda e: e.tensor_tensor(out=A[:, ci, :], in0=A[:, ci, :], in1=gp_[:, :], op=ALU.add),
                             reads=[Ak[ci], gpk], writes=[Ak[ci]])
                    else:
                        P.op("dve", lambda e: e.tensor_tensor(out=U3[:, 24 + ci, :], in0=A[:, ci, :], in1=gp_[:, :], op=ALU.add),
                             reads=[Ak[ci], gpk], writes=[Uk[24 + ci]])
            stream_matmul_fm(P, PS, w_br[b * 1024:(b + 1) * 1024, :], 0, D, 8, U3[:, 8 * b:8 * b + 8, :], Uk[8 * b:8 * b + 8], T, evac,
                             wv(8, 512), "W", WC=512)
        P.dma("sp", B[:, :, :], c_h[:, ts].rearrange("(k p) t -> p k t", p=128), writes=Bk)

        def evac_o(ci, hf, bank, bk):
            P.op("dve", lambda e: e.scalar_tensor_tensor(out=B[:, ci, :], in0=B[:, ci, :], scalar=ALPHA, in1=bank[:, 0:T],
                                                         op0=ALU.mult, op1=ALU.add), reads=[bk, Bk[ci]], writes=[Bk[ci]])
        stream_matmul_fm(P, PS, w_out, 0, D, 16, U3[:, 24:40, :], Uk[24:40], T, evac_o, wv(16, 256), "W", WC=256)
        layernorm_fm(P, PS, B, ("B",), T, g1, b1, ["g1", "b1"], ones, B, ("B",), h1_16, ("h1_16",), A, ("A",))
        h1k = [("h1_16", k) for k in range(16)]

        def evac_f(ci, hf, bank, bk):
            if ci < 44:
                P.op("act", lambda e: e.activation(out=U3[:, ci, :], in_=bank[:, 0:T], func=AF.Silu), reads=[bk], writes=[Uk[ci]])
            else:
                f = ci - 44
                P.op("dve", lambda e: e.tensor_tensor(out=U3[:, f, :], in0=bank[:, 0:T], in1=U3[:, f, :], op=ALU.mult),
                     reads=[bk, Uk[f]], writes=[Uk[f]])
        stream_matmul_fm(P, PS, w_f1, 0, 2 * DFF, 16, h1_16, h1k, T, evac_f, wv(16, 256), "W", WC=256)

        def evac_2(ci, hf, bank, bk):
            P.op("dve", lambda e: e.scalar_tensor_tensor(out=B[:, ci, :], in0=B[:, ci, :], scalar=ALPHA, in1=bank[:, 0:T],
                                                         op0=ALU.mult, op1=ALU.add), reads=[bk, Bk[ci]], writes=[Bk[ci]])
        stream_matmul_fm(P, PS, w_f2, 0, D, 44, U3, Uk, T, evac_2, wv(44, 128), "W", WC=128)
        layernorm_fm(P, PS, B, ("B",), T, g2, b2, ["g2", "b2"], ones, B, ("B",), h1_16, ("h1_16",), A, ("A",))
        toks.append(P.dma("sp", h2T[:, ts].rearrange("(k p) t -> p k t", p=128), B[:, :, :], reads=Bk))
        if with_next_proj:
            P.dma("act", h2s[:, ts].rearrange("(k p) t -> p k t", p=128), B[:, :, :], reads=Bk, writes=[("h2s", half)])
    if with_next_proj:
        h16 = U[:, 0:16 * TPC].rearrange("p (k t) -> p k t", t=TPC)
        for k in range(16):
            P.dma("pool", h16[:, k, :], h2s[k * 128:(k + 1) * 128, :], reads=[("h2s", 0), ("h2s", 1)], writes=Uk[2 * k:2 * k + 2])
        obufs = [P.sb([128, 512], F32, f"ob{i}") for i in range(2)]
        toks += emit_inproj(P, PS, h16, Uk[0:32], w_in, projT, wv(16, 256), obufs, wname="W")
    P.emit(toks)
    return nc


def prep_C(l, inp, projT, hT, yf, yb, na, hf, hb, with_next):
    yfT, ybT, naT = np.ascontiguousarray(yf.T), np.ascontiguousarray(yb.T), np.ascontiguousarray(na.T)
    fm16 = lambda v: np.ascontiguousarray(v.reshape(16, 128).T)
    shared = {
        "c_gb": np.ascontiguousarray(inp["gate_b"][l].reshape(48, 128).T),
        "w_br": np.ascontiguousarray(inp["w_branch"][l].reshape(3072, D)), "w_out": inp["w_out"][l],
        "ln1g": fm16(inp["ln1_g"][l]), "ln1b": fm16(inp["ln1_b"][l]), "ln2g": fm16(inp["ln2_g"][l]), "ln2b": fm16(inp["ln2_b"][l]),
        "w_f1": inp["w_ffn_in"][l], "w_f2": inp["w_ffn_out"][l],
    }
    if with_next:
        shared["w_in"] = inp["w_in"][l + 1]
    ims = []
    for c in range(NCORES):
        ts = slice(c * TPC, (c + 1) * TPC)
        cc = lambda a: np.ascontiguousarray(a[:, ts])
        m = dict(shared)
        m.update({"c_yf": cc(yfT), "c_yb": cc(ybT), "c_g": cc(projT[3072:4096]), "c_na": cc(naT), "c_hf": cc(hf), "c_hb": cc(hb),
                  "c_ly": cc(projT[8192:9216]), "c_gp": cc(projT[9216:15360]), "c_h": cc(hT)})
        ims.append(m)
    return ims


def _run(nc, ims):
    return run_bass_kernel_spmd(nc, ims, core_ids=list(range(NCORES))).results


def kernel(**inputs):
    inp = {k: np.asarray(v) for k, v in inputs.items()}
    x = inp["x"][0]
    fm16 = lambda v: np.ascontiguousarray(v.reshape(16, 128).T)
    ims = [{"xT": np.ascontiguousarray(x[c * TPC:(c + 1) * TPC].T), "lng": fm16(inp["ln_in_g"]), "lnb": fm16(inp["ln_in_b"]),
            "w_in": inp["w_in"][0]} for c in range(NCORES)]
    res = _run(build_A0(), ims)
    hT = np.concatenate([r["hT"] for r in res], axis=1)
    projT = np.concatenate([r["projT"] for r in res], axis=1)
    for l in range(DEPTH):
        resB = _run(build_B(), prep_B(projT, l, inp))
        yf, yb, na, hf, hb = post_B(resB)
        del resB
        nxt = l + 1 < DEPTH
        res = _run(build_C(nxt), prep_C(l, inp, projT, hT, yf, yb, na, hf, hb, nxt))
        del yf, yb, na, hf, hb
        hT = np.concatenate([r["h2T"] for r in res], axis=1)
        if nxt:
            projT = np.concatenate([r["projT"] for r in res], axis=1)
        del res
    return np.ascontiguousarray(hT.T)[None].astype(np.float32)
```

```python
import contextlib
import numpy as np
import concourse.bass as bass
import concourse.mybir as mybir
from concourse.bass_utils import run_bass_kernel_spmd

F32 = mybir.dt.float32
BF16 = mybir.dt.bfloat16
AF = mybir.ActivationFunctionType
ALU = mybir.AluOpType
AX = mybir.AxisListType

NCORES = 8
D = 2048
SEQ = 8192
TPC = SEQ // NCORES
DEPTH = 4
IN_COLS = 15360
DFF = 5632
ALPHA = (2 * DEPTH) ** 0.25
EPS = 1e-5
GRID_W = 64


class Prog:
    ENGS = ("pe", "act", "dve", "pool", "sp")
    DMA_RING = 16

    def __init__(self, nc):
        self.nc = nc
        self.stack = contextlib.ExitStack()
        self.ops = {e: [] for e in self.ENGS}
        self.cnt = {e: 0 for e in self.ENGS}
        self.dcnt = {e: 0 for e in self.ENGS}
        self.known = {e: {} for e in self.ENGS}
        self.last_w = {}
        self.readers = {}
        self.sem = {e: self.stack.enter_context(nc.semaphore("s_" + e)) for e in self.ENGS}
        self.dsem = {}
        for q in ("sp", "pool", "act"):
            self.dsem[q] = [self.stack.enter_context(nc.semaphore(f"d_{q}{i}")) for i in range(self.DMA_RING)]
        self.out_tokens = []
        self._n = 0

    def sb(self, shape, dtype, name=None):
        self._n += 1
        return self.stack.enter_context(self.nc.sbuf_tensor("sb_" + (name or f"t{self._n}"), list(shape), dtype))

    def ps(self, name=None):
        self._n += 1
        return self.stack.enter_context(self.nc.psum_tensor(name or f"p{self._n}", [128, 512], F32))

    def _tok_sem(self, tok):
        if tok[0] == "e":
            return ("e", tok[1]), tok[2]
        return ("d", tok[1], tok[2] % self.DMA_RING), 16 * (tok[2] // self.DMA_RING + 1)

    def _waits(self, eng, reads, writes):
        toks = set()
        for r in reads:
            if r in self.last_w:
                toks.add(self.last_w[r])
        for w in writes:
            if w in self.last_w:
                toks.add(self.last_w[w])
            for t in self.readers.get(w, ()):
                toks.add(t)
        need = {}
        for t in toks:
            key, val = self._tok_sem(t)
            if self.known[eng].get(key, 0) >= val:
                continue
            need[key] = max(need.get(key, 0), val)
        for key, val in need.items():
            self.known[eng][key] = val
        return list(need.items())

    def _commit(self, tok, reads, writes):
        for r in reads:
            self.readers.setdefault(r, []).append(tok)
        for w in writes:
            self.last_w[w] = tok
            self.readers[w] = []

    def op(self, eng, fn, reads=(), writes=()):
        waits = self._waits(eng, reads, writes)
        self.cnt[eng] += 1
        tok = ("e", eng, self.cnt[eng])
        self.ops[eng].append(("op", fn, waits))
        self._commit(tok, reads, writes)
        return tok

    def dma(self, q, out, in_, reads=(), writes=(), **kw):
        idx = self.dcnt[q]
        waits = self._waits(q, reads, writes)
        if idx >= self.DMA_RING:
            key, val = self._tok_sem(("d", q, idx - self.DMA_RING))
            if self.known[q].get(key, 0) < val:
                self.known[q][key] = val
                waits.append((key, val))
        self.dcnt[q] += 1
        tok = ("d", q, idx)
        self.ops[q].append(("dma", (out, in_, kw, idx), waits))
        self._commit(tok, reads, writes)
        return tok

    def _semh(self, key):
        return self.sem[key[1]] if key[0] == "e" else self.dsem[key[1]][key[2]]

    def emit(self, final_tokens):
        nc = self.nc
        emap = {"pe": "tensor", "act": "scalar", "dve": "vector", "pool": "gpsimd", "sp": "sync"}
        fin = {}
        for t in final_tokens:
            key, val = self._tok_sem(t)
            fin[key] = max(fin.get(key, 0), val)
        with nc.Block() as block:
            for e in self.ENGS:
                def body(eng, e=e):
                    for kind, payload, waits in self.ops[e]:
                        for key, val in waits:
                            eng.wait_ge(self._semh(key), val)
                        if kind == "op":
                            ins = payload(eng)
                            ins.then_inc(self.sem[e], 1)
                        else:
                            out, in_, kw, idx = payload
                            eng.dma_start(out=out, in_=in_, **kw).then_inc(self.dsem[e][idx % self.DMA_RING], 16)
                    if e == "sp":
                        for key, val in fin.items():
                            eng.wait_ge(self._semh(key), val)
                getattr(block, emap[e])(body)
        self.stack.close()


class PsumRing:
    def __init__(self, P, n=8):
        self.P = P
        self.banks = [P.ps(f"bank{i}") for i in range(n)]
        self.i = 0

    def next(self):
        b = self.banks[self.i % len(self.banks)]
        k = ("psum", self.i % len(self.banks))
        self.i += 1
        return b, k

    def sub(self, idxs):
        return _SubRing(self, list(idxs))


class _SubRing:
    def __init__(self, parent, idxs):
        self.parent, self.idxs, self.i = parent, idxs, 0

    def next(self):
        j = self.idxs[self.i % len(self.idxs)]
        self.i += 1
        return self.parent.banks[j], ("psum", j)


def load_consts(P, nc_in, name, shape_free, q="sp"):
    t = P.sb([128, shape_free], F32, name)
    P.dma(q, t[:, :], nc_in[:, :], writes=[name])
    return t


def layernorm_fm(P, PS, x, xkey, T, gam, bet, gkeys, ones, out32, out32key, out16, out16key, scr, scrkey, KC=16):
    nfeat = KC * 128
    xk = [xkey + (k,) for k in range(KC)]
    sk = [scrkey + (k,) for k in range(KC)]
    P.op("act", lambda e: e.activation(out=scr[:, 0:KC, 0:T], in_=x[:, 0:KC, 0:T], func=AF.Square),
         reads=xk, writes=sk)
    b_sum, k_sum = PS.next()
    b_sq, k_sq = PS.next()

    def mm_sum(e):
        for k in range(KC):
            ins = e.matmul(b_sum[:, 0:T], lhsT=ones[:, :], rhs=x[:, k, 0:T], start=(k == 0), stop=(k == KC - 1))
        return ins

    def mm_sq(e):
        for k in range(KC):
            ins = e.matmul(b_sq[:, 0:T], lhsT=ones[:, :], rhs=scr[:, k, 0:T], start=(k == 0), stop=(k == KC - 1))
        return ins
    P.op("pe", mm_sum, reads=xk + ["ones"], writes=[k_sum])
    P.op("pe", mm_sq, reads=sk + ["ones"], writes=[k_sq])
    if not hasattr(P, "_ln_tmp"):
        P._ln_tmp = (P.sb([128, 512], F32, "ln_mean"), P.sb([128, 512], F32, "ln_rstd"))
    mean, rstd = P._ln_tmp
    mk, rk = "ln_mean", "ln_rstd"
    P.op("act", lambda e: e.mul(out=mean[:, 0:T], in_=b_sum[:, 0:T], mul=1.0 / nfeat), reads=[k_sum], writes=[mk])
    P.op("dve", lambda e: e.tensor_tensor(out=rstd[:, 0:T], in0=mean[:, 0:T], in1=mean[:, 0:T], op=ALU.mult),
         reads=[mk], writes=[rk])
    P.op("dve", lambda e: e.scalar_tensor_tensor(out=rstd[:, 0:T], in0=b_sq[:, 0:T], scalar=1.0 / nfeat, in1=rstd[:, 0:T],
                                                 op0=ALU.mult, op1=ALU.subtract), reads=[k_sq, rk], writes=[rk])
    P.op("dve", lambda e: e.tensor_scalar(out=rstd[:, 0:T], in0=rstd[:, 0:T], scalar1=EPS, scalar2=None,
                                          op0=ALU.add), reads=[rk], writes=[rk])
    P.op("act", lambda e: e.activation(out=rstd[:, 0:T], in_=rstd[:, 0:T], func=AF.Sqrt), reads=[rk], writes=[rk])
    P.op("dve", lambda e: e.reciprocal(out=rstd[:, 0:T], in_=rstd[:, 0:T]), reads=[rk], writes=[rk])
    for k in range(KC):
        P.op("dve", lambda e, k=k: e.tensor_tensor(out=scr[:, k, 0:T], in0=x[:, k, 0:T], in1=mean[:, 0:T], op=ALU.subtract),
             reads=[xk[k], mk], writes=[sk[k]])
        P.op("dve", lambda e, k=k: e.tensor_tensor(out=scr[:, k, 0:T], in0=scr[:, k, 0:T], in1=rstd[:, 0:T], op=ALU.mult),
             reads=[sk[k], rk], writes=[sk[k]])
        P.op("act", lambda e, k=k: e.activation(out=out32[:, k, 0:T], in_=scr[:, k, 0:T], func=AF.Identity,
                                                scale=gam[:, k:k + 1], bias=bet[:, k:k + 1]),
             reads=[sk[k]] + list(gkeys), writes=[out32key + (k,)])
        P.op("act", lambda e, k=k: e.activation(out=out16[:, k, 0:T], in_=scr[:, k, 0:T], func=AF.Identity,
                                                scale=gam[:, k:k + 1], bias=bet[:, k:k + 1]),
             reads=[sk[k]] + list(gkeys), writes=[out16key + (k,)])


def stream_matmul_fm(P, PS, w_dram, col0, ncols, KC, rhs16, rhs_keys, T, evac, wbufs, wname, WC=256, TH=512, KT=1):
    assert ncols % 128 == 0 and KC % KT == 0
    KS = KC // KT
    c = 0
    gi = getattr(P, "_wcount", 0)
    while c < ncols:
        w = min(WC, ncols - c)
        tiles = []
        for kt in range(KT):
            wb = wbufs[gi % len(wbufs)]
            wk = (wname, gi % len(wbufs))
            gi += 1
            src = w_dram[kt * KS * 128:(kt + 1) * KS * 128, col0 + c: col0 + c + w].rearrange("(k p) c -> p k c", p=128)
            P.dma("pool", wb[:, 0:KS, 0:w], src, writes=[wk])
            tiles.append((wb, wk))
        for cc in range(w // 128):
            for half in range(T // TH):
                bank, bk = PS.next()

                def mm(e, cc=cc, half=half, bank=bank, tiles=tiles):
                    for kt, (wb, _) in enumerate(tiles):
                        for k in range(KS):
                            kk = kt * KS + k
                            ins = e.matmul(bank[:, 0:TH], lhsT=wb[:, k, cc * 128:(cc + 1) * 128],
                                           rhs=rhs16[:, kk, half * TH:(half + 1) * TH], start=(kk == 0), stop=(kk == KC - 1))
                    return ins
                P.op("pe", mm, reads=[wk for _, wk in tiles] + list(rhs_keys), writes=[bk])
                evac((c // 128) + cc, half, bank, bk)
        c += w
    P._wcount = gi


def emit_inproj(P, PS, h16, h16keys, w_in, projT, wbufs, obufs, wname="w_in"):
    state = {"i": 0}
    toks = []

    def evac(ci, half, bank, bk):
        i = state["i"]
        state["i"] += 1
        ob = obufs[i % len(obufs)]
        ok = ("projo", i % len(obufs))
        eng = "act" if i % 2 == 0 else "dve"
        if eng == "act":
            P.op("act", lambda e: e.copy(out=ob[:, :], in_=bank[:, 0:512]), reads=[bk], writes=[ok])
        else:
            P.op("dve", lambda e: e.tensor_copy(out=ob[:, :], in_=bank[:, 0:512]), reads=[bk], writes=[ok])
        toks.append(P.dma("sp", projT[ci * 128:(ci + 1) * 128, half * 512:(half + 1) * 512], ob[:, :], reads=[ok]))
    stream_matmul_fm(P, PS, w_in, 0, IN_COLS, 16, h16, h16keys, TPC, evac, wbufs, wname)
    return toks


def build_A0():
    nc = bass.Bass("TRN2", target_bir_lowering=False)
    xT = nc.dram_tensor("xT", [D, TPC], F32, kind="ExternalInput").ap()
    lng = nc.dram_tensor("lng", [128, 16], F32, kind="ExternalInput").ap()
    lnb = nc.dram_tensor("lnb", [128, 16], F32, kind="ExternalInput").ap()
    w_in = nc.dram_tensor("w_in", [D, IN_COLS], F32, kind="ExternalInput").ap()
    projT = nc.dram_tensor("projT", [IN_COLS, TPC], F32, kind="ExternalOutput").ap()
    hT = nc.dram_tensor("hT", [D, TPC], F32, kind="ExternalOutput").ap()
    P = Prog(nc)
    PS = PsumRing(P)
    ones = P.sb([128, 128], F32, "ones")
    P.op("pool", lambda e: e.memset(ones[:, :], 1.0), writes=["ones"])
    gam = load_consts(P, lng, "gam", 16)
    bet = load_consts(P, lnb, "bet", 16)
    h16 = P.sb([128, 16, TPC], BF16, "h16")
    x32 = P.sb([128, 16, 512], F32, "x32")
    scr = P.sb([128, 16, 512], F32, "scr")
    toks = []
    x32k = [("x32", k) for k in range(16)]
    for half in range(2):
        P.dma("sp", x32[:, :, :], xT[:, half * 512:(half + 1) * 512].rearrange("(k p) t -> p k t", p=128), writes=x32k)
        h16v = h16[:, :, half * 512:(half + 1) * 512]
        layernorm_fm(P, PS, x32, ("x32",), 512, gam, bet, ["gam", "bet"], ones, x32, ("x32",), h16v, ("h16", half),
                     scr, ("scr",))
        toks.append(P.dma("sp", hT[:, half * 512:(half + 1) * 512].rearrange("(k p) t -> p k t", p=128), x32[:, :, :],
                          reads=x32k))
    wbufs = [P.sb([128, 16, 256], BF16, f"wb{i}") for i in range(2)]
    obufs = [P.sb([128, 512], F32, f"ob{i}") for i in range(4)]
    h16keys = [("h16", hf, k) for hf in range(2) for k in range(16)]
    toks += emit_inproj(P, PS, h16, h16keys, w_in, projT, wbufs, obufs)
    P.emit(toks)
    return nc


NEG = -30000.0
RB = 512
NCH = RB // 128
LB = 1024


def emit_retention(P, PS, nc, io):
    qT, kT, ktm, v, cosT, sinT, costm, sintm, logit, diffT, keepT, idx1, kidx, y = io
    cst = P.sb([128, 8], F32, "r_cst")
    dT = P.sb([128, 128], F32, "r_diffT")
    kpT = P.sb([128, 128], F32, "r_keepT")
    i1 = P.sb([128, 128], F32, "r_idx1")
    maskT = P.sb([128, 128], F32, "r_maskT")
    qdec = P.sb([128, RB], F32, "r_qdec")
    P.dma("sp", cst[:, 0:1], logit[:, :], writes=["r_c0"])
    P.dma("sp", cst[:, 6:7], kidx[:, :], writes=["r_c6"])
    P.dma("sp", dT[:, :], diffT[:, :], writes=["r_dT"])
    P.dma("sp", kpT[:, :], keepT[:, :], writes=["r_kpT"])
    P.dma("sp", i1[:, :], idx1[:, :], writes=["r_i1"])
    P.op("act", lambda e: e.activation(out=cst[:, 1:2], in_=cst[:, 0:1], func=AF.Exp, scale=-1.0), reads=["r_c0"], writes=["r_c1"])
    P.op("act", lambda e: e.activation(out=cst[:, 2:3], in_=cst[:, 1:2], func=AF.Ln, bias=1.0), reads=["r_c1"], writes=["r_c2"])
    P.op("act", lambda e: e.mul(out=cst[:, 3:4], in_=cst[:, 2:3], mul=-1.0), reads=["r_c2"], writes=["r_logg"])
    P.op("act", lambda e: e.activation(out=maskT[:, :], in_=dT[:, :], func=AF.Exp, scale=cst[:, 3:4]),
         reads=["r_dT", "r_logg"], writes=["r_maskT"])
    P.op("dve", lambda e: e.tensor_tensor(out=maskT[:, :], in0=maskT[:, :], in1=kpT[:, :], op=ALU.mult),
         reads=["r_maskT", "r_kpT"], writes=["r_maskT"])
    for n in range(RB // 128):
        P.op("act", lambda e, n=n: e.activation(out=qdec[:, n * 128:(n + 1) * 128], in_=i1[:, :], func=AF.Exp, scale=cst[:, 3:4]),
             reads=["r_i1", "r_logg"], writes=[("r_qdec", n)])
    qdk = [("r_qdec", n) for n in range(RB // 128)]
    P.op("act", lambda e: e.activation(out=cst[:, 4:5], in_=cst[:, 6:7], func=AF.Exp, scale=cst[:, 3:4]),
         reads=["r_c6", "r_logg"], writes=["r_kdec"])
    P.op("act", lambda e: e.activation(out=cst[:, 5:6], in_=cst[:, 3:4], func=AF.Exp, scale=128.0),
         reads=["r_logg"], writes=["r_cdec"])
    S32 = P.sb([128, 512], F32, "r_S32")
    S16 = [P.sb([128, 512], BF16, f"r_S16_{i}") for i in range(2)]
    P.op("pool", lambda e: e.memset(S32[:, :], 0.0), writes=["r_S32"])
    for i in range(2):
        P.op("pool", lambda e, i=i: e.memset(S16[i][:, :], 0.0), writes=[("r_S16", i)])
    qr = P.sb([128, 2, RB], F32, "r_qr")
    kr = P.sb([128, 2, RB], F32, "r_kr")
    ktr = P.sb([128, NCH, 256], F32, "r_ktr")
    vr = P.sb([128, NCH, 256], F32, "r_vr")
    cs = P.sb([128, RB], F32, "r_cs")
    sn = P.sb([128, RB], F32, "r_sn")
    cst_ = P.sb([128, NCH, 128], F32, "r_cstm")
    snt = P.sb([128, NCH, 128], F32, "r_sntm")
    ta = P.sb([128, RB], F32, "r_ta")
    tb = P.sb([128, RB], F32, "r_tb")
    tc_ = P.sb([128, RB], F32, "r_tc")
    q16 = P.sb([128, 2, RB], BF16, "r_q16")
    qd16 = P.sb([128, 2, RB], BF16, "r_qd16")
    k16 = P.sb([128, 2, RB], BF16, "r_k16")
    kt16 = P.sb([128, NCH, 256], BF16, "r_kt16")
    v16 = P.sb([128, NCH, 256], BF16, "r_v16")
    vd16 = P.sb([128, NCH, 256], BF16, "r_vd16")
    sm16 = [P.sb([128, 128], BF16, f"r_sm16_{i}") for i in range(2)]
    yb = [P.sb([128, NCH, 256], F32, f"r_yb{i}") for i in range(2)]
    toks = []
    KS = 256 ** -0.5
    for b in range(SEQ // RB):
        t0 = b * RB
        P.dma("sp", qr[:, :, :], qT[:, t0:t0 + RB].rearrange("(j p) t -> p j t", p=128), writes=["r_qr"])
        P.dma("sp", kr[:, :, :], kT[:, t0:t0 + RB].rearrange("(j p) t -> p j t", p=128), writes=["r_kr"])
        P.dma("act", ktr[:, :, :], ktm[t0:t0 + RB, :].rearrange("(n p) d -> p n d", p=128), writes=["r_ktr"])
        P.dma("act", vr[:, :, :], v[t0:t0 + RB, :].rearrange("(n p) d -> p n d", p=128), writes=["r_vr"])
        P.dma("sp", cs[:, :], cosT[:, t0:t0 + RB], writes=["r_cs"])
        P.dma("sp", sn[:, :], sinT[:, t0:t0 + RB], writes=["r_sn"])
        P.dma("act", cst_[:, :, :], costm[t0:t0 + RB, :].rearrange("(n p) d -> p n d", p=128), writes=["r_cstm"])
        P.dma("act", snt[:, :, :], sintm[t0:t0 + RB, :].rearrange("(n p) d -> p n d", p=128), writes=["r_sntm"])

        def rot_fm(src, skey, outs, okeys, scale):
            t1, t2 = src[:, 0, :], src[:, 1, :]
            P.op("dve", lambda e: e.tensor_tensor(out=ta[:, :], in0=t1, in1=cs[:, :], op=ALU.mult), reads=[skey, "r_cs"], writes=["r_ta"])
            P.op("pool", lambda e: e.tensor_tensor(out=tb[:, :], in0=t2, in1=sn[:, :], op=ALU.mult), reads=[skey, "r_sn"], writes=["r_tb"])
            P.op("dve", lambda e: e.tensor_tensor(out=ta[:, :], in0=ta[:, :], in1=tb[:, :], op=ALU.subtract), reads=["r_ta", "r_tb"], writes=["r_ta"])
            P.op("pool", lambda e: e.tensor_tensor(out=tb[:, :], in0=t1, in1=sn[:, :], op=ALU.mult), reads=[skey, "r_sn", "r_ta"], writes=["r_tb"])
            P.op("dve", lambda e: e.tensor_tensor(out=tc_[:, :], in0=t2, in1=cs[:, :], op=ALU.mult), reads=[skey, "r_cs"], writes=["r_tc"])
            P.op("dve", lambda e: e.tensor_tensor(out=tb[:, :], in0=tb[:, :], in1=tc_[:, :], op=ALU.add), reads=["r_tb", "r_tc"], writes=["r_tb"])
            for (o, ok, extra) in outs:
                if extra is None:
                    P.op("act", lambda e, o=o: e.mul(out=o[:, 0, :], in_=ta[:, :], mul=scale), reads=["r_ta"], writes=[ok + "0"])
                    P.op("act", lambda e, o=o: e.mul(out=o[:, 1, :], in_=tb[:, :], mul=scale), reads=["r_tb"], writes=[ok + "1"])
                else:
                    P.op("dve", lambda e, o=o: e.tensor_tensor(out=o[:, 0, :], in0=ta[:, :], in1=qdec[:, :], op=ALU.mult), reads=["r_ta"] + qdk, writes=[ok + "0"])
                    P.op("pool", lambda e, o=o: e.tensor_tensor(out=o[:, 1, :], in0=tb[:, :], in1=qdec[:, :], op=ALU.mult), reads=["r_tb"] + qdk, writes=[ok + "1"])
        rot_fm(qr, "r_qr", [(q16, "r_q16", None), (qd16, "r_qd16", True)], None, 1.0)
        rot_fm(kr, "r_kr", [(k16, "r_k16", None)], None, KS)
        ta3 = ta[:, :].rearrange("p (n d) -> p n d", d=128)
        tb3 = tb[:, :].rearrange("p (n d) -> p n d", d=128)
        tc3 = tc_[:, :].rearrange("p (n d) -> p n d", d=128)
        t1, t2 = ktr[:, :, 0:128], ktr[:, :, 128:256]
        P.op("dve", lambda e: e.tensor_tensor(out=ta3, in0=t1, in1=cst_[:, :, :], op=ALU.mult), reads=["r_ktr", "r_cstm"], writes=["r_ta"])
        P.op("pool", lambda e: e.tensor_tensor(out=tb3, in0=t2, in1=snt[:, :, :], op=ALU.mult), reads=["r_ktr", "r_sntm"], writes=["r_tb"])
        P.op("dve", lambda e: e.tensor_tensor(out=ta3, in0=ta3, in1=tb3, op=ALU.subtract), reads=["r_ta", "r_tb"], writes=["r_ta"])
        P.op("act", lambda e: e.mul(out=kt16[:, :, 0:128], in_=ta3, mul=KS), reads=["r_ta"], writes=["r_kt16a"])
        P.op("pool", lambda e: e.tensor_tensor(out=tb3, in0=t1, in1=snt[:, :, :], op=ALU.mult), reads=["r_ktr", "r_sntm", "r_ta"], writes=["r_tb"])
        P.op("dve", lambda e: e.tensor_tensor(out=tc3, in0=t2, in1=cst_[:, :, :], op=ALU.mult), reads=["r_ktr", "r_cstm"], writes=["r_tc"])
        P.op("dve", lambda e: e.tensor_tensor(out=tb3, in0=tb3, in1=tc3, op=ALU.add), reads=["r_tb", "r_tc"], writes=["r_tb"])
        P.op("act", lambda e: e.mul(out=kt16[:, :, 128:256], in_=tb3, mul=KS), reads=["r_tb"], writes=["r_kt16b"])
        P.op("pool", lambda e: e.tensor_copy(out=v16[:, :, :], in_=vr[:, :, :]), reads=["r_vr"], writes=["r_v16"])
        P.op("act", lambda e: e.activation(out=vd16[:, :, :], in_=vr[:, :, :], func=AF.Copy, scale=cst[:, 4:5]),
             reads=["r_vr", "r_kdec"], writes=["r_vd16"])
        ybuf = yb[b % 2]
        ybk = ("r_yb", b % 2)

        def stage1(n):
            cs_ = slice(n * 128, (n + 1) * 128)
            bs, bsk = PS.next()

            def mm_s(e, cs_=cs_, bs=bs):
                for j in range(2):
                    ins = e.matmul(bs[:, 0:128], lhsT=k16[:, j, cs_], rhs=q16[:, j, cs_], start=(j == 0), stop=(j == 1))
                return ins
            P.op("pe", mm_s, reads=["r_k160", "r_k161", "r_q160", "r_q161"], writes=[bsk])
            sm = sm16[n % 2]
            smk = ("r_sm", n % 2)
            P.op("dve", lambda e, sm=sm, bs=bs: e.tensor_tensor(out=sm[:, :], in0=bs[:, 0:128], in1=maskT[:, :], op=ALU.mult),
                 reads=[bsk, "r_maskT"], writes=[smk])
            bkv, bkvk = PS.next()

            def mm_kv(e, bkv=bkv, n=n):
                for j in range(2):
                    ins = e.matmul(bkv[:, j * 256:(j + 1) * 256], lhsT=kt16[:, n, j * 128:(j + 1) * 128], rhs=vd16[:, n, :],
                                   start=True, stop=True)
                return ins
            P.op("pe", mm_kv, reads=["r_kt16a", "r_kt16b", "r_vd16"], writes=[bkvk])
            return sm, smk, bkv, bkvk

        def stage2(n, st):
            sm, smk, bkv, bkvk = st
            cs_ = slice(n * 128, (n + 1) * 128)
            g = b * NCH + n
            Sin, Sink = S16[g % 2], ("r_S16", g % 2)
            Sout, Soutk = S16[(g + 1) % 2], ("r_S16", (g + 1) % 2)
            P.op("dve", lambda e, bkv=bkv: e.scalar_tensor_tensor(out=S32[:, :], in0=S32[:, :], scalar=cst[:, 5:6], in1=bkv[:, :],
                                                                 op0=ALU.mult, op1=ALU.add),
                 reads=["r_S32", "r_cdec", bkvk], writes=["r_S32"])
            P.op("pool", lambda e, Sout=Sout: e.tensor_copy(out=Sout[:, :], in_=S32[:, :]), reads=["r_S32"], writes=[Soutk])
            by, byk = PS.next()

            def mm_y(e, cs_=cs_, by=by, sm=sm, n=n, Sin=Sin):
                e.matmul(by[:, 0:256], lhsT=sm[:, :], rhs=v16[:, n, :], start=True, stop=False)
                for j in range(2):
                    ins = e.matmul(by[:, 0:256], lhsT=qd16[:, j, cs_], rhs=Sin[:, j * 256:(j + 1) * 256], start=False, stop=(j == 1))
                return ins
            P.op("pe", mm_y, reads=[smk, "r_v16", "r_qd160", "r_qd161", Sink], writes=[byk])
            P.op("act", lambda e, by=by, n=n, ybuf=ybuf: e.copy(out=ybuf[:, n, :], in_=by[:, 0:256]), reads=[byk], writes=[ybk + (n,)])
        st = stage1(0)
        for n in range(NCH):
            nxt = stage1(n + 1) if n + 1 < NCH else None
            stage2(n, st)
            st = nxt
            yield
        toks.append(P.dma("sp", y[t0:t0 + RB, :].rearrange("(n p) e -> p n e", p=128), ybuf[:, :, :],
                          reads=[ybk + (n,) for n in range(NCH)]))
    return toks


def emit_na(P, PS, nc, io):
    qT, kT, v64, biasd, o = io
    q16 = P.sb([128, SEQ], BF16, "n_q16")
    k16 = P.sb([128, SEQ], BF16, "n_k16")
    va = P.sb([64, 128, 132], BF16, "n_va")
    bias = P.sb([64, 8, 512], F32, "n_bias")
    for i in range(4):
        sl = slice(i * 2048, (i + 1) * 2048)
        P.dma("pool", q16[:, sl], qT[:, sl], writes=[("n_q16", i)], max_dma_last_dim=4096)
        P.dma("pool", k16[:, sl], kT[:, sl], writes=[("n_k16", i)], max_dma_last_dim=4096)
    qk_keys = [("n_q16", i) for i in range(4)] + [("n_k16", i) for i in range(4)]
    P.op("pool", lambda e: e.memset(va[:, :, 128:132], 1.0), writes=["n_va1"])
    for i in range(4):
        P.dma("pool", va[:, i * 32:(i + 1) * 32, 0:128], v64[:, i * 32:(i + 1) * 32, :], writes=[("n_va", i)])
    va_keys = ["n_va1"] + [("n_va", i) for i in range(4)]
    P.dma("sp", bias[:, :, :], biasd[:, :, :], writes=["n_bias"])
    st = [P.sb([64, 512], F32, f"n_st{i}") for i in range(2)]
    e16 = [P.sb([64, 512], BF16, f"n_e16_{i}") for i in range(2)]
    rc = [P.sb([64, 1], F32, f"n_rc{i}") for i in range(2)]
    ob = [P.sb([64, 8, 128], F32, f"n_ob{i}") for i in range(2)]
    toks = []
    scale = 128 ** -0.5
    nrows = SEQ // GRID_W
    for r in range(nrows):
        rs = min(max(r - 4, 0), nrows - 8)
        cls = r - rs
        bs, bsk = PS.next()

        def mm_s(e, r=r, rs=rs, bs=bs):
            for i in range(8):
                ks = slice((rs + i) * 64, (rs + i + 1) * 64)
                ins = e.matmul(bs[0:64, i * 64:(i + 1) * 64], lhsT=k16[:, ks], rhs=q16[:, r * 64:(r + 1) * 64], start=True, stop=True)
            return ins
        P.op("pe", mm_s, reads=qk_keys, writes=[bsk])
        s_, sk = st[r % 2], ("n_st", r % 2)
        P.op("dve", lambda e, s_=s_, bs=bs, cls=cls: e.scalar_tensor_tensor(out=s_[:, :], in0=bs[0:64, :], scalar=scale, in1=bias[:, cls, :],
                                                                            op0=ALU.mult, op1=ALU.add),
             reads=[bsk, "n_bias"], writes=[sk])
        e_, ek = e16[r % 2], ("n_e16", r % 2)
        P.op("act", lambda e, s_=s_, e_=e_: e.activation(out=e_[:, :], in_=s_[:, :], func=AF.Exp), reads=[sk], writes=[ek])
        bo, bok = PS.next()

        def mm_o(e, rs=rs, bo=bo, e_=e_):
            for i in range(8):
                ins = e.matmul(bo[0:64, 0:129], lhsT=e_[:, i * 64:(i + 1) * 64], rhs=va[:, rs + i, 0:129], start=(i == 0), stop=(i == 7))
            return ins
        P.op("pe", mm_o, reads=[ek] + va_keys, writes=[bok])
        rc_, rck = rc[r % 2], ("n_rc", r % 2)
        P.op("dve", lambda e, rc_=rc_, bo=bo: e.reciprocal(out=rc_[:, :], in_=bo[0:64, 128:129]), reads=[bok], writes=[rck])
        obuf, obk = ob[(r // 8) % 2], ("n_ob", (r // 8) % 2)
        P.op("act", lambda e, obuf=obuf, bo=bo, rc_=rc_, r=r: e.activation(out=obuf[:, r % 8, :], in_=bo[0:64, 0:128], func=AF.Copy, scale=rc_[:, 0:1]),
             reads=[bok, rck], writes=[obk + (r % 8,)])
        if r % 2 == 1:
            yield
        if r % 8 == 7:
            g = r // 8
            toks.append(P.dma("sp", o[g * 512:(g + 1) * 512, :].rearrange("(r p) d -> p r d", p=64), obuf[:, :, :],
                              reads=[obk + (i,) for i in range(8)]))
    return toks


def emit_lru(P, PS, nc, io):
    xpf, xpb, wtap, bconv, wad, wid, gb, lam, hout = io
    tap = P.sb([128, 8], F32, "l_tap")
    bc = P.sb([128, 1], F32, "l_bc")
    gbt = P.sb([128, 4], F32, "l_gb")
    lm = P.sb([128, 8], F32, "l_lm")
    wa16 = P.sb([128, 2, 128], BF16, "l_wa16")
    wi16 = P.sb([128, 2, 128], BF16, "l_wi16")
    P.dma("sp", tap[:, :], wtap[:, :], writes=["l_tap"])
    P.dma("sp", bc[:, :], bconv[:, :], writes=["l_bc"])
    P.dma("sp", gbt[:, :], gb[:, :], writes=["l_gb"])
    P.dma("sp", lm[:, 0:2], lam[:, :], writes=["l_lm0"])
    P.dma("pool", wa16[:, :, :], wad[:, :, :], writes=["l_wa16"])
    P.dma("pool", wi16[:, :, :], wid[:, :, :], writes=["l_wi16"])
    P.op("act", lambda e: e.activation(out=lm[:, 2:4], in_=lm[:, 0:2], func=AF.Exp, scale=-1.0), reads=["l_lm0"], writes=["l_lm1"])
    P.op("act", lambda e: e.activation(out=lm[:, 4:6], in_=lm[:, 2:4], func=AF.Ln, bias=1.0), reads=["l_lm1"], writes=["l_lm2"])
    P.op("act", lambda e: e.mul(out=lm[:, 6:8], in_=lm[:, 4:6], mul=-8.0), reads=["l_lm2"], writes=["l_lm3"])
    xp = P.sb([128, LB + 3], F32, "l_xp")
    xc = P.sb([128, LB], F32, "l_xc")
    xc16 = P.sb([128, LB], BF16, "l_xc16")
    rg = P.sb([128, LB], F32, "l_rg")
    ig = P.sb([128, LB], F32, "l_ig")
    hb = [P.sb([128, LB], F32, f"l_h{i}") for i in range(2)]
    toks = []
    it = 0
    for d in range(2):
        src = xpf if d == 0 else xpb
        for b in range(SEQ // LB):
            t0 = b * LB
            P.dma("sp", xp[:, :], src[:, t0:t0 + LB + 3], writes=["l_xp"])
            P.op("dve", lambda e, d=d: e.tensor_scalar(out=xc[:, :], in0=xp[:, 0:LB], scalar1=tap[:, 4 * d:4 * d + 1], scalar2=bc[:, 0:1],
                                                       op0=ALU.mult, op1=ALU.add), reads=["l_xp", "l_tap", "l_bc"], writes=["l_xc"])
            for j in range(1, 4):
                P.op("dve", lambda e, d=d, j=j: e.scalar_tensor_tensor(out=xc[:, :], in0=xp[:, j:j + LB], scalar=tap[:, 4 * d + j:4 * d + j + 1],
                                                                       in1=xc[:, :], op0=ALU.mult, op1=ALU.add),
                     reads=["l_xp", "l_tap", "l_xc"], writes=["l_xc"])
            P.op("pool", lambda e: e.tensor_copy(out=xc16[:, :], in_=xc[:, :]), reads=["l_xc"], writes=["l_xc16"])
            for s in range(LB // 512):
                sl = slice(s * 512, (s + 1) * 512)
                br, brk = PS.next()
                bi_, bik = PS.next()
                P.op("pe", lambda e, d=d, sl=sl, br=br: e.matmul(br[:, :], lhsT=wa16[:, d, :], rhs=xc16[:, sl], start=True, stop=True),
                     reads=["l_wa16", "l_xc16"], writes=[brk])
                P.op("pe", lambda e, d=d, sl=sl, bi_=bi_: e.matmul(bi_[:, :], lhsT=wi16[:, d, :], rhs=xc16[:, sl], start=True, stop=True),
                     reads=["l_wi16", "l_xc16"], writes=[bik])
                P.op("act", lambda e, d=d, sl=sl, br=br: e.activation(out=rg[:, sl], in_=br[:, :], func=AF.Sigmoid, bias=gbt[:, 2 * d:2 * d + 1]),
                     reads=[brk, "l_gb"], writes=[("l_rg", s)])
                P.op("act", lambda e, d=d, sl=sl, bi_=bi_: e.activation(out=ig[:, sl], in_=bi_[:, :], func=AF.Sigmoid, bias=gbt[:, 2 * d + 1:2 * d + 2]),
                     reads=[bik, "l_gb"], writes=[("l_ig", s)])
                yield
            rgk = [("l_rg", s) for s in range(LB // 512)]
            igk = [("l_ig", s) for s in range(LB // 512)]
            P.op("act", lambda e, d=d: e.activation(out=rg[:, :], in_=rg[:, :], func=AF.Exp, scale=lm[:, 6 + d:7 + d]), reads=rgk + ["l_lm3"], writes=rgk)
            P.op("dve", lambda e: e.tensor_tensor(out=ig[:, :], in0=ig[:, :], in1=xc[:, :], op=ALU.mult), reads=igk + ["l_xc"], writes=igk)
            P.op("pool", lambda e: e.tensor_tensor(out=xc[:, :], in0=rg[:, :], in1=rg[:, :], op=ALU.mult), reads=rgk + igk, writes=["l_xc"])
            P.op("act", lambda e: e.activation(out=xc[:, :], in_=xc[:, :], func=AF.Sqrt, scale=-1.0, bias=1.0), reads=["l_xc"], writes=["l_xc"])
            P.op("dve", lambda e: e.tensor_tensor(out=ig[:, :], in0=ig[:, :], in1=xc[:, :], op=ALU.mult), reads=igk + ["l_xc"], writes=igk)
            h, hk = hb[it % 2], ("l_h", it % 2)
            hprev, hpk = hb[(it + 1) % 2], ("l_h", (it + 1) % 2)
            if b == 0:
                P.op("dve", lambda e, h=h: e.tensor_tensor_scan(out=h[:, :], data0=rg[:, :], data1=ig[:, :], initial=0.0,
                                                                op0=ALU.mult, op1=ALU.add), reads=rgk + igk, writes=[hk])
            else:
                P.op("dve", lambda e, h=h, hprev=hprev: e.tensor_tensor_scan(out=h[:, :], data0=rg[:, :], data1=ig[:, :],
                                                                             initial=hprev[:, LB - 1:LB], op0=ALU.mult, op1=ALU.add),
                     reads=rgk + igk + [hpk], writes=[hk])
            toks.append(P.dma("sp", hout[d, :, t0:t0 + LB], h[:, :], reads=[hk]))
            it += 1
            yield
    return toks


def build_B(parts=("ret", "na", "lru")):
    nc = bass.Bass("TRN2", target_bir_lowering=False)
    di = lambda n, s: nc.dram_tensor(n, s, F32, kind="ExternalInput").ap()
    do = lambda n, s: nc.dram_tensor(n, s, F32, kind="ExternalOutput").ap()
    P = Prog(nc)
    PS = PsumRing(P)
    toks = []
    gens = []
    if "ret" in parts:
        io = (di("r_qT", [256, SEQ]), di("r_kT", [256, SEQ]), di("r_ktm", [SEQ, 256]), di("r_v", [SEQ, 256]),
              di("r_cosT", [128, SEQ]), di("r_sinT", [128, SEQ]), di("r_costm", [SEQ, 128]), di("r_sintm", [SEQ, 128]),
              di("r_logit", [128, 1]), di("r_diffT", [128, 128]), di("r_keepT", [128, 128]), di("r_idx1", [128, 128]),
              di("r_kidx", [128, 1]), do("r_y", [SEQ, 256]))
        gens.append(emit_retention(P, PS.sub(range(0, 4)), nc, io))
    if "na" in parts:
        io = (di("n_qT", [128, SEQ]), di("n_kT", [128, SEQ]), di("n_v64", [64, 128, 128]), di("n_bias", [64, 8, 512]),
              do("n_o", [SEQ, 128]))
        gens.append(emit_na(P, PS.sub(range(4, 6)), nc, io))
    if "lru" in parts:
        io = (di("l_xpf", [128, SEQ + 3]), di("l_xpb", [128, SEQ + 3]), di("l_wtap", [128, 8]), di("l_bconv", [128, 1]),
              di("l_wa", [128, 2, 128]), di("l_wi", [128, 2, 128]), di("l_gb", [128, 4]), di("l_lam", [128, 2]),
              do("l_h", [2, 128, SEQ]))
        gens.append(emit_lru(P, PS.sub(range(6, 8)), nc, io))
    while gens:
        for g in list(gens):
            try:
                next(g)
            except StopIteration as stop:
                toks += stop.value
                gens.remove(g)
    P.emit(toks)
    return nc


def _rot_tables():
    half = 128
    inv = (10000.0 ** (-np.arange(half, dtype=np.float32) / np.float32(half))).astype(np.float32)
    pos = np.arange(SEQ, dtype=np.float32)
    ang = (pos[:, None] * inv[None, :]).astype(np.float32)
    return np.cos(ang).astype(np.float32), np.sin(ang).astype(np.float32)


_CONST = {}


def _consts():
    if _CONST:
        return _CONST
    cos, sin = _rot_tables()
    _CONST["cos_tm"] = [cos, np.ascontiguousarray(cos[::-1])]
    _CONST["sin_tm"] = [sin, np.ascontiguousarray(sin[::-1])]
    _CONST["cos_fm"] = [np.ascontiguousarray(c.T) for c in _CONST["cos_tm"]]
    _CONST["sin_fm"] = [np.ascontiguousarray(c.T) for c in _CONST["sin_tm"]]
    idx = np.arange(128, dtype=np.float32)
    diff = idx[None, :] - idx[:, None]
    keep = [(diff >= 0), (diff > 0)]
    _CONST["diffT"] = [np.where(k, diff, 0.0).astype(np.float32) for k in keep]
    _CONST["keepT"] = [k.astype(np.float32) for k in keep]
    _CONST["idx1"] = np.ascontiguousarray(np.broadcast_to((idx + 1.0)[None, :], (128, 128))).astype(np.float32)
    _CONST["kidx"] = (127.0 - idx)[:, None].astype(np.float32)
    kc = np.arange(64)[:, None, None, None]
    cl = np.arange(8)[None, :, None, None]
    ki = np.arange(8)[None, None, :, None]
    q = np.arange(64)[None, None, None, :]
    cstart = np.clip(q - 8, 0, 48)
    inwin = (kc >= cstart) & (kc < cstart + 16)
    dr = ki - cl + 7 + 0 * kc + 0 * q
    dc = np.clip(kc - q, -15, 15) + 15 + 0 * cl + 0 * ki
    _CONST["na_dr"] = np.broadcast_to(dr, (64, 8, 8, 64)).copy()
    _CONST["na_dc"] = np.broadcast_to(dc, (64, 8, 8, 64)).copy()
    _CONST["na_win"] = np.broadcast_to(inwin, (64, 8, 8, 64)).copy()
    return _CONST


def prep_B(projT, l, inp):
    C = _consts()
    ims = []
    for c in range(NCORES):
        hh, dd = c // 2, c % 2
        fl = (lambda a: a[:, ::-1]) if dd else (lambda a: a)
        m = {}
        qT = fl(projT[hh * 256:(hh + 1) * 256])
        kT = fl(projT[1024 + hh * 256:1024 + (hh + 1) * 256])
        vT = fl(projT[2048 + hh * 256:2048 + (hh + 1) * 256])
        m["r_qT"] = np.ascontiguousarray(qT)
        m["r_kT"] = np.ascontiguousarray(kT)
        m["r_ktm"] = np.ascontiguousarray(kT.T)
        m["r_v"] = np.ascontiguousarray(vT.T)
        m["r_cosT"], m["r_sinT"] = C["cos_fm"][dd], C["sin_fm"][dd]
        m["r_costm"], m["r_sintm"] = C["cos_tm"][dd], C["sin_tm"][dd]
        m["r_logit"] = np.full((128, 1), inp["ret_decay"][l, dd, hh], np.float32)
        m["r_diffT"], m["r_keepT"] = C["diffT"][dd], C["keepT"][dd]
        m["r_idx1"], m["r_kidx"] = C["idx1"], C["kidx"]
        m["n_qT"] = np.ascontiguousarray(projT[4096 + c * 128:4096 + (c + 1) * 128])
        m["n_kT"] = np.ascontiguousarray(projT[5120 + c * 128:5120 + (c + 1) * 128])
        vh = projT[6144 + c * 128:6144 + (c + 1) * 128]
        m["n_v64"] = np.ascontiguousarray(vh.T.reshape(128, 64, 128).transpose(1, 0, 2))
        rpb = inp["na_rpb"][l, c]
        drv = np.clip(C["na_dr"], 0, 14)
        b = rpb[drv, C["na_dc"]]
        valid = C["na_win"] & (C["na_dr"] >= 0) & (C["na_dr"] <= 14)
        b = np.where(valid, b, np.float32(NEG)).astype(np.float32)
        m["n_bias"] = np.ascontiguousarray(b.reshape(64, 8, 512))
        x = projT[7168 + c * 128:7168 + (c + 1) * 128]
        xpf = np.zeros((128, SEQ + 3), np.float32)
        xpf[:, 2:2 + SEQ] = x
        xpb = np.zeros((128, SEQ + 3), np.float32)
        xpb[:, 1:1 + SEQ] = x[:, ::-1]
        m["l_xpf"], m["l_xpb"] = xpf, xpb
        wc = inp["w_conv"][l][:, c * 128:(c + 1) * 128]
        m["l_wtap"] = np.ascontiguousarray(np.concatenate([wc.T, wc[::-1].T], axis=1))
        m["l_bconv"] = np.ascontiguousarray(inp["b_conv"][l][c * 128:(c + 1) * 128, None])
        m["l_wa"] = np.ascontiguousarray(inp["lru_wa"][l][:, c].transpose(1, 0, 2))
        m["l_wi"] = np.ascontiguousarray(inp["lru_wi"][l][:, c].transpose(1, 0, 2))
        sl = slice(c * 128, (c + 1) * 128)
        m["l_gb"] = np.ascontiguousarray(np.stack([inp["lru_ba"][l][0, sl], inp["lru_bi"][l][0, sl],
                                                   inp["lru_ba"][l][1, sl], inp["lru_bi"][l][1, sl]], axis=1))
        m["l_lam"] = np.ascontiguousarray(inp["lru_lambda"][l][:, sl].T)
        ims.append(m)
    return ims


def post_B(results):
    yf = np.concatenate([results[2 * h]["r_y"] for h in range(4)], axis=1)
    yb = np.concatenate([results[2 * h + 1]["r_y"][::-1] for h in range(4)], axis=1)
    na = np.concatenate([results[c]["n_o"] for c in range(8)], axis=1)
    hf = np.concatenate([results[c]["l_h"][0] for c in range(8)], axis=0)
    hb = np.concatenate([results[c]["l_h"][1][:, ::-1] for c in range(8)], axis=0)
    return yf, yb, na, hf, hb


def fm_stats(P, PS, x, xk, KC, T, ones, scr, sk):
    nfeat = KC * 128
    P.op("act", lambda e: e.activation(out=scr[:, 0:KC, 0:T], in_=x[:, 0:KC, 0:T], func=AF.Square), reads=xk, writes=sk)
    b_sum, k_sum = PS.next()
    b_sq, k_sq = PS.next()

    def mm_sum(e):
        for k in range(KC):
            ins = e.matmul(b_sum[:, 0:T], lhsT=ones[:, :], rhs=x[:, k, 0:T], start=(k == 0), stop=(k == KC - 1))
        return ins

    def mm_sq(e):
        for k in range(KC):
            ins = e.matmul(b_sq[:, 0:T], lhsT=ones[:, :], rhs=scr[:, k, 0:T], start=(k == 0), stop=(k == KC - 1))
        return ins
    P.op("pe", mm_sum, reads=xk + ["ones"], writes=[k_sum])
    P.op("pe", mm_sq, reads=sk + ["ones"], writes=[k_sq])
    if not hasattr(P, "_ln_tmp"):
        P._ln_tmp = (P.sb([128, 512], F32, "ln_mean"), P.sb([128, 512], F32, "ln_rstd"))
    mean, rstd = P._ln_tmp
    mk, rk = "ln_mean", "ln_rstd"
    P.op("act", lambda e: e.mul(out=mean[:, 0:T], in_=b_sum[:, 0:T], mul=1.0 / nfeat), reads=[k_sum], writes=[mk])
    P.op("dve", lambda e: e.tensor_tensor(out=rstd[:, 0:T], in0=mean[:, 0:T], in1=mean[:, 0:T], op=ALU.mult), reads=[mk], writes=[rk])
    P.op("dve", lambda e: e.scalar_tensor_tensor(out=rstd[:, 0:T], in0=b_sq[:, 0:T], scalar=1.0 / nfeat, in1=rstd[:, 0:T],
                                                 op0=ALU.mult, op1=ALU.subtract), reads=[k_sq, rk], writes=[rk])
    P.op("dve", lambda e: e.tensor_scalar(out=rstd[:, 0:T], in0=rstd[:, 0:T], scalar1=EPS, scalar2=None, op0=ALU.add),
         reads=[rk], writes=[rk])
    P.op("act", lambda e: e.activation(out=rstd[:, 0:T], in_=rstd[:, 0:T], func=AF.Sqrt), reads=[rk], writes=[rk])
    P.op("dve", lambda e: e.reciprocal(out=rstd[:, 0:T], in_=rstd[:, 0:T]), reads=[rk], writes=[rk])
    return mean, rstd, mk, rk


def build_C(with_next_proj):
    T = 512
    nc = bass.Bass("TRN2", target_bir_lowering=False)
    di = lambda n, s: nc.dram_tensor(n, s, F32, kind="ExternalInput").ap()
    do = lambda n, s: nc.dram_tensor(n, s, F32, kind="ExternalOutput").ap()
    c_yf, c_yb, c_g = di("c_yf", [1024, TPC]), di("c_yb", [1024, TPC]), di("c_g", [1024, TPC])
    c_na, c_hf, c_hb, c_ly = di("c_na", [1024, TPC]), di("c_hf", [1024, TPC]), di("c_hb", [1024, TPC]), di("c_ly", [1024, TPC])
    c_gp, c_gb, c_h = di("c_gp", [6144, TPC]), di("c_gb", [128, 48]), di("c_h", [D, TPC])
    w_br, w_out = di("w_br", [3072, D]), di("w_out", [D, D])
    ln1g, ln1b, ln2g, ln2b = di("ln1g", [128, 16]), di("ln1b", [128, 16]), di("ln2g", [128, 16]), di("ln2b", [128, 16])
    w_f1, w_f2 = di("w_f1", [D, 2 * DFF]), di("w_f2", [DFF, D])
    h2T = do("h2T", [D, TPC])
    h2s = nc.dram_tensor("h2s", [D, TPC], F32).ap()
    if with_next_proj:
        w_in = di("w_in", [D, IN_COLS])
        projT = do("projT", [IN_COLS, TPC])
    P = Prog(nc)
    PS = PsumRing(P)
    ones = P.sb([128, 128], F32, "ones")
    P.op("pool", lambda e: e.memset(ones[:, :], 1.0), writes=["ones"])
    g1, b1 = load_consts(P, ln1g, "g1", 16), load_consts(P, ln1b, "b1", 16)
    g2, b2 = load_consts(P, ln2g, "g2", 16), load_consts(P, ln2b, "b2", 16)
    gbt = load_consts(P, c_gb, "gbt", 48)
    A = P.sb([128, 16, T], F32, "arenaA")
    B = P.sb([128, 16, T], F32, "arenaB")
    U = P.sb([128, 44 * T], BF16, "arenaU")
    U3 = U[:, :].rearrange("p (k t) -> p k t", t=T)
    h1_16 = P.sb([128, 16, T], BF16, "h1_16")
    wflat = [P.sb([128, 5632], BF16, f"wf{i}") for i in range(4)]
    wv = lambda kc, wc: [w[:, 0:kc * wc].rearrange("p (k c) -> p k c", c=wc) for w in wflat]
    sc = [P.sb([128, T], F32, f"sc{i}") for i in range(8)]
    sck = [("sc", i) for i in range(8)]
    gpt = [P.sb([128, T], F32, f"gpt{i}") for i in range(2)]
    Ak = [("A", k) for k in range(16)]
    Bk = [("B", k) for k in range(16)]
    Uk = [("U", k) for k in range(44)]
    toks = []
    for half in range(2):
        ts = slice(half * T, (half + 1) * T)
        for hh in range(4):
            rows = slice(hh * 256, (hh + 1) * 256)
            o8 = 8 * (hh % 2)
            y, yk = A[:, o8:o8 + 2, :], Ak[o8:o8 + 2]
            y2, y2k = A[:, o8 + 2:o8 + 4, :], Ak[o8 + 2:o8 + 4]
            gg, ggk = A[:, o8 + 4:o8 + 6, :], Ak[o8 + 4:o8 + 6]
            sq, sqk = A[:, o8 + 6:o8 + 8, :], Ak[o8 + 6:o8 + 8]
            P.dma("sp", y, c_yf[rows, ts].rearrange("(k p) t -> p k t", p=128), writes=yk)
            P.dma("act", y2, c_yb[rows, ts].rearrange("(k p) t -> p k t", p=128), writes=y2k)
            P.dma("sp", gg, c_g[rows, ts].rearrange("(k p) t -> p k t", p=128), writes=ggk)
            P.op("dve", lambda e, y=y, y2=y2: e.tensor_tensor(out=y, in0=y, in1=y2, op=ALU.add), reads=yk + y2k, writes=yk)
            mean, rstd, mk, rk = fm_stats(P, PS, y, yk, 2, T, ones, sq, sqk)
            P.op("act", lambda e, gg=gg: e.activation(out=gg, in_=gg, func=AF.Silu), reads=ggk, writes=ggk)
            for j in range(2):
                P.op("dve", lambda e, j=j, o8=o8: e.tensor_tensor(out=A[:, o8 + j, :], in0=A[:, o8 + j, :], in1=mean[:, 0:T], op=ALU.subtract),
                     reads=[Ak[o8 + j], mk], writes=[Ak[o8 + j]])
                P.op("dve", lambda e, j=j, o8=o8: e.tensor_tensor(out=A[:, o8 + j, :], in0=A[:, o8 + j, :], in1=rstd[:, 0:T], op=ALU.mult),
                     reads=[Ak[o8 + j], rk], writes=[Ak[o8 + j]])
                P.op("dve", lambda e, j=j, hh=hh, o8=o8: e.tensor_tensor(out=U3[:, 2 * hh + j, :], in0=A[:, o8 + j, :], in1=A[:, o8 + 4 + j, :], op=ALU.mult),
                     reads=[Ak[o8 + j], Ak[o8 + 4 + j]], writes=[Uk[2 * hh + j]])
        P.dma("pool", U3[:, 8:16, :], c_na[:, ts].rearrange("(k p) t -> p k t", p=128), writes=Uk[8:16])
        for k in range(8):
            rows = slice(k * 128, (k + 1) * 128)
            o4 = 4 * (k % 2)
            hf_, hb_, ly_, t_ = sc[o4], sc[o4 + 1], sc[o4 + 2], sc[o4 + 3]
            k0, k1, k2, k3 = sck[o4], sck[o4 + 1], sck[o4 + 2], sck[o4 + 3]
            P.dma("sp", hf_[:, :], c_hf[rows, ts], writes=[k0])
            P.dma("act", hb_[:, :], c_hb[rows, ts], writes=[k1])
            P.dma("sp", ly_[:, :], c_ly[rows, ts], writes=[k2])
            P.op("dve", lambda e, hf_=hf_, hb_=hb_: e.tensor_tensor(out=hf_[:, :], in0=hf_[:, :], in1=hb_[:, :], op=ALU.add), reads=[k0, k1], writes=[k0])
            P.op("dve", lambda e, t_=t_, ly_=ly_: e.tensor_tensor(out=t_[:, :], in0=ly_[:, :], in1=ly_[:, :], op=ALU.mult), reads=[k2], writes=[k3])
            P.op("dve", lambda e, t_=t_: e.tensor_scalar(out=t_[:, :], in0=t_[:, :], scalar1=0.044715, scalar2=1.0, op0=ALU.mult, op1=ALU.add),
                 reads=[k3], writes=[k3])
            P.op("dve", lambda e, t_=t_, ly_=ly_: e.tensor_tensor(out=t_[:, :], in0=t_[:, :], in1=ly_[:, :], op=ALU.mult), reads=[k3, k2], writes=[k3])
            P.op("act", lambda e, t_=t_: e.activation(out=t_[:, :], in_=t_[:, :], func=AF.Sigmoid, scale=1.5957691216057308), reads=[k3], writes=[k3])
            P.op("dve", lambda e, t_=t_, ly_=ly_: e.tensor_tensor(out=t_[:, :], in0=t_[:, :], in1=ly_[:, :], op=ALU.mult), reads=[k3, k2], writes=[k3])
            P.op("dve", lambda e, k=k, t_=t_, hf_=hf_: e.tensor_tensor(out=U3[:, 16 + k, :], in0=t_[:, :], in1=hf_[:, :], op=ALU.mult),
                 reads=[k3, k0], writes=[Uk[16 + k]])
        for b in range(3):
            def evac(ci, hf, bank, bk, b=b):
                gp_, gpk = gpt[ci % 2], ("gpt", ci % 2)
                rows = slice(b * 2048 + ci * 128, b * 2048 + (ci + 1) * 128)
                P.dma("sp", gp_[:, :], c_gp[rows, ts], writes=[gpk])
                P.op("act", lambda e: e.activation(out=gp_[:, :], in_=gp_[:, :], func=AF.Sigmoid, bias=gbt[:, b * 16 + ci:b * 16 + ci + 1]),
                     reads=[gpk, "gbt"], writes=[gpk])
                if b == 0:
                    P.op("dve", lambda e: e.tensor_tensor(out=A[:, ci, :], in0=bank[:, 0:T], in1=gp_[:, :], op=ALU.mult),
                         reads=[bk, gpk], writes=[Ak[ci]])
                else:
                    P.op("dve", lambda e: e.tensor_tensor(out=gp_[:, :], in0=bank[:, 0:T], in1=gp_[:, :], op=ALU.mult),
                         reads=[bk, gpk], writes=[gpk])
                    if b == 1:
                        P.op("dve", lambda e: e.tensor_tensor(out=A[:, ci, :], in0=A[:, ci, :], in1=gp_[:, :], op=ALU.add),
                             reads=[Ak[ci], gpk], writes=[Ak[ci]])
                    else:
                        P.op("dve", lambda e: e.tensor_tensor(out=U3[:, 24 + ci, :], in0=A[:, ci, :], in1=gp_[:, :], op=ALU.add),
                             reads=[Ak[ci], gpk], writes=[Uk[24 + ci]])
            stream_matmul_fm(P, PS, w_br[b * 1024:(b + 1) * 1024, :], 0, D, 8, U3[:, 8 * b:8 * b + 8, :], Uk[8 * b:8 * b + 8], T, evac,
                             wv(8, 512), "W", WC=512)
        P.dma("sp", B[:, :, :], c_h[:, ts].rearrange("(k p) t -> p k t", p=128), writes=Bk)

        def evac_o(ci, hf, bank, bk):
            P.op("dve", lambda e: e.scalar_tensor_tensor(out=B[:, ci, :], in0=B[:, ci, :], scalar=ALPHA, in1=bank[:, 0:T],
                                                         op0=ALU.mult, op1=ALU.add), reads=[bk, Bk[ci]], writes=[Bk[ci]])
        stream_matmul_fm(P, PS, w_out, 0, D, 16, U3[:, 24:40, :], Uk[24:40], T, evac_o, wv(16, 256), "W", WC=256)
        layernorm_fm(P, PS, B, ("B",), T, g1, b1, ["g1", "b1"], ones, B, ("B",), h1_16, ("h1_16",), A, ("A",))
        h1k = [("h1_16", k) for k in range(16)]

        def evac_f(ci, hf, bank, bk):
            if ci < 44:
                P.op("act", lambda e: e.activation(out=U3[:, ci, :], in_=bank[:, 0:T], func=AF.Silu), reads=[bk], writes=[Uk[ci]])
            else:
                f = ci - 44
                P.op("dve", lambda e: e.tensor_tensor(out=U3[:, f, :], in0=bank[:, 0:T], in1=U3[:, f, :], op=ALU.mult),
                     reads=[bk, Uk[f]], writes=[Uk[f]])
        stream_matmul_fm(P, PS, w_f1, 0, 2 * DFF, 16, h1_16, h1k, T, evac_f, wv(16, 256), "W", WC=256)

        def evac_2(ci, hf, bank, bk):
            P.op("dve", lambda e: e.scalar_tensor_tensor(out=B[:, ci, :], in0=B[:, ci, :], scalar=ALPHA, in1=bank[:, 0:T],
                                                         op0=ALU.mult, op1=ALU.add), reads=[bk, Bk[ci]], writes=[Bk[ci]])
        stream_matmul_fm(P, PS, w_f2, 0, D, 44, U3, Uk, T, evac_2, wv(22, 256), "W", WC=256, KT=2)
        layernorm_fm(P, PS, B, ("B",), T, g2, b2, ["g2", "b2"], ones, B, ("B",), h1_16, ("h1_16",), A, ("A",))
        toks.append(P.dma("sp", h2T[:, ts].rearrange("(k p) t -> p k t", p=128), B[:, :, :], reads=Bk))
        if with_next_proj:
            P.dma("act", h2s[:, ts].rearrange("(k p) t -> p k t", p=128), B[:, :, :], reads=Bk, writes=[("h2s", half)])
    if with_next_proj:
        h16 = U[:, 0:16 * TPC].rearrange("p (k t) -> p k t", t=TPC)
        for k in range(16):
            P.dma("pool", h16[:, k, :], h2s[k * 128:(k + 1) * 128, :], reads=[("h2s", 0), ("h2s", 1)], writes=Uk[2 * k:2 * k + 2])
        obufs = [P.sb([128, 512], F32, f"ob{i}") for i in range(2)]
        toks += emit_inproj(P, PS, h16, Uk[0:32], w_in, projT, wv(16, 256), obufs, wname="W")
    P.emit(toks)
    return nc


def prep_C(l, inp, projT, hT, yf, yb, na, hf, hb, with_next):
    yfT, ybT, naT = np.ascontiguousarray(yf.T), np.ascontiguousarray(yb.T), np.ascontiguousarray(na.T)
    fm16 = lambda v: np.ascontiguousarray(v.reshape(16, 128).T)
    shared = {
        "c_gb": np.ascontiguousarray(inp["gate_b"][l].reshape(48, 128).T),
        "w_br": np.ascontiguousarray(inp["w_branch"][l].reshape(3072, D)), "w_out": inp["w_out"][l],
        "ln1g": fm16(inp["ln1_g"][l]), "ln1b": fm16(inp["ln1_b"][l]), "ln2g": fm16(inp["ln2_g"][l]), "ln2b": fm16(inp["ln2_b"][l]),
        "w_f1": inp["w_ffn_in"][l], "w_f2": inp["w_ffn_out"][l],
    }
    if with_next:
        shared["w_in"] = inp["w_in"][l + 1]
    ims = []
    for c in range(NCORES):
        ts = slice(c * TPC, (c + 1) * TPC)
        cc = lambda a: np.ascontiguousarray(a[:, ts])
        m = dict(shared)
        m.update({"c_yf": cc(yfT), "c_yb": cc(ybT), "c_g": cc(projT[3072:4096]), "c_na": cc(naT), "c_hf": cc(hf), "c_hb": cc(hb),
                  "c_ly": cc(projT[8192:9216]), "c_gp": cc(projT[9216:15360]), "c_h": cc(hT)})
        ims.append(m)
    return ims


def _run(nc, ims):
    return run_bass_kernel_spmd(nc, ims, core_ids=list(range(NCORES))).results


def kernel(**inputs):
    inp = {k: np.asarray(v) for k, v in inputs.items()}
    x = inp["x"][0]
    fm16 = lambda v: np.ascontiguousarray(v.reshape(16, 128).T)
    ims = [{"xT": np.ascontiguousarray(x[c * TPC:(c + 1) * TPC].T), "lng": fm16(inp["ln_in_g"]), "lnb": fm16(inp["ln_in_b"]),
            "w_in": inp["w_in"][0]} for c in range(NCORES)]
    res = _run(build_A0(), ims)
    hT = np.concatenate([r["hT"] for r in res], axis=1)
    projT = np.concatenate([r["projT"] for r in res], axis=1)
    for l in range(DEPTH):
        resB = _run(build_B(), prep_B(projT, l, inp))
        yf, yb, na, hf, hb = post_B(resB)
        del resB
        nxt = l + 1 < DEPTH
        res = _run(build_C(nxt), prep_C(l, inp, projT, hT, yf, yb, na, hf, hb, nxt))
        del yf, yb, na, hf, hb
        hT = np.concatenate([r["h2T"] for r in res], axis=1)
        if nxt:
            projT = np.concatenate([r["projT"] for r in res], axis=1)
        del res
    return np.ascontiguousarray(hT.T)[None].astype(np.float32)
```

```python
import contextlib
import numpy as np
import concourse.bass as bass
import concourse.mybir as mybir
from concourse.bass_utils import run_bass_kernel_spmd

F32 = mybir.dt.float32
BF16 = mybir.dt.bfloat16
AF = mybir.ActivationFunctionType
ALU = mybir.AluOpType
AX = mybir.AxisListType

NCORES = 8
D = 2048
SEQ = 8192
TPC = SEQ // NCORES
DEPTH = 4
IN_COLS = 15360
DFF = 5632
ALPHA = (2 * DEPTH) ** 0.25
EPS = 1e-5
GRID_W = 64


class Prog:
    ENGS = ("pe", "act", "dve", "pool", "sp")
    DMA_RING = 16

    def __init__(self, nc):
        self.nc = nc
        self.stack = contextlib.ExitStack()
        self.ops = {e: [] for e in self.ENGS}
        self.cnt = {e: 0 for e in self.ENGS}
        self.dcnt = {e: 0 for e in self.ENGS}
        self.known = {e: {} for e in self.ENGS}
        self.last_w = {}
        self.readers = {}
        self.sem = {e: self.stack.enter_context(nc.semaphore("s_" + e)) for e in self.ENGS}
        self.dsem = {}
        for q in ("sp", "pool", "act"):
            self.dsem[q] = [self.stack.enter_context(nc.semaphore(f"d_{q}{i}")) for i in range(self.DMA_RING)]
        self.out_tokens = []
        self._n = 0

    def sb(self, shape, dtype, name=None):
        self._n += 1
        return self.stack.enter_context(self.nc.sbuf_tensor("sb_" + (name or f"t{self._n}"), list(shape), dtype))

    def ps(self, name=None):
        self._n += 1
        return self.stack.enter_context(self.nc.psum_tensor(name or f"p{self._n}", [128, 512], F32))

    def _tok_sem(self, tok):
        if tok[0] == "e":
            return ("e", tok[1]), tok[2]
        return ("d", tok[1], tok[2] % self.DMA_RING), 16 * (tok[2] // self.DMA_RING + 1)

    def _waits(self, eng, reads, writes):
        toks = set()
        for r in reads:
            if r in self.last_w:
                toks.add(self.last_w[r])
        for w in writes:
            if w in self.last_w:
                toks.add(self.last_w[w])
            for t in self.readers.get(w, ()):
                toks.add(t)
        need = {}
        for t in toks:
            key, val = self._tok_sem(t)
            if self.known[eng].get(key, 0) >= val:
                continue
            need[key] = max(need.get(key, 0), val)
        for key, val in need.items():
            self.known[eng][key] = val
        return list(need.items())

    def _commit(self, tok, reads, writes):
        for r in reads:
            self.readers.setdefault(r, []).append(tok)
        for w in writes:
            self.last_w[w] = tok
            self.readers[w] = []

    def op(self, eng, fn, reads=(), writes=()):
        waits = self._waits(eng, reads, writes)
        self.cnt[eng] += 1
        tok = ("e", eng, self.cnt[eng])
        self.ops[eng].append(("op", fn, waits))
        self._commit(tok, reads, writes)
        return tok

    def dma(self, q, out, in_, reads=(), writes=(), **kw):
        idx = self.dcnt[q]
        waits = self._waits(q, reads, writes)
        if idx >= self.DMA_RING:
            key, val = self._tok_sem(("d", q, idx - self.DMA_RING))
            if self.known[q].get(key, 0) < val:
                self.known[q][key] = val
                waits.append((key, val))
        self.dcnt[q] += 1
        tok = ("d", q, idx)
        self.ops[q].append(("dma", (out, in_, kw, idx), waits))
        self._commit(tok, reads, writes)
        return tok

    def _semh(self, key):
        return self.sem[key[1]] if key[0] == "e" else self.dsem[key[1]][key[2]]

    def emit(self, final_tokens):
        nc = self.nc
        emap = {"pe": "tensor", "act": "scalar", "dve": "vector", "pool": "gpsimd", "sp": "sync"}
        fin = {}
        for t in final_tokens:
            key, val = self._tok_sem(t)
            fin[key] = max(fin.get(key, 0), val)
        with nc.Block() as block:
            for e in self.ENGS:
                def body(eng, e=e):
                    for kind, payload, waits in self.ops[e]:
                        for key, val in waits:
                            eng.wait_ge(self._semh(key), val)
                        if kind == "op":
                            ins = payload(eng)
                            ins.then_inc(self.sem[e], 1)
                        else:
                            out, in_, kw, idx = payload
                            eng.dma_start(out=out, in_=in_, **kw).then_inc(self.dsem[e][idx % self.DMA_RING], 16)
                    if e == "sp":
                        for key, val in fin.items():
                            eng.wait_ge(self._semh(key), val)
                getattr(block, emap[e])(body)
        self.stack.close()


class PsumRing:
    def __init__(self, P, n=8):
        self.P = P
        self.banks = [P.ps(f"bank{i}") for i in range(n)]
        self.i = 0

    def next(self):
        b = self.banks[self.i % len(self.banks)]
        k = ("psum", self.i % len(self.banks))
        self.i += 1
        return b, k

    def sub(self, idxs):
        return _SubRing(self, list(idxs))


class _SubRing:
    def __init__(self, parent, idxs):
        self.parent, self.idxs, self.i = parent, idxs, 0

    def next(self):
        j = self.idxs[self.i % len(self.idxs)]
        self.i += 1
        return self.parent.banks[j], ("psum", j)


def load_consts(P, nc_in, name, shape_free, q="sp"):
    t = P.sb([128, shape_free], F32, name)
    P.dma(q, t[:, :], nc_in[:, :], writes=[name])
    return t


def layernorm_fm(P, PS, x, xkey, T, gam, bet, gkeys, ones, out32, out32key, out16, out16key, scr, scrkey, KC=16):
    nfeat = KC * 128
    xk = [xkey + (k,) for k in range(KC)]
    sk = [scrkey + (k,) for k in range(KC)]
    P.op("act", lambda e: e.activation(out=scr[:, 0:KC, 0:T], in_=x[:, 0:KC, 0:T], func=AF.Square),
         reads=xk, writes=sk)
    b_sum, k_sum = PS.next()
    b_sq, k_sq = PS.next()

    def mm_sum(e):
        for k in range(KC):
            ins = e.matmul(b_sum[:, 0:T], lhsT=ones[:, :], rhs=x[:, k, 0:T], start=(k == 0), stop=(k == KC - 1))
        return ins

    def mm_sq(e):
        for k in range(KC):
            ins = e.matmul(b_sq[:, 0:T], lhsT=ones[:, :], rhs=scr[:, k, 0:T], start=(k == 0), stop=(k == KC - 1))
        return ins
    P.op("pe", mm_sum, reads=xk + ["ones"], writes=[k_sum])
    P.op("pe", mm_sq, reads=sk + ["ones"], writes=[k_sq])
    if not hasattr(P, "_ln_tmp"):
        P._ln_tmp = (P.sb([128, 512], F32, "ln_mean"), P.sb([128, 512], F32, "ln_rstd"))
    mean, rstd = P._ln_tmp
    mk, rk = "ln_mean", "ln_rstd"
    P.op("act", lambda e: e.mul(out=mean[:, 0:T], in_=b_sum[:, 0:T], mul=1.0 / nfeat), reads=[k_sum], writes=[mk])
    P.op("dve", lambda e: e.tensor_tensor(out=rstd[:, 0:T], in0=mean[:, 0:T], in1=mean[:, 0:T], op=ALU.mult),
         reads=[mk], writes=[rk])
    P.op("dve", lambda e: e.scalar_tensor_tensor(out=rstd[:, 0:T], in0=b_sq[:, 0:T], scalar=1.0 / nfeat, in1=rstd[:, 0:T],
                                                 op0=ALU.mult, op1=ALU.subtract), reads=[k_sq, rk], writes=[rk])
    P.op("dve", lambda e: e.tensor_scalar(out=rstd[:, 0:T], in0=rstd[:, 0:T], scalar1=EPS, scalar2=None,
                                          op0=ALU.add), reads=[rk], writes=[rk])
    P.op("act", lambda e: e.activation(out=rstd[:, 0:T], in_=rstd[:, 0:T], func=AF.Sqrt), reads=[rk], writes=[rk])
    P.op("dve", lambda e: e.reciprocal(out=rstd[:, 0:T], in_=rstd[:, 0:T]), reads=[rk], writes=[rk])
    for k in range(KC):
        P.op("dve", lambda e, k=k: e.tensor_tensor(out=scr[:, k, 0:T], in0=x[:, k, 0:T], in1=mean[:, 0:T], op=ALU.subtract),
             reads=[xk[k], mk], writes=[sk[k]])
        P.op("dve", lambda e, k=k: e.tensor_tensor(out=scr[:, k, 0:T], in0=scr[:, k, 0:T], in1=rstd[:, 0:T], op=ALU.mult),
             reads=[sk[k], rk], writes=[sk[k]])
        P.op("act", lambda e, k=k: e.activation(out=out32[:, k, 0:T], in_=scr[:, k, 0:T], func=AF.Identity,
                                                scale=gam[:, k:k + 1], bias=bet[:, k:k + 1]),
             reads=[sk[k]] + list(gkeys), writes=[out32key + (k,)])
        P.op("act", lambda e, k=k: e.activation(out=out16[:, k, 0:T], in_=scr[:, k, 0:T], func=AF.Identity,
                                                scale=gam[:, k:k + 1], bias=bet[:, k:k + 1]),
             reads=[sk[k]] + list(gkeys), writes=[out16key + (k,)])


def stream_matmul_fm(P, PS, w_dram, col0, ncols, KC, rhs16, rhs_keys, T, evac, wbufs, wname, WC=256, TH=512, KT=1):
    assert ncols % 128 == 0 and KC % KT == 0
    KS = KC // KT
    c = 0
    gi = getattr(P, "_wcount", 0)
    while c < ncols:
        w = min(WC, ncols - c)
        tiles = []
        for kt in range(KT):
            wb = wbufs[gi % len(wbufs)]
            wk = (wname, gi % len(wbufs))
            gi += 1
            src = w_dram[kt * KS * 128:(kt + 1) * KS * 128, col0 + c: col0 + c + w].rearrange("(k p) c -> p k c", p=128)
            P.dma("pool", wb[:, 0:KS, 0:w], src, writes=[wk])
            tiles.append((wb, wk))
        for cc in range(w // 128):
            for half in range(T // TH):
                bank, bk = PS.next()

                def mm(e, cc=cc, half=half, bank=bank, tiles=tiles):
                    for kt, (wb, _) in enumerate(tiles):
                        for k in range(KS):
                            kk = kt * KS + k
                            ins = e.matmul(bank[:, 0:TH], lhsT=wb[:, k, cc * 128:(cc + 1) * 128],
                                           rhs=rhs16[:, kk, half * TH:(half + 1) * TH], start=(kk == 0), stop=(kk == KC - 1))
                    return ins
                P.op("pe", mm, reads=[wk for _, wk in tiles] + list(rhs_keys), writes=[bk])
                evac((c // 128) + cc, half, bank, bk)
        c += w
    P._wcount = gi


def emit_inproj(P, PS, h16, h16keys, w_in, projT, wbufs, obufs, wname="w_in"):
    state = {"i": 0}
    toks = []

    def evac(ci, half, bank, bk):
        i = state["i"]
        state["i"] += 1
        ob = obufs[i % len(obufs)]
        ok = ("projo", i % len(obufs))
        eng = "act" if i % 2 == 0 else "dve"
        if eng == "act":
            P.op("act", lambda e: e.copy(out=ob[:, :], in_=bank[:, 0:512]), reads=[bk], writes=[ok])
        else:
            P.op("dve", lambda e: e.tensor_copy(out=ob[:, :], in_=bank[:, 0:512]), reads=[bk], writes=[ok])
        toks.append(P.dma("sp", projT[ci * 128:(ci + 1) * 128, half * 512:(half + 1) * 512], ob[:, :], reads=[ok]))
    stream_matmul_fm(P, PS, w_in, 0, IN_COLS, 16, h16, h16keys, TPC, evac, wbufs, wname)
    return toks


def build_A0():
    nc = bass.Bass("TRN2", target_bir_lowering=False)
    xT = nc.dram_tensor("xT", [D, TPC], F32, kind="ExternalInput").ap()
    lng = nc.dram_tensor("lng", [128, 16], F32, kind="ExternalInput").ap()
    lnb = nc.dram_tensor("lnb", [128, 16], F32, kind="ExternalInput").ap()
    w_in = nc.dram_tensor("w_in", [D, IN_COLS], F32, kind="ExternalInput").ap()
    projT = nc.dram_tensor("projT", [IN_COLS, TPC], F32, kind="ExternalOutput").ap()
    hT = nc.dram_tensor("hT", [D, TPC], F32, kind="ExternalOutput").ap()
    P = Prog(nc)
    PS = PsumRing(P)
    ones = P.sb([128, 128], F32, "ones")
    P.op("pool", lambda e: e.memset(ones[:, :], 1.0), writes=["ones"])
    gam = load_consts(P, lng, "gam", 16)
    bet = load_consts(P, lnb, "bet", 16)
    h16 = P.sb([128, 16, TPC], BF16, "h16")
    x32 = P.sb([128, 16, 512], F32, "x32")
    scr = P.sb([128, 16, 512], F32, "scr")
    toks = []
    x32k = [("x32", k) for k in range(16)]
    for half in range(2):
        P.dma("sp", x32[:, :, :], xT[:, half * 512:(half + 1) * 512].rearrange("(k p) t -> p k t", p=128), writes=x32k)
        h16v = h16[:, :, half * 512:(half + 1) * 512]
        layernorm_fm(P, PS, x32, ("x32",), 512, gam, bet, ["gam", "bet"], ones, x32, ("x32",), h16v, ("h16", half),
                     scr, ("scr",))
        toks.append(P.dma("sp", hT[:, half * 512:(half + 1) * 512].rearrange("(k p) t -> p k t", p=128), x32[:, :, :],
                          reads=x32k))
    wbufs = [P.sb([128, 16, 256], BF16, f"wb{i}") for i in range(2)]
    obufs = [P.sb([128, 512], F32, f"ob{i}") for i in range(4)]
    h16keys = [("h16", hf, k) for hf in range(2) for k in range(16)]
    toks += emit_inproj(P, PS, h16, h16keys, w_in, projT, wbufs, obufs)
    P.emit(toks)
    return nc


NEG = -30000.0
RB = 512
NCH = RB // 128
LB = 1024


def emit_retention(P, PS, nc, io):
    qT, kT, ktm, v, cosT, sinT, costm, sintm, logit, diffT, keepT, idx1, kidx, y = io
    cst = P.sb([128, 8], F32, "r_cst")
    dT = P.sb([128, 128], F32, "r_diffT")
    kpT = P.sb([128, 128], F32, "r_keepT")
    i1 = P.sb([128, 128], F32, "r_idx1")
    maskT = P.sb([128, 128], F32, "r_maskT")
    qdec = P.sb([128, RB], F32, "r_qdec")
    P.dma("sp", cst[:, 0:1], logit[:, :], writes=["r_c0"])
    P.dma("sp", cst[:, 6:7], kidx[:, :], writes=["r_c6"])
    P.dma("sp", dT[:, :], diffT[:, :], writes=["r_dT"])
    P.dma("sp", kpT[:, :], keepT[:, :], writes=["r_kpT"])
    P.dma("sp", i1[:, :], idx1[:, :], writes=["r_i1"])
    P.op("act", lambda e: e.activation(out=cst[:, 1:2], in_=cst[:, 0:1], func=AF.Exp, scale=-1.0), reads=["r_c0"], writes=["r_c1"])
    P.op("act", lambda e: e.activation(out=cst[:, 2:3], in_=cst[:, 1:2], func=AF.Ln, bias=1.0), reads=["r_c1"], writes=["r_c2"])
    P.op("act", lambda e: e.mul(out=cst[:, 3:4], in_=cst[:, 2:3], mul=-1.0), reads=["r_c2"], writes=["r_logg"])
    P.op("act", lambda e: e.activation(out=maskT[:, :], in_=dT[:, :], func=AF.Exp, scale=cst[:, 3:4]),
         reads=["r_dT", "r_logg"], writes=["r_maskT"])
    P.op("dve", lambda e: e.tensor_tensor(out=maskT[:, :], in0=maskT[:, :], in1=kpT[:, :], op=ALU.mult),
         reads=["r_maskT", "r_kpT"], writes=["r_maskT"])
    for n in range(RB // 128):
        P.op("act", lambda e, n=n: e.activation(out=qdec[:, n * 128:(n + 1) * 128], in_=i1[:, :], func=AF.Exp, scale=cst[:, 3:4]),
             reads=["r_i1", "r_logg"], writes=[("r_qdec", n)])
    qdk = [("r_qdec", n) for n in range(RB // 128)]
    P.op("act", lambda e: e.activation(out=cst[:, 4:5], in_=cst[:, 6:7], func=AF.Exp, scale=cst[:, 3:4]),
         reads=["r_c6", "r_logg"], writes=["r_kdec"])
    P.op("act", lambda e: e.activation(out=cst[:, 5:6], in_=cst[:, 3:4], func=AF.Exp, scale=128.0),
         reads=["r_logg"], writes=["r_cdec"])
    S32 = P.sb([128, 512], F32, "r_S32")
    S16 = [P.sb([128, 512], BF16, f"r_S16_{i}") for i in range(2)]
    P.op("pool", lambda e: e.memset(S32[:, :], 0.0), writes=["r_S32"])
    for i in range(2):
        P.op("pool", lambda e, i=i: e.memset(S16[i][:, :], 0.0), writes=[("r_S16", i)])
    qr = P.sb([128, 2, RB], F32, "r_qr")
    kr = P.sb([128, 2, RB], F32, "r_kr")
    ktr = P.sb([128, NCH, 256], F32, "r_ktr")
    vr = P.sb([128, NCH, 256], F32, "r_vr")
    cs = P.sb([128, RB], F32, "r_cs")
    sn = P.sb([128, RB], F32, "r_sn")
    cst_ = P.sb([128, NCH, 128], F32, "r_cstm")
    snt = P.sb([128, NCH, 128], F32, "r_sntm")
    ta = P.sb([128, RB], F32, "r_ta")
    tb = P.sb([128, RB], F32, "r_tb")
    tc_ = P.sb([128, RB], F32, "r_tc")
    q16 = P.sb([128, 2, RB], BF16, "r_q16")
    qd16 = P.sb([128, 2, RB], BF16, "r_qd16")
    k16 = P.sb([128, 2, RB], BF16, "r_k16")
    kt16 = P.sb([128, NCH, 256], BF16, "r_kt16")
    v16 = P.sb([128, NCH, 256], BF16, "r_v16")
    vd16 = P.sb([128, NCH, 256], BF16, "r_vd16")
    sm16 = [P.sb([128, 128], BF16, f"r_sm16_{i}") for i in range(2)]
    yb = [P.sb([128, NCH, 256], F32, f"r_yb{i}") for i in range(2)]
    toks = []
    KS = 256 ** -0.5
    for b in range(SEQ // RB):
        t0 = b * RB
        P.dma("sp", qr[:, :, :], qT[:, t0:t0 + RB].rearrange("(j p) t -> p j t", p=128), writes=["r_qr"])
        P.dma("sp", kr[:, :, :], kT[:, t0:t0 + RB].rearrange("(j p) t -> p j t", p=128), writes=["r_kr"])
        P.dma("act", ktr[:, :, :], ktm[t0:t0 + RB, :].rearrange("(n p) d -> p n d", p=128), writes=["r_ktr"])
        P.dma("act", vr[:, :, :], v[t0:t0 + RB, :].rearrange("(n p) d -> p n d", p=128), writes=["r_vr"])
        P.dma("sp", cs[:, :], cosT[:, t0:t0 + RB], writes=["r_cs"])
        P.dma("sp", sn[:, :], sinT[:, t0:t0 + RB], writes=["r_sn"])
        P.dma("act", cst_[:, :, :], costm[t0:t0 + RB, :].rearrange("(n p) d -> p n d", p=128), writes=["r_cstm"])
        P.dma("act", snt[:, :, :], sintm[t0:t0 + RB, :].rearrange("(n p) d -> p n d", p=128), writes=["r_sntm"])

        def rot_fm(src, skey, outs, okeys, scale):
            t1, t2 = src[:, 0, :], src[:, 1, :]
            P.op("dve", lambda e: e.tensor_tensor(out=ta[:, :], in0=t1, in1=cs[:, :], op=ALU.mult), reads=[skey, "r_cs"], writes=["r_ta"])
            P.op("pool", lambda e: e.tensor_tensor(out=tb[:, :], in0=t2, in1=sn[:, :], op=ALU.mult), reads=[skey, "r_sn"], writes=["r_tb"])
            P.op("dve", lambda e: e.tensor_tensor(out=ta[:, :], in0=ta[:, :], in1=tb[:, :], op=ALU.subtract), reads=["r_ta", "r_tb"], writes=["r_ta"])
            P.op("pool", lambda e: e.tensor_tensor(out=tb[:, :], in0=t1, in1=sn[:, :], op=ALU.mult), reads=[skey, "r_sn", "r_ta"], writes=["r_tb"])
            P.op("dve", lambda e: e.tensor_tensor(out=tc_[:, :], in0=t2, in1=cs[:, :], op=ALU.mult), reads=[skey, "r_cs"], writes=["r_tc"])
            P.op("dve", lambda e: e.tensor_tensor(out=tb[:, :], in0=tb[:, :], in1=tc_[:, :], op=ALU.add), reads=["r_tb", "r_tc"], writes=["r_tb"])
            for (o, ok, extra) in outs:
                if extra is None:
                    P.op("act", lambda e, o=o: e.mul(out=o[:, 0, :], in_=ta[:, :], mul=scale), reads=["r_ta"], writes=[ok + "0"])
                    P.op("act", lambda e, o=o: e.mul(out=o[:, 1, :], in_=tb[:, :], mul=scale), reads=["r_tb"], writes=[ok + "1"])
                else:
                    P.op("dve", lambda e, o=o: e.tensor_tensor(out=o[:, 0, :], in0=ta[:, :], in1=qdec[:, :], op=ALU.mult), reads=["r_ta"] + qdk, writes=[ok + "0"])
                    P.op("pool", lambda e, o=o: e.tensor_tensor(out=o[:, 1, :], in0=tb[:, :], in1=qdec[:, :], op=ALU.mult), reads=["r_tb"] + qdk, writes=[ok + "1"])
        rot_fm(qr, "r_qr", [(q16, "r_q16", None), (qd16, "r_qd16", True)], None, 1.0)
        rot_fm(kr, "r_kr", [(k16, "r_k16", None)], None, KS)
        ta3 = ta[:, :].rearrange("p (n d) -> p n d", d=128)
        tb3 = tb[:, :].rearrange("p (n d) -> p n d", d=128)
        tc3 = tc_[:, :].rearrange("p (n d) -> p n d", d=128)
        t1, t2 = ktr[:, :, 0:128], ktr[:, :, 128:256]
        P.op("dve", lambda e: e.tensor_tensor(out=ta3, in0=t1, in1=cst_[:, :, :], op=ALU.mult), reads=["r_ktr", "r_cstm"], writes=["r_ta"])
        P.op("pool", lambda e: e.tensor_tensor(out=tb3, in0=t2, in1=snt[:, :, :], op=ALU.mult), reads=["r_ktr", "r_sntm"], writes=["r_tb"])
        P.op("dve", lambda e: e.tensor_tensor(out=ta3, in0=ta3, in1=tb3, op=ALU.subtract), reads=["r_ta", "r_tb"], writes=["r_ta"])
        P.op("act", lambda e: e.mul(out=kt16[:, :, 0:128], in_=ta3, mul=KS), reads=["r_ta"], writes=["r_kt16a"])
        P.op("pool", lambda e: e.tensor_tensor(out=tb3, in0=t1, in1=snt[:, :, :], op=ALU.mult), reads=["r_ktr", "r_sntm", "r_ta"], writes=["r_tb"])
        P.op("dve", lambda e: e.tensor_tensor(out=tc3, in0=t2, in1=cst_[:, :, :], op=ALU.mult), reads=["r_ktr", "r_cstm"], writes=["r_tc"])
        P.op("dve", lambda e: e.tensor_tensor(out=tb3, in0=tb3, in1=tc3, op=ALU.add), reads=["r_tb", "r_tc"], writes=["r_tb"])
        P.op("act", lambda e: e.mul(out=kt16[:, :, 128:256], in_=tb3, mul=KS), reads=["r_tb"], writes=["r_kt16b"])
        P.op("pool", lambda e: e.tensor_copy(out=v16[:, :, :], in_=vr[:, :, :]), reads=["r_vr"], writes=["r_v16"])
        P.op("act", lambda e: e.activation(out=vd16[:, :, :], in_=vr[:, :, :], func=AF.Copy, scale=cst[:, 4:5]),
             reads=["r_vr", "r_kdec"], writes=["r_vd16"])
        ybuf = yb[b % 2]
        ybk = ("r_yb", b % 2)

        def stage1(n):
            cs_ = slice(n * 128, (n + 1) * 128)
            bs, bsk = PS.next()

            def mm_s(e, cs_=cs_, bs=bs):
                for j in range(2):
                    ins = e.matmul(bs[:, 0:128], lhsT=k16[:, j, cs_], rhs=q16[:, j, cs_], start=(j == 0), stop=(j == 1))
                return ins
            P.op("pe", mm_s, reads=["r_k160", "r_k161", "r_q160", "r_q161"], writes=[bsk])
            sm = sm16[n % 2]
            smk = ("r_sm", n % 2)
            P.op("dve", lambda e, sm=sm, bs=bs: e.tensor_tensor(out=sm[:, :], in0=bs[:, 0:128], in1=maskT[:, :], op=ALU.mult),
                 reads=[bsk, "r_maskT"], writes=[smk])
            bkv, bkvk = PS.next()

            def mm_kv(e, bkv=bkv, n=n):
                for j in range(2):
                    ins = e.matmul(bkv[:, j * 256:(j + 1) * 256], lhsT=kt16[:, n, j * 128:(j + 1) * 128], rhs=vd16[:, n, :],
                                   start=True, stop=True)
                return ins
            P.op("pe", mm_kv, reads=["r_kt16a", "r_kt16b", "r_vd16"], writes=[bkvk])
            return sm, smk, bkv, bkvk

        def stage2(n, st):
            sm, smk, bkv, bkvk = st
            cs_ = slice(n * 128, (n + 1) * 128)
            g = b * NCH + n
            Sin, Sink = S16[g % 2], ("r_S16", g % 2)
            Sout, Soutk = S16[(g + 1) % 2], ("r_S16", (g + 1) % 2)
            P.op("dve", lambda e, bkv=bkv: e.scalar_tensor_tensor(out=S32[:, :], in0=S32[:, :], scalar=cst[:, 5:6], in1=bkv[:, :],
                                                                 op0=ALU.mult, op1=ALU.add),
                 reads=["r_S32", "r_cdec", bkvk], writes=["r_S32"])
            P.op("pool", lambda e, Sout=Sout: e.tensor_copy(out=Sout[:, :], in_=S32[:, :]), reads=["r_S32"], writes=[Soutk])
            by, byk = PS.next()

            def mm_y(e, cs_=cs_, by=by, sm=sm, n=n, Sin=Sin):
                e.matmul(by[:, 0:256], lhsT=sm[:, :], rhs=v16[:, n, :], start=True, stop=False)
                for j in range(2):
                    ins = e.matmul(by[:, 0:256], lhsT=qd16[:, j, cs_], rhs=Sin[:, j * 256:(j + 1) * 256], start=False, stop=(j == 1))
                return ins
            P.op("pe", mm_y, reads=[smk, "r_v16", "r_qd160", "r_qd161", Sink], writes=[byk])
            P.op("act", lambda e, by=by, n=n, ybuf=ybuf: e.copy(out=ybuf[:, n, :], in_=by[:, 0:256]), reads=[byk], writes=[ybk + (n,)])
        st = stage1(0)
        for n in range(NCH):
            nxt = stage1(n + 1) if n + 1 < NCH else None
            stage2(n, st)
            st = nxt
            yield
        toks.append(P.dma("sp", y[t0:t0 + RB, :].rearrange("(n p) e -> p n e", p=128), ybuf[:, :, :],
                          reads=[ybk + (n,) for n in range(NCH)]))
    return toks


def _na_geom():
    nrows = SEQ // GRID_W
    kc = np.arange(64)
    q = np.arange(64)
    cstart = np.clip(q - 8, 0, 48)
    inwin = (kc[:, None] >= cstart[None, :]) & (kc[:, None] < cstart[None, :] + 16)
    dc = np.clip(kc[:, None] - q[None, :], -15, 15) + 15
    pats, classes, pairs = {}, [], []
    for pr in range(nrows // 2):
        r0 = 2 * pr
        base = min(min(max(r0 - 4, 0), nrows - 8), nrows - 10)
        pat = []
        for a_ in range(2):
            r = r0 + a_
            rs = min(max(r - 4, 0), nrows - 8)
            for i in range(10):
                kr = base + i
                ok = rs <= kr <= rs + 7
                pat.append((ok, kr - r + 7 if ok else 0))
        pat = tuple(pat)
        if pat not in pats:
            pats[pat] = len(classes)
            dr_t = np.zeros((128, 5, 128), np.int64)
            dc_t = np.zeros((128, 5, 128), np.int64)
            va_t = np.zeros((128, 5, 128), bool)
            for a_ in range(2):
                for i in range(10):
                    ok, drv = pat[a_ * 10 + i]
                    ps_, qs_ = slice((i % 2) * 64, (i % 2) * 64 + 64), slice(a_ * 64, a_ * 64 + 64)
                    dr_t[ps_, i // 2, qs_] = drv
                    dc_t[ps_, i // 2, qs_] = dc
                    va_t[ps_, i // 2, qs_] = inwin & ok
            classes.append((dr_t, dc_t, va_t))
        pairs.append((base, pats[pat]))
    return pairs, classes


def emit_na(P, PS, nc, io):
    qT, kT, v128, biasd, o = io
    pairs, classes = _na_geom()
    ncls = len(classes)
    q16 = P.sb([128, SEQ], BF16, "n_q16")
    k16 = P.sb([128, SEQ], BF16, "n_k16")
    va = P.sb([128, 64, 132], BF16, "n_va")
    bias = P.sb([128, ncls, 640], F32, "n_bias")
    for i in range(4):
        sl = slice(i * 2048, (i + 1) * 2048)
        P.dma("pool", q16[:, sl], qT[:, sl], writes=[("n_q16", i)], max_dma_last_dim=4096)
        P.dma("pool", k16[:, sl], kT[:, sl], writes=[("n_k16", i)], max_dma_last_dim=4096)
    qk_keys = [("n_q16", i) for i in range(4)] + [("n_k16", i) for i in range(4)]
    P.op("pool", lambda e: e.memset(va[:, :, 128:132], 1.0), writes=["n_va1"])
    for i in range(4):
        P.dma("pool", va[:, i * 16:(i + 1) * 16, 0:128], v128[:, i * 16:(i + 1) * 16, :], writes=[("n_va", i)])
    va_keys = ["n_va1"] + [("n_va", i) for i in range(4)]
    P.dma("sp", bias[:, :, :], biasd[:, :, :], writes=["n_bias"])
    st = [P.sb([128, 640], F32, f"n_st{i}") for i in range(2)]
    e16 = [P.sb([128, 640], BF16, f"n_e16_{i}") for i in range(2)]
    rc = [P.sb([128, 1], F32, f"n_rc{i}") for i in range(2)]
    ob = [P.sb([128, 8, 128], F32, f"n_ob{i}") for i in range(2)]
    toks = []
    scale = 128 ** -0.5
    for pr, (base, cls) in enumerate(pairs):
        t0 = (base // 2) * 128
        qs = slice(pr * 128, (pr + 1) * 128)
        bx, bxk = PS.next()
        by, byk = PS.next()

        def mm_s(e, t0=t0, qs=qs, bx=bx):
            for i in range(4):
                ins = e.matmul(bx[:, i * 128:(i + 1) * 128], lhsT=k16[:, t0 + i * 128:t0 + (i + 1) * 128], rhs=q16[:, qs], start=True, stop=True)
            return ins
        P.op("pe", mm_s, reads=qk_keys, writes=[bxk])
        P.op("pe", lambda e, t0=t0, qs=qs, by=by: e.matmul(by[:, 0:128], lhsT=k16[:, t0 + 512:t0 + 640], rhs=q16[:, qs], start=True, stop=True),
             reads=qk_keys, writes=[byk])
        s_, sk = st[pr % 2], ("n_st", pr % 2)
        P.op("dve", lambda e, s_=s_, bx=bx, cls=cls: e.scalar_tensor_tensor(out=s_[:, 0:512], in0=bx[:, :], scalar=scale, in1=bias[:, cls, 0:512],
                                                                            op0=ALU.mult, op1=ALU.add), reads=[bxk, "n_bias"], writes=[sk + ("a",)])
        P.op("dve", lambda e, s_=s_, by=by, cls=cls: e.scalar_tensor_tensor(out=s_[:, 512:640], in0=by[:, 0:128], scalar=scale, in1=bias[:, cls, 512:640],
                                                                            op0=ALU.mult, op1=ALU.add), reads=[byk, "n_bias"], writes=[sk + ("b",)])
        e_, ek = e16[pr % 2], ("n_e16", pr % 2)
        P.op("act", lambda e, s_=s_, e_=e_: e.activation(out=e_[:, :], in_=s_[:, :], func=AF.Exp), reads=[sk + ("a",), sk + ("b",)], writes=[ek])

        def mm_o(e, base=base, by=by, e_=e_):
            for i in range(5):
                ins = e.matmul(by[:, 128:257], lhsT=e_[:, i * 128:(i + 1) * 128], rhs=va[:, base // 2 + i, 0:129], start=(i == 0), stop=(i == 4))
            return ins
        P.op("pe", mm_o, reads=[ek] + va_keys, writes=[byk])
        rc_, rck = rc[pr % 2], ("n_rc", pr % 2)
        P.op("dve", lambda e, rc_=rc_, by=by: e.reciprocal(out=rc_[:, :], in_=by[:, 256:257]), reads=[byk], writes=[rck])
        obuf, obk = ob[(pr // 8) % 2], ("n_ob", (pr // 8) % 2)
        P.op("act", lambda e, obuf=obuf, by=by, rc_=rc_, pr=pr: e.activation(out=obuf[:, pr % 8, :], in_=by[:, 128:256], func=AF.Copy, scale=rc_[:, 0:1]),
             reads=[byk, rck], writes=[obk + (pr % 8,)])
        yield
        if pr % 8 == 7:
            g = pr // 8
            toks.append(P.dma("sp", o[g * 1024:(g + 1) * 1024, :].rearrange("(j p) d -> p j d", p=128), obuf[:, :, :],
                              reads=[obk + (i,) for i in range(8)]))
    return toks


def emit_lru(P, PS, nc, io):
    xpf, xpb, wtap, bconv, wad, wid, gb, lam, hout = io
    tap = P.sb([128, 8], F32, "l_tap")
    bc = P.sb([128, 1], F32, "l_bc")
    gbt = P.sb([128, 4], F32, "l_gb")
    lm = P.sb([128, 8], F32, "l_lm")
    wa16 = P.sb([128, 2, 128], BF16, "l_wa16")
    wi16 = P.sb([128, 2, 128], BF16, "l_wi16")
    P.dma("sp", tap[:, :], wtap[:, :], writes=["l_tap"])
    P.dma("sp", bc[:, :], bconv[:, :], writes=["l_bc"])
    P.dma("sp", gbt[:, :], gb[:, :], writes=["l_gb"])
    P.dma("sp", lm[:, 0:2], lam[:, :], writes=["l_lm0"])
    P.dma("pool", wa16[:, :, :], wad[:, :, :], writes=["l_wa16"])
    P.dma("pool", wi16[:, :, :], wid[:, :, :], writes=["l_wi16"])
    P.op("act", lambda e: e.activation(out=lm[:, 2:4], in_=lm[:, 0:2], func=AF.Exp, scale=-1.0), reads=["l_lm0"], writes=["l_lm1"])
    P.op("act", lambda e: e.activation(out=lm[:, 4:6], in_=lm[:, 2:4], func=AF.Ln, bias=1.0), reads=["l_lm1"], writes=["l_lm2"])
    P.op("act", lambda e: e.mul(out=lm[:, 6:8], in_=lm[:, 4:6], mul=-8.0), reads=["l_lm2"], writes=["l_lm3"])
    xps = [P.sb([128, LB + 3], F32, f"l_xp{i}") for i in range(2)]
    xcs = [P.sb([128, LB], F32, f"l_xc{i}") for i in range(2)]
    xc16s = [P.sb([128, LB], BF16, f"l_xc16_{i}") for i in range(2)]
    rgs = [P.sb([128, LB], F32, f"l_rg{i}") for i in range(2)]
    igs = [P.sb([128, LB], F32, f"l_ig{i}") for i in range(2)]
    hb = [P.sb([128, LB], F32, f"l_h{i}") for i in range(2)]
    toks = []
    it = 0
    NS = LB // 512
    for d in range(2):
        src = xpf if d == 0 else xpb
        for b in range(SEQ // LB):
            t0 = b * LB
            u = it % 2
            xp, xc, xc16, rg, ig = xps[u], xcs[u], xc16s[u], rgs[u], igs[u]
            kxp, kxc, kxc16 = ("l_xp", u), ("l_xc", u), ("l_xc16", u)
            rgk = [("l_rg", u, s) for s in range(NS)]
            igk = [("l_ig", u, s) for s in range(NS)]
            P.dma("sp", xp[:, :], src[:, t0:t0 + LB + 3], writes=[kxp])
            P.op("dve", lambda e, d=d, xc=xc, xp=xp: e.tensor_scalar(out=xc[:, :], in0=xp[:, 0:LB], scalar1=tap[:, 4 * d:4 * d + 1],
                                                                     scalar2=bc[:, 0:1], op0=ALU.mult, op1=ALU.add),
                 reads=[kxp, "l_tap", "l_bc"], writes=[kxc])
            for j in range(1, 4):
                P.op("dve", lambda e, d=d, j=j, xc=xc, xp=xp: e.scalar_tensor_tensor(out=xc[:, :], in0=xp[:, j:j + LB],
                                                                                     scalar=tap[:, 4 * d + j:4 * d + j + 1],
                                                                                     in1=xc[:, :], op0=ALU.mult, op1=ALU.add),
                     reads=[kxp, "l_tap", kxc], writes=[kxc])
            P.op("pool", lambda e, xc=xc, xc16=xc16: e.tensor_copy(out=xc16[:, :], in_=xc[:, :]), reads=[kxc], writes=[kxc16])
            for s_ in range(NS):
                sl = slice(s_ * 512, (s_ + 1) * 512)
                br, brk = PS.next()
                bi_, bik = PS.next()
                P.op("pe", lambda e, d=d, sl=sl, br=br, xc16=xc16: e.matmul(br[:, :], lhsT=wa16[:, d, :], rhs=xc16[:, sl], start=True, stop=True),
                     reads=["l_wa16", kxc16], writes=[brk])
                P.op("pe", lambda e, d=d, sl=sl, bi_=bi_, xc16=xc16: e.matmul(bi_[:, :], lhsT=wi16[:, d, :], rhs=xc16[:, sl], start=True, stop=True),
                     reads=["l_wi16", kxc16], writes=[bik])
                P.op("act", lambda e, d=d, sl=sl, br=br, rg=rg: e.activation(out=rg[:, sl], in_=br[:, :], func=AF.Sigmoid, bias=gbt[:, 2 * d:2 * d + 1]),
                     reads=[brk, "l_gb"], writes=[rgk[s_]])
                P.op("act", lambda e, d=d, sl=sl, bi_=bi_, ig=ig: e.activation(out=ig[:, sl], in_=bi_[:, :], func=AF.Sigmoid,
                                                                               bias=gbt[:, 2 * d + 1:2 * d + 2]),
                     reads=[bik, "l_gb"], writes=[igk[s_]])
                yield
            P.op("act", lambda e, d=d, rg=rg: e.activation(out=rg[:, :], in_=rg[:, :], func=AF.Exp, scale=lm[:, 6 + d:7 + d]),
                 reads=rgk + ["l_lm3"], writes=rgk)
            P.op("dve", lambda e, ig=ig, xc=xc: e.tensor_tensor(out=ig[:, :], in0=ig[:, :], in1=xc[:, :], op=ALU.mult), reads=igk + [kxc], writes=igk)
            P.op("pool", lambda e, xc=xc, rg=rg: e.tensor_tensor(out=xc[:, :], in0=rg[:, :], in1=rg[:, :], op=ALU.mult), reads=rgk + igk, writes=[kxc])
            P.op("act", lambda e, xc=xc: e.activation(out=xc[:, :], in_=xc[:, :], func=AF.Sqrt, scale=-1.0, bias=1.0), reads=[kxc], writes=[kxc])
            P.op("dve", lambda e, ig=ig, xc=xc: e.tensor_tensor(out=ig[:, :], in0=ig[:, :], in1=xc[:, :], op=ALU.mult), reads=igk + [kxc], writes=igk)
            h, hk = hb[it % 2], ("l_h", it % 2)
            hprev, hpk = hb[(it + 1) % 2], ("l_h", (it + 1) % 2)
            if b == 0:
                P.op("dve", lambda e, h=h, rg=rg, ig=ig: e.tensor_tensor_scan(out=h[:, :], data0=rg[:, :], data1=ig[:, :], initial=0.0,
                                                                              op0=ALU.mult, op1=ALU.add), reads=rgk + igk, writes=[hk])
            else:
                P.op("dve", lambda e, h=h, hprev=hprev, rg=rg, ig=ig: e.tensor_tensor_scan(out=h[:, :], data0=rg[:, :], data1=ig[:, :],
                                                                                           initial=hprev[:, LB - 1:LB], op0=ALU.mult, op1=ALU.add),
                     reads=rgk + igk + [hpk], writes=[hk])
            toks.append(P.dma("sp", hout[d, :, t0:t0 + LB], h[:, :], reads=[hk]))
            it += 1
            yield
    return toks


def build_B(parts=("ret", "na", "lru")):
    nc = bass.Bass("TRN2", target_bir_lowering=False)
    di = lambda n, s: nc.dram_tensor(n, s, F32, kind="ExternalInput").ap()
    do = lambda n, s: nc.dram_tensor(n, s, F32, kind="ExternalOutput").ap()
    P = Prog(nc)
    PS = PsumRing(P)
    toks = []
    gens = []
    if "ret" in parts:
        io = (di("r_qT", [256, SEQ]), di("r_kT", [256, SEQ]), di("r_ktm", [SEQ, 256]), di("r_v", [SEQ, 256]),
              di("r_cosT", [128, SEQ]), di("r_sinT", [128, SEQ]), di("r_costm", [SEQ, 128]), di("r_sintm", [SEQ, 128]),
              di("r_logit", [128, 1]), di("r_diffT", [128, 128]), di("r_keepT", [128, 128]), di("r_idx1", [128, 128]),
              di("r_kidx", [128, 1]), do("r_y", [SEQ, 256]))
        gens.append(emit_retention(P, PS.sub(range(0, 4)), nc, io))
    if "na" in parts:
        io = (di("n_qT", [128, SEQ]), di("n_kT", [128, SEQ]), di("n_v128", [128, 64, 128]), di("n_bias", [128, len(_na_geom()[1]), 640]),
              do("n_o", [SEQ, 128]))
        gens.append(emit_na(P, PS.sub(range(4, 6)), nc, io))
    if "lru" in parts:
        io = (di("l_xpf", [128, SEQ + 3]), di("l_xpb", [128, SEQ + 3]), di("l_wtap", [128, 8]), di("l_bconv", [128, 1]),
              di("l_wa", [128, 2, 128]), di("l_wi", [128, 2, 128]), di("l_gb", [128, 4]), di("l_lam", [128, 2]),
              do("l_h", [2, 128, SEQ]))
        gens.append(emit_lru(P, PS.sub(range(6, 8)), nc, io))
    while gens:
        for g in list(gens):
            try:
                next(g)
            except StopIteration as stop:
                toks += stop.value
                gens.remove(g)
    P.emit(toks)
    return nc


def _rot_tables():
    half = 128
    inv = (10000.0 ** (-np.arange(half, dtype=np.float32) / np.float32(half))).astype(np.float32)
    pos = np.arange(SEQ, dtype=np.float32)
    ang = (pos[:, None] * inv[None, :]).astype(np.float32)
    return np.cos(ang).astype(np.float32), np.sin(ang).astype(np.float32)


_CONST = {}


def _consts():
    if _CONST:
        return _CONST
    cos, sin = _rot_tables()
    _CONST["cos_tm"] = [cos, np.ascontiguousarray(cos[::-1])]
    _CONST["sin_tm"] = [sin, np.ascontiguousarray(sin[::-1])]
    _CONST["cos_fm"] = [np.ascontiguousarray(c.T) for c in _CONST["cos_tm"]]
    _CONST["sin_fm"] = [np.ascontiguousarray(c.T) for c in _CONST["sin_tm"]]
    idx = np.arange(128, dtype=np.float32)
    diff = idx[None, :] - idx[:, None]
    keep = [(diff >= 0), (diff > 0)]
    _CONST["diffT"] = [np.where(k, diff, 0.0).astype(np.float32) for k in keep]
    _CONST["keepT"] = [k.astype(np.float32) for k in keep]
    _CONST["idx1"] = np.ascontiguousarray(np.broadcast_to((idx + 1.0)[None, :], (128, 128))).astype(np.float32)
    _CONST["kidx"] = (127.0 - idx)[:, None].astype(np.float32)
    kc = np.arange(64)[:, None, None, None]
    cl = np.arange(8)[None, :, None, None]
    ki = np.arange(8)[None, None, :, None]
    q = np.arange(64)[None, None, None, :]
    cstart = np.clip(q - 8, 0, 48)
    inwin = (kc >= cstart) & (kc < cstart + 16)
    dr = ki - cl + 7 + 0 * kc + 0 * q
    dc = np.clip(kc - q, -15, 15) + 15 + 0 * cl + 0 * ki
    _CONST["na_classes"] = _na_geom()[1]
    _CONST["na_dr"] = np.broadcast_to(dr, (64, 8, 8, 64)).copy()
    _CONST["na_dc"] = np.broadcast_to(dc, (64, 8, 8, 64)).copy()
    _CONST["na_win"] = np.broadcast_to(inwin, (64, 8, 8, 64)).copy()
    return _CONST


def prep_B(projT, l, inp):
    C = _consts()
    ims = []
    for c in range(NCORES):
        hh, dd = c // 2, c % 2
        fl = (lambda a: a[:, ::-1]) if dd else (lambda a: a)
        m = {}
        qT = fl(projT[hh * 256:(hh + 1) * 256])
        kT = fl(projT[1024 + hh * 256:1024 + (hh + 1) * 256])
        vT = fl(projT[2048 + hh * 256:2048 + (hh + 1) * 256])
        m["r_qT"] = np.ascontiguousarray(qT)
        m["r_kT"] = np.ascontiguousarray(kT)
        m["r_ktm"] = np.ascontiguousarray(kT.T)
        m["r_v"] = np.ascontiguousarray(vT.T)
        m["r_cosT"], m["r_sinT"] = C["cos_fm"][dd], C["sin_fm"][dd]
        m["r_costm"], m["r_sintm"] = C["cos_tm"][dd], C["sin_tm"][dd]
        m["r_logit"] = np.full((128, 1), inp["ret_decay"][l, dd, hh], np.float32)
        m["r_diffT"], m["r_keepT"] = C["diffT"][dd], C["keepT"][dd]
        m["r_idx1"], m["r_kidx"] = C["idx1"], C["kidx"]
        m["n_qT"] = np.ascontiguousarray(projT[4096 + c * 128:4096 + (c + 1) * 128])
        m["n_kT"] = np.ascontiguousarray(projT[5120 + c * 128:5120 + (c + 1) * 128])
        vh = projT[6144 + c * 128:6144 + (c + 1) * 128]
        m["n_v128"] = np.ascontiguousarray(vh.T.reshape(64, 128, 128).transpose(1, 0, 2))
        rpb = inp["na_rpb"][l, c]
        tabs = []
        for dr_t, dc_t, va_t in C["na_classes"]:
            tabs.append(np.where(va_t, rpb[dr_t, dc_t], np.float32(NEG)).astype(np.float32).reshape(128, 640))
        m["n_bias"] = np.ascontiguousarray(np.stack(tabs, axis=1))
        x = projT[7168 + c * 128:7168 + (c + 1) * 128]
        xpf = np.zeros((128, SEQ + 3), np.float32)
        xpf[:, 2:2 + SEQ] = x
        xpb = np.zeros((128, SEQ + 3), np.float32)
        xpb[:, 1:1 + SEQ] = x[:, ::-1]
        m["l_xpf"], m["l_xpb"] = xpf, xpb
        wc = inp["w_conv"][l][:, c * 128:(c + 1) * 128]
        m["l_wtap"] = np.ascontiguousarray(np.concatenate([wc.T, wc[::-1].T], axis=1))
        m["l_bconv"] = np.ascontiguousarray(inp["b_conv"][l][c * 128:(c + 1) * 128, None])
        m["l_wa"] = np.ascontiguousarray(inp["lru_wa"][l][:, c].transpose(1, 0, 2))
        m["l_wi"] = np.ascontiguousarray(inp["lru_wi"][l][:, c].transpose(1, 0, 2))
        sl = slice(c * 128, (c + 1) * 128)
        m["l_gb"] = np.ascontiguousarray(np.stack([inp["lru_ba"][l][0, sl], inp["lru_bi"][l][0, sl],
                                                   inp["lru_ba"][l][1, sl], inp["lru_bi"][l][1, sl]], axis=1))
        m["l_lam"] = np.ascontiguousarray(inp["lru_lambda"][l][:, sl].T)
        ims.append(m)
    return ims


def post_B(results):
    yf = np.concatenate([results[2 * h]["r_y"] for h in range(4)], axis=1)
    yb = np.concatenate([results[2 * h + 1]["r_y"][::-1] for h in range(4)], axis=1)
    na = np.concatenate([results[c]["n_o"] for c in range(8)], axis=1)
    hf = np.concatenate([results[c]["l_h"][0] for c in range(8)], axis=0)
    hb = np.concatenate([results[c]["l_h"][1][:, ::-1] for c in range(8)], axis=0)
    return yf, yb, na, hf, hb


def fm_stats(P, PS, x, xk, KC, T, ones, scr, sk):
    nfeat = KC * 128
    P.op("act", lambda e: e.activation(out=scr[:, 0:KC, 0:T], in_=x[:, 0:KC, 0:T], func=AF.Square), reads=xk, writes=sk)
    b_sum, k_sum = PS.next()
    b_sq, k_sq = PS.next()

    def mm_sum(e):
        for k in range(KC):
            ins = e.matmul(b_sum[:, 0:T], lhsT=ones[:, :], rhs=x[:, k, 0:T], start=(k == 0), stop=(k == KC - 1))
        return ins

    def mm_sq(e):
        for k in range(KC):
            ins = e.matmul(b_sq[:, 0:T], lhsT=ones[:, :], rhs=scr[:, k, 0:T], start=(k == 0), stop=(k == KC - 1))
        return ins
    P.op("pe", mm_sum, reads=xk + ["ones"], writes=[k_sum])
    P.op("pe", mm_sq, reads=sk + ["ones"], writes=[k_sq])
    if not hasattr(P, "_ln_tmp"):
        P._ln_tmp = (P.sb([128, 512], F32, "ln_mean"), P.sb([128, 512], F32, "ln_rstd"))
    mean, rstd = P._ln_tmp
    mk, rk = "ln_mean", "ln_rstd"
    P.op("act", lambda e: e.mul(out=mean[:, 0:T], in_=b_sum[:, 0:T], mul=1.0 / nfeat), reads=[k_sum], writes=[mk])
    P.op("dve", lambda e: e.tensor_tensor(out=rstd[:, 0:T], in0=mean[:, 0:T], in1=mean[:, 0:T], op=ALU.mult), reads=[mk], writes=[rk])
    P.op("dve", lambda e: e.scalar_tensor_tensor(out=rstd[:, 0:T], in0=b_sq[:, 0:T], scalar=1.0 / nfeat, in1=rstd[:, 0:T],
                                                 op0=ALU.mult, op1=ALU.subtract), reads=[k_sq, rk], writes=[rk])
    P.op("dve", lambda e: e.tensor_scalar(out=rstd[:, 0:T], in0=rstd[:, 0:T], scalar1=EPS, scalar2=None, op0=ALU.add),
         reads=[rk], writes=[rk])
    P.op("act", lambda e: e.activation(out=rstd[:, 0:T], in_=rstd[:, 0:T], func=AF.Sqrt), reads=[rk], writes=[rk])
    P.op("dve", lambda e: e.reciprocal(out=rstd[:, 0:T], in_=rstd[:, 0:T]), reads=[rk], writes=[rk])
    return mean, rstd, mk, rk


def build_C(with_next_proj):
    T = 512
    nc = bass.Bass("TRN2", target_bir_lowering=False)
    di = lambda n, s: nc.dram_tensor(n, s, F32, kind="ExternalInput").ap()
    do = lambda n, s: nc.dram_tensor(n, s, F32, kind="ExternalOutput").ap()
    c_yf, c_yb, c_g = di("c_yf", [1024, TPC]), di("c_yb", [1024, TPC]), di("c_g", [1024, TPC])
    c_na, c_hf, c_hb, c_ly = di("c_na", [1024, TPC]), di("c_hf", [1024, TPC]), di("c_hb", [1024, TPC]), di("c_ly", [1024, TPC])
    c_gp, c_gb, c_h = di("c_gp", [6144, TPC]), di("c_gb", [128, 48]), di("c_h", [D, TPC])
    w_br, w_out = di("w_br", [3072, D]), di("w_out", [D, D])
    ln1g, ln1b, ln2g, ln2b = di("ln1g", [128, 16]), di("ln1b", [128, 16]), di("ln2g", [128, 16]), di("ln2b", [128, 16])
    w_f1, w_f2 = di("w_f1", [D, 2 * DFF]), di("w_f2", [DFF, D])
    h2T = do("h2T", [D, TPC])
    h2s = nc.dram_tensor("h2s", [D, TPC], F32).ap()
    if with_next_proj:
        w_in = di("w_in", [D, IN_COLS])
        projT = do("projT", [IN_COLS, TPC])
    P = Prog(nc)
    PS = PsumRing(P)
    ones = P.sb([128, 128], F32, "ones")
    P.op("pool", lambda e: e.memset(ones[:, :], 1.0), writes=["ones"])
    g1, b1 = load_consts(P, ln1g, "g1", 16), load_consts(P, ln1b, "b1", 16)
    g2, b2 = load_consts(P, ln2g, "g2", 16), load_consts(P, ln2b, "b2", 16)
    gbt = load_consts(P, c_gb, "gbt", 48)
    A = P.sb([128, 16, T], F32, "arenaA")
    B = P.sb([128, 16, T], F32, "arenaB")
    U = P.sb([128, 44 * T], BF16, "arenaU")
    U3 = U[:, :].rearrange("p (k t) -> p k t", t=T)
    h1_16 = P.sb([128, 16, T], BF16, "h1_16")
    wflat = [P.sb([128, 5632], BF16, f"wf{i}") for i in range(4)]
    wv = lambda kc, wc: [w[:, 0:kc * wc].rearrange("p (k c) -> p k c", c=wc) for w in wflat]
    sc = [P.sb([128, T], F32, f"sc{i}") for i in range(8)]
    sck = [("sc", i) for i in range(8)]
    gpt = [P.sb([128, T], F32, f"gpt{i}") for i in range(2)]
    Ak = [("A", k) for k in range(16)]
    Bk = [("B", k) for k in range(16)]
    Uk = [("U", k) for k in range(44)]
    toks = []
    for half in range(2):
        ts = slice(half * T, (half + 1) * T)
        for hh in range(4):
            rows = slice(hh * 256, (hh + 1) * 256)
            o8 = 8 * (hh % 2)
            y, yk = A[:, o8:o8 + 2, :], Ak[o8:o8 + 2]
            y2, y2k = A[:, o8 + 2:o8 + 4, :], Ak[o8 + 2:o8 + 4]
            gg, ggk = A[:, o8 + 4:o8 + 6, :], Ak[o8 + 4:o8 + 6]
            sq, sqk = A[:, o8 + 6:o8 + 8, :], Ak[o8 + 6:o8 + 8]
            P.dma("sp", y, c_yf[rows, ts].rearrange("(k p) t -> p k t", p=128), writes=yk)
            P.dma("act", y2, c_yb[rows, ts].rearrange("(k p) t -> p k t", p=128), writes=y2k)
            P.dma("sp", gg, c_g[rows, ts].rearrange("(k p) t -> p k t", p=128), writes=ggk)
            P.op("dve", lambda e, y=y, y2=y2: e.tensor_tensor(out=y, in0=y, in1=y2, op=ALU.add), reads=yk + y2k, writes=yk)
            mean, rstd, mk, rk = fm_stats(P, PS, y, yk, 2, T, ones, sq, sqk)
            P.op("act", lambda e, gg=gg: e.activation(out=gg, in_=gg, func=AF.Silu), reads=ggk, writes=ggk)
            for j in range(2):
                P.op("dve", lambda e, j=j, o8=o8: e.tensor_tensor(out=A[:, o8 + j, :], in0=A[:, o8 + j, :], in1=mean[:, 0:T], op=ALU.subtract),
                     reads=[Ak[o8 + j], mk], writes=[Ak[o8 + j]])
                P.op("dve", lambda e, j=j, o8=o8: e.tensor_tensor(out=A[:, o8 + j, :], in0=A[:, o8 + j, :], in1=rstd[:, 0:T], op=ALU.mult),
                     reads=[Ak[o8 + j], rk], writes=[Ak[o8 + j]])
                P.op("dve", lambda e, j=j, hh=hh, o8=o8: e.tensor_tensor(out=U3[:, 2 * hh + j, :], in0=A[:, o8 + j, :], in1=A[:, o8 + 4 + j, :], op=ALU.mult),
                     reads=[Ak[o8 + j], Ak[o8 + 4 + j]], writes=[Uk[2 * hh + j]])
        P.dma("pool", U3[:, 8:16, :], c_na[:, ts].rearrange("(k p) t -> p k t", p=128), writes=Uk[8:16])
        for k in range(8):
            rows = slice(k * 128, (k + 1) * 128)
            o4 = 4 * (k % 2)
            hf_, hb_, ly_, t_ = sc[o4], sc[o4 + 1], sc[o4 + 2], sc[o4 + 3]
            k0, k1, k2, k3 = sck[o4], sck[o4 + 1], sck[o4 + 2], sck[o4 + 3]
            P.dma("sp", hf_[:, :], c_hf[rows, ts], writes=[k0])
            P.dma("act", hb_[:, :], c_hb[rows, ts], writes=[k1])
            P.dma("sp", ly_[:, :], c_ly[rows, ts], writes=[k2])
            P.op("dve", lambda e, hf_=hf_, hb_=hb_: e.tensor_tensor(out=hf_[:, :], in0=hf_[:, :], in1=hb_[:, :], op=ALU.add), reads=[k0, k1], writes=[k0])
            P.op("dve", lambda e, t_=t_, ly_=ly_: e.tensor_tensor(out=t_[:, :], in0=ly_[:, :], in1=ly_[:, :], op=ALU.mult), reads=[k2], writes=[k3])
            P.op("dve", lambda e, t_=t_: e.tensor_scalar(out=t_[:, :], in0=t_[:, :], scalar1=0.044715, scalar2=1.0, op0=ALU.mult, op1=ALU.add),
                 reads=[k3], writes=[k3])
            P.op("dve", lambda e, t_=t_, ly_=ly_: e.tensor_tensor(out=t_[:, :], in0=t_[:, :], in1=ly_[:, :], op=ALU.mult), reads=[k3, k2], writes=[k3])
            P.op("act", lambda e, t_=t_: e.activation(out=t_[:, :], in_=t_[:, :], func=AF.Sigmoid, scale=1.5957691216057308), reads=[k3], writes=[k3])
            P.op("dve", lambda e, t_=t_, ly_=ly_: e.tensor_tensor(out=t_[:, :], in0=t_[:, :], in1=ly_[:, :], op=ALU.mult), reads=[k3, k2], writes=[k3])
            P.op("dve", lambda e, k=k, t_=t_, hf_=hf_: e.tensor_tensor(out=U3[:, 16 + k, :], in0=t_[:, :], in1=hf_[:, :], op=ALU.mult),
                 reads=[k3, k0], writes=[Uk[16 + k]])
        for b in range(3):
            def evac(ci, hf, bank, bk, b=b):
                gp_, gpk = gpt[ci % 2], ("gpt", ci % 2)
                rows = slice(b * 2048 + ci * 128, b * 2048 + (ci + 1) * 128)
                P.dma("sp", gp_[:, :], c_gp[rows, ts], writes=[gpk])
                P.op("act", lambda e: e.activation(out=gp_[:, :], in_=gp_[:, :], func=AF.Sigmoid, bias=gbt[:, b * 16 + ci:b * 16 + ci + 1]),
                     reads=[gpk, "gbt"], writes=[gpk])
                if b == 0:
                    P.op("dve", lambda e: e.tensor_tensor(out=A[:, ci, :], in0=bank[:, 0:T], in1=gp_[:, :], op=ALU.mult),
                         reads=[bk, gpk], writes=[Ak[ci]])
                else:
                    P.op("dve", lambda e: e.tensor_tensor(out=gp_[:, :], in0=bank[:, 0:T], in1=gp_[:, :], op=ALU.mult),
                         reads=[bk, gpk], writes=[gpk])
                    if b == 1:
                        P.op("dve", lambda e: e.tensor_tensor(out=A[:, ci, :], in0=A[:, ci, :], in1=gp_[:, :], op=ALU.add),
                             reads=[Ak[ci], gpk], writes=[Ak[ci]])
                    else:
                        P.op("dve", lambda e: e.tensor_tensor(out=U3[:, 24 + ci, :], in0=A[:, ci, :], in1=gp_[:, :], op=ALU.add),
                             reads=[Ak[ci], gpk], writes=[Uk[24 + ci]])
            stream_matmul_fm(P, PS, w_br[b * 1024:(b + 1) * 1024, :], 0, D, 8, U3[:, 8 * b:8 * b + 8, :], Uk[8 * b:8 * b + 8], T, evac,
                             wv(8, 512), "W", WC=512)
        P.dma("sp", B[:, :, :], c_h[:, ts].rearrange("(k p) t -> p k t", p=128), writes=Bk)

        def evac_o(ci, hf, bank, bk):
            P.op("dve", lambda e: e.scalar_tensor_tensor(out=B[:, ci, :], in0=B[:, ci, :], scalar=ALPHA, in1=bank[:, 0:T],
                                                         op0=ALU.mult, op1=ALU.add), reads=[bk, Bk[ci]], writes=[Bk[ci]])
        stream_matmul_fm(P, PS, w_out, 0, D, 16, U3[:, 24:40, :], Uk[24:40], T, evac_o, wv(16, 256), "W", WC=256)
        layernorm_fm(P, PS, B, ("B",), T, g1, b1, ["g1", "b1"], ones, B, ("B",), h1_16, ("h1_16",), A, ("A",))
        h1k = [("h1_16", k) for k in range(16)]

        def evac_f(ci, hf, bank, bk):
            if ci < 44:
                P.op("act", lambda e: e.activation(out=U3[:, ci, :], in_=bank[:, 0:T], func=AF.Silu), reads=[bk], writes=[Uk[ci]])
            else:
                f = ci - 44
                P.op("dve", lambda e: e.tensor_tensor(out=U3[:, f, :], in0=bank[:, 0:T], in1=U3[:, f, :], op=ALU.mult),
                     reads=[bk, Uk[f]], writes=[Uk[f]])
        stream_matmul_fm(P, PS, w_f1, 0, 2 * DFF, 16, h1_16, h1k, T, evac_f, wv(16, 256), "W", WC=256)

        def evac_2(ci, hf, bank, bk):
            P.op("dve", lambda e: e.scalar_tensor_tensor(out=B[:, ci, :], in0=B[:, ci, :], scalar=ALPHA, in1=bank[:, 0:T],
                                                         op0=ALU.mult, op1=ALU.add), reads=[bk, Bk[ci]], writes=[Bk[ci]])
        stream_matmul_fm(P, PS, w_f2, 0, D, 44, U3, Uk, T, evac_2, wv(22, 256), "W", WC=256, KT=2)
        layernorm_fm(P, PS, B, ("B",), T, g2, b2, ["g2", "b2"], ones, B, ("B",), h1_16, ("h1_16",), A, ("A",))
        toks.append(P.dma("sp", h2T[:, ts].rearrange("(k p) t -> p k t", p=128), B[:, :, :], reads=Bk))
        if with_next_proj:
            P.dma("act", h2s[:, ts].rearrange("(k p) t -> p k t", p=128), B[:, :, :], reads=Bk, writes=[("h2s", half)])
    if with_next_proj:
        h16 = U[:, 0:16 * TPC].rearrange("p (k t) -> p k t", t=TPC)
        for k in range(16):
            P.dma("pool", h16[:, k, :], h2s[k * 128:(k + 1) * 128, :], reads=[("h2s", 0), ("h2s", 1)], writes=Uk[2 * k:2 * k + 2])
        obufs = [P.sb([128, 512], F32, f"ob{i}") for i in range(2)]
        toks += emit_inproj(P, PS, h16, Uk[0:32], w_in, projT, wv(16, 256), obufs, wname="W")
    P.emit(toks)
    return nc


def prep_C(l, inp, projT, hT, yf, yb, na, hf, hb, with_next):
    yfT, ybT, naT = np.ascontiguousarray(yf.T), np.ascontiguousarray(yb.T), np.ascontiguousarray(na.T)
    fm16 = lambda v: np.ascontiguousarray(v.reshape(16, 128).T)
    shared = {
        "c_gb": np.ascontiguousarray(inp["gate_b"][l].reshape(48, 128).T),
        "w_br": np.ascontiguousarray(inp["w_branch"][l].reshape(3072, D)), "w_out": inp["w_out"][l],
        "ln1g": fm16(inp["ln1_g"][l]), "ln1b": fm16(inp["ln1_b"][l]), "ln2g": fm16(inp["ln2_g"][l]), "ln2b": fm16(inp["ln2_b"][l]),
        "w_f1": inp["w_ffn_in"][l], "w_f2": inp["w_ffn_out"][l],
    }
    if with_next:
        shared["w_in"] = inp["w_in"][l + 1]
    ims = []
    for c in range(NCORES):
        ts = slice(c * TPC, (c + 1) * TPC)
        cc = lambda a: np.ascontiguousarray(a[:, ts])
        m = dict(shared)
        m.update({"c_yf": cc(yfT), "c_yb": cc(ybT), "c_g": cc(projT[3072:4096]), "c_na": cc(naT), "c_hf": cc(hf), "c_hb": cc(hb),
                  "c_ly": cc(projT[8192:9216]), "c_gp": cc(projT[9216:15360]), "c_h": cc(hT)})
        ims.append(m)
    return ims


def _run(nc, ims):
    return run_bass_kernel_spmd(nc, ims, core_ids=list(range(NCORES))).results


def kernel(**inputs):
    inp = {k: np.asarray(v) for k, v in inputs.items()}
    x = inp["x"][0]
    fm16 = lambda v: np.ascontiguousarray(v.reshape(16, 128).T)
    ims = [{"xT": np.ascontiguousarray(x[c * TPC:(c + 1) * TPC].T), "lng": fm16(inp["ln_in_g"]), "lnb": fm16(inp["ln_in_b"]),
            "w_in": inp["w_in"][0]} for c in range(NCORES)]
    res = _run(build_A0(), ims)
    hT = np.concatenate([r["hT"] for r in res], axis=1)
    projT = np.concatenate([r["projT"] for r in res], axis=1)
    for l in range(DEPTH):
        resB = _run(build_B(), prep_B(projT, l, inp))
        yf, yb, na, hf, hb = post_B(resB)
        del resB
        nxt = l + 1 < DEPTH
        res = _run(build_C(nxt), prep_C(l, inp, projT, hT, yf, yb, na, hf, hb, nxt))
        del yf, yb, na, hf, hb
        hT = np.concatenate([r["h2T"] for r in res], axis=1)
        if nxt:
            projT = np.concatenate([r["projT"] for r in res], axis=1)
        del res
    return np.ascontiguousarray(hT.T)[None].astype(np.float32)
```

```python
import contextlib
import numpy as np
import concourse.bass as bass
import concourse.mybir as mybir
from concourse.bass_utils import run_bass_kernel_spmd

F32 = mybir.dt.float32
BF16 = mybir.dt.bfloat16
AF = mybir.ActivationFunctionType
ALU = mybir.AluOpType
AX = mybir.AxisListType

NCORES = 8
D = 2048
SEQ = 8192
TPC = SEQ // NCORES
DEPTH = 4
IN_COLS = 15360
DFF = 5632
ALPHA = (2 * DEPTH) ** 0.25
EPS = 1e-5
GRID_W = 64


class Prog:
    ENGS = ("pe", "act", "dve", "pool", "sp")
    DMA_RING = 16

    def __init__(self, nc):
        self.nc = nc
        self.stack = contextlib.ExitStack()
        self.ops = {e: [] for e in self.ENGS}
        self.cnt = {e: 0 for e in self.ENGS}
        self.dcnt = {e: 0 for e in self.ENGS}
        self.known = {e: {} for e in self.ENGS}
        self.last_w = {}
        self.readers = {}
        self.sem = {e: self.stack.enter_context(nc.semaphore("s_" + e)) for e in self.ENGS}
        self.dsem = {}
        for q in ("sp", "pool", "act"):
            self.dsem[q] = [self.stack.enter_context(nc.semaphore(f"d_{q}{i}")) for i in range(self.DMA_RING)]
        self.out_tokens = []
        self._n = 0

    def sb(self, shape, dtype, name=None):
        self._n += 1
        return self.stack.enter_context(self.nc.sbuf_tensor("sb_" + (name or f"t{self._n}"), list(shape), dtype))

    def ps(self, name=None):
        self._n += 1
        return self.stack.enter_context(self.nc.psum_tensor(name or f"p{self._n}", [128, 512], F32))

    def _tok_sem(self, tok):
        if tok[0] == "e":
            return ("e", tok[1]), tok[2]
        return ("d", tok[1], tok[2] % self.DMA_RING), 16 * (tok[2] // self.DMA_RING + 1)

    def _waits(self, eng, reads, writes):
        toks = set()
        for r in reads:
            if r in self.last_w:
                toks.add(self.last_w[r])
        for w in writes:
            if w in self.last_w:
                toks.add(self.last_w[w])
            for t in self.readers.get(w, ()):
                toks.add(t)
        need = {}
        for t in toks:
            key, val = self._tok_sem(t)
            if self.known[eng].get(key, 0) >= val:
                continue
            need[key] = max(need.get(key, 0), val)
        for key, val in need.items():
            self.known[eng][key] = val
        return list(need.items())

    def _commit(self, tok, reads, writes):
        for r in reads:
            self.readers.setdefault(r, []).append(tok)
        for w in writes:
            self.last_w[w] = tok
            self.readers[w] = []

    def op(self, eng, fn, reads=(), writes=()):
        waits = self._waits(eng, reads, writes)
        self.cnt[eng] += 1
        tok = ("e", eng, self.cnt[eng])
        self.ops[eng].append(("op", fn, waits))
        self._commit(tok, reads, writes)
        return tok

    def dma(self, q, out, in_, reads=(), writes=(), **kw):
        idx = self.dcnt[q]
        waits = self._waits(q, reads, writes)
        if idx >= self.DMA_RING:
            key, val = self._tok_sem(("d", q, idx - self.DMA_RING))
            if self.known[q].get(key, 0) < val:
                self.known[q][key] = val
                waits.append((key, val))
        self.dcnt[q] += 1
        tok = ("d", q, idx)
        self.ops[q].append(("dma", (out, in_, kw, idx), waits))
        self._commit(tok, reads, writes)
        return tok

    def _semh(self, key):
        return self.sem[key[1]] if key[0] == "e" else self.dsem[key[1]][key[2]]

    def emit(self, final_tokens):
        nc = self.nc
        emap = {"pe": "tensor", "act": "scalar", "dve": "vector", "pool": "gpsimd", "sp": "sync"}
        fin = {}
        for t in final_tokens:
            key, val = self._tok_sem(t)
            fin[key] = max(fin.get(key, 0), val)
        with nc.Block() as block:
            for e in self.ENGS:
                def body(eng, e=e):
                    for kind, payload, waits in self.ops[e]:
                        for key, val in waits:
                            eng.wait_ge(self._semh(key), val)
                        if kind == "op":
                            ins = payload(eng)
                            ins.then_inc(self.sem[e], 1)
                        else:
                            out, in_, kw, idx = payload
                            eng.dma_start(out=out, in_=in_, **kw).then_inc(self.dsem[e][idx % self.DMA_RING], 16)
                    if e == "sp":
                        for key, val in fin.items():
                            eng.wait_ge(self._semh(key), val)
                getattr(block, emap[e])(body)
        self.stack.close()


class PsumRing:
    def __init__(self, P, n=8):
        self.P = P
        self.banks = [P.ps(f"bank{i}") for i in range(n)]
        self.i = 0

    def next(self):
        b = self.banks[self.i % len(self.banks)]
        k = ("psum", self.i % len(self.banks))
        self.i += 1
        return b, k

    def sub(self, idxs):
        return _SubRing(self, list(idxs))


class _SubRing:
    def __init__(self, parent, idxs):
        self.parent, self.idxs, self.i = parent, idxs, 0

    def next(self):
        j = self.idxs[self.i % len(self.idxs)]
        self.i += 1
        return self.parent.banks[j], ("psum", j)


def load_consts(P, nc_in, name, shape_free, q="sp"):
    t = P.sb([128, shape_free], F32, name)
    P.dma(q, t[:, :], nc_in[:, :], writes=[name])
    return t


def layernorm_fm(P, PS, x, xkey, T, gam, bet, gkeys, ones, out32, out32key, out16, out16key, scr, scrkey, KC=16):
    nfeat = KC * 128
    xk = [xkey + (k,) for k in range(KC)]
    sk = [scrkey + (k,) for k in range(KC)]
    P.op("act", lambda e: e.activation(out=scr[:, 0:KC, 0:T], in_=x[:, 0:KC, 0:T], func=AF.Square),
         reads=xk, writes=sk)
    b_sum, k_sum = PS.next()
    b_sq, k_sq = PS.next()

    def mm_sum(e):
        for k in range(KC):
            ins = e.matmul(b_sum[:, 0:T], lhsT=ones[:, :], rhs=x[:, k, 0:T], start=(k == 0), stop=(k == KC - 1))
        return ins

    def mm_sq(e):
        for k in range(KC):
            ins = e.matmul(b_sq[:, 0:T], lhsT=ones[:, :], rhs=scr[:, k, 0:T], start=(k == 0), stop=(k == KC - 1))
        return ins
    P.op("pe", mm_sum, reads=xk + ["ones"], writes=[k_sum])
    P.op("pe", mm_sq, reads=sk + ["ones"], writes=[k_sq])
    if not hasattr(P, "_ln_tmp"):
        P._ln_tmp = (P.sb([128, 512], F32, "ln_mean"), P.sb([128, 512], F32, "ln_rstd"))
    mean, rstd = P._ln_tmp
    mk, rk = "ln_mean", "ln_rstd"
    P.op("act", lambda e: e.mul(out=mean[:, 0:T], in_=b_sum[:, 0:T], mul=1.0 / nfeat), reads=[k_sum], writes=[mk])
    P.op("dve", lambda e: e.tensor_tensor(out=rstd[:, 0:T], in0=mean[:, 0:T], in1=mean[:, 0:T], op=ALU.mult),
         reads=[mk], writes=[rk])
    P.op("dve", lambda e: e.scalar_tensor_tensor(out=rstd[:, 0:T], in0=b_sq[:, 0:T], scalar=1.0 / nfeat, in1=rstd[:, 0:T],
                                                 op0=ALU.mult, op1=ALU.subtract), reads=[k_sq, rk], writes=[rk])
    P.op("dve", lambda e: e.tensor_scalar(out=rstd[:, 0:T], in0=rstd[:, 0:T], scalar1=EPS, scalar2=None,
                                          op0=ALU.add), reads=[rk], writes=[rk])
    P.op("act", lambda e: e.activation(out=rstd[:, 0:T], in_=rstd[:, 0:T], func=AF.Sqrt), reads=[rk], writes=[rk])
    P.op("dve", lambda e: e.reciprocal(out=rstd[:, 0:T], in_=rstd[:, 0:T]), reads=[rk], writes=[rk])
    for k in range(KC):
        P.op("dve", lambda e, k=k: e.tensor_tensor(out=scr[:, k, 0:T], in0=x[:, k, 0:T], in1=mean[:, 0:T], op=ALU.subtract),
             reads=[xk[k], mk], writes=[sk[k]])
        P.op("dve", lambda e, k=k: e.tensor_tensor(out=scr[:, k, 0:T], in0=scr[:, k, 0:T], in1=rstd[:, 0:T], op=ALU.mult),
             reads=[sk[k], rk], writes=[sk[k]])
        P.op("act", lambda e, k=k: e.activation(out=out32[:, k, 0:T], in_=scr[:, k, 0:T], func=AF.Identity,
                                                scale=gam[:, k:k + 1], bias=bet[:, k:k + 1]),
             reads=[sk[k]] + list(gkeys), writes=[out32key + (k,)])
        P.op("act", lambda e, k=k: e.activation(out=out16[:, k, 0:T], in_=scr[:, k, 0:T], func=AF.Identity,
                                                scale=gam[:, k:k + 1], bias=bet[:, k:k + 1]),
             reads=[sk[k]] + list(gkeys), writes=[out16key + (k,)])


def stream_matmul_fm(P, PS, w_dram, col0, ncols, KC, rhs16, rhs_keys, T, evac, wbufs, wname, WC=256, TH=512, KT=1):
    assert ncols % 128 == 0 and KC % KT == 0
    KS = KC // KT
    c = 0
    gi = getattr(P, "_wcount", 0)
    while c < ncols:
        w = min(WC, ncols - c)
        tiles = []
        for kt in range(KT):
            wb = wbufs[gi % len(wbufs)]
            wk = (wname, gi % len(wbufs))
            gi += 1
            src = w_dram[kt * KS * 128:(kt + 1) * KS * 128, col0 + c: col0 + c + w].rearrange("(k p) c -> p k c", p=128)
            P.dma("pool", wb[:, 0:KS, 0:w], src, writes=[wk])
            tiles.append((wb, wk))
        for cc in range(w // 128):
            for half in range(T // TH):
                bank, bk = PS.next()

                def mm(e, cc=cc, half=half, bank=bank, tiles=tiles):
                    for kt, (wb, _) in enumerate(tiles):
                        for k in range(KS):
                            kk = kt * KS + k
                            ins = e.matmul(bank[:, 0:TH], lhsT=wb[:, k, cc * 128:(cc + 1) * 128],
                                           rhs=rhs16[:, kk, half * TH:(half + 1) * TH], start=(kk == 0), stop=(kk == KC - 1))
                    return ins
                P.op("pe", mm, reads=[wk for _, wk in tiles] + list(rhs_keys), writes=[bk])
                evac((c // 128) + cc, half, bank, bk)
        c += w
    P._wcount = gi


def emit_inproj(P, PS, h16, h16keys, w_in, projT, wbufs, obufs, wname="w_in"):
    state = {"i": 0}
    toks = []

    def evac(ci, half, bank, bk):
        i = state["i"]
        state["i"] += 1
        ob = obufs[i % len(obufs)]
        ok = ("projo", i % len(obufs))
        eng = "act" if i % 2 == 0 else "dve"
        if eng == "act":
            P.op("act", lambda e: e.copy(out=ob[:, :], in_=bank[:, 0:512]), reads=[bk], writes=[ok])
        else:
            P.op("dve", lambda e: e.tensor_copy(out=ob[:, :], in_=bank[:, 0:512]), reads=[bk], writes=[ok])
        toks.append(P.dma("sp", projT[ci * 128:(ci + 1) * 128, half * 512:(half + 1) * 512], ob[:, :], reads=[ok]))
    stream_matmul_fm(P, PS, w_in, 0, IN_COLS, 16, h16, h16keys, TPC, evac, wbufs, wname)
    return toks


def build_A0():
    nc = bass.Bass("TRN2", target_bir_lowering=False)
    xT = nc.dram_tensor("xT", [D, TPC], F32, kind="ExternalInput").ap()
    lng = nc.dram_tensor("lng", [128, 16], F32, kind="ExternalInput").ap()
    lnb = nc.dram_tensor("lnb", [128, 16], F32, kind="ExternalInput").ap()
    w_in = nc.dram_tensor("w_in", [D, IN_COLS], F32, kind="ExternalInput").ap()
    projT = nc.dram_tensor("projT", [IN_COLS, TPC], F32, kind="ExternalOutput").ap()
    hT = nc.dram_tensor("hT", [D, TPC], F32, kind="ExternalOutput").ap()
    P = Prog(nc)
    PS = PsumRing(P)
    ones = P.sb([128, 128], F32, "ones")
    P.op("pool", lambda e: e.memset(ones[:, :], 1.0), writes=["ones"])
    gam = load_consts(P, lng, "gam", 16)
    bet = load_consts(P, lnb, "bet", 16)
    h16 = P.sb([128, 16, TPC], BF16, "h16")
    x32 = P.sb([128, 16, 512], F32, "x32")
    scr = P.sb([128, 16, 512], F32, "scr")
    toks = []
    x32k = [("x32", k) for k in range(16)]
    for half in range(2):
        P.dma("sp", x32[:, :, :], xT[:, half * 512:(half + 1) * 512].rearrange("(k p) t -> p k t", p=128), writes=x32k)
        h16v = h16[:, :, half * 512:(half + 1) * 512]
        layernorm_fm(P, PS, x32, ("x32",), 512, gam, bet, ["gam", "bet"], ones, x32, ("x32",), h16v, ("h16", half),
                     scr, ("scr",))
        toks.append(P.dma("sp", hT[:, half * 512:(half + 1) * 512].rearrange("(k p) t -> p k t", p=128), x32[:, :, :],
                          reads=x32k))
    wbufs = [P.sb([128, 16, 256], BF16, f"wb{i}") for i in range(2)]
    obufs = [P.sb([128, 512], F32, f"ob{i}") for i in range(4)]
    h16keys = [("h16", hf, k) for hf in range(2) for k in range(16)]
    toks += emit_inproj(P, PS, h16, h16keys, w_in, projT, wbufs, obufs)
    P.emit(toks)
    return nc


NEG = -30000.0
RB = 512
NCH = RB // 128
LB = 1024


def emit_retention(P, PS, nc, io):
    qT, kT, ktm, v, cosT, sinT, costm, sintm, logit, diffT, keepT, idx1, kidx, y = io
    cst = P.sb([128, 8], F32, "r_cst")
    dT = P.sb([128, 128], F32, "r_diffT")
    kpT = P.sb([128, 128], F32, "r_keepT")
    i1 = P.sb([128, 128], F32, "r_idx1")
    maskT = P.sb([128, 128], F32, "r_maskT")
    qdec = P.sb([128, RB], F32, "r_qdec")
    P.dma("sp", cst[:, 0:1], logit[:, :], writes=["r_c0"])
    P.dma("sp", cst[:, 6:7], kidx[:, :], writes=["r_c6"])
    P.dma("sp", dT[:, :], diffT[:, :], writes=["r_dT"])
    P.dma("sp", kpT[:, :], keepT[:, :], writes=["r_kpT"])
    P.dma("sp", i1[:, :], idx1[:, :], writes=["r_i1"])
    P.op("act", lambda e: e.activation(out=cst[:, 1:2], in_=cst[:, 0:1], func=AF.Exp, scale=-1.0), reads=["r_c0"], writes=["r_c1"])
    P.op("act", lambda e: e.activation(out=cst[:, 2:3], in_=cst[:, 1:2], func=AF.Ln, bias=1.0), reads=["r_c1"], writes=["r_c2"])
    P.op("act", lambda e: e.mul(out=cst[:, 3:4], in_=cst[:, 2:3], mul=-1.0), reads=["r_c2"], writes=["r_logg"])
    P.op("act", lambda e: e.activation(out=maskT[:, :], in_=dT[:, :], func=AF.Exp, scale=cst[:, 3:4]),
         reads=["r_dT", "r_logg"], writes=["r_maskT"])
    P.op("dve", lambda e: e.tensor_tensor(out=maskT[:, :], in0=maskT[:, :], in1=kpT[:, :], op=ALU.mult),
         reads=["r_maskT", "r_kpT"], writes=["r_maskT"])
    for n in range(RB // 128):
        P.op("act", lambda e, n=n: e.activation(out=qdec[:, n * 128:(n + 1) * 128], in_=i1[:, :], func=AF.Exp, scale=cst[:, 3:4]),
             reads=["r_i1", "r_logg"], writes=[("r_qdec", n)])
    qdk = [("r_qdec", n) for n in range(RB // 128)]
    P.op("act", lambda e: e.activation(out=cst[:, 4:5], in_=cst[:, 6:7], func=AF.Exp, scale=cst[:, 3:4]),
         reads=["r_c6", "r_logg"], writes=["r_kdec"])
    P.op("act", lambda e: e.activation(out=cst[:, 5:6], in_=cst[:, 3:4], func=AF.Exp, scale=128.0),
         reads=["r_logg"], writes=["r_cdec"])
    S32 = P.sb([128, 512], F32, "r_S32")
    S16 = [P.sb([128, 512], BF16, f"r_S16_{i}") for i in range(2)]
    P.op("pool", lambda e: e.memset(S32[:, :], 0.0), writes=["r_S32"])
    for i in range(2):
        P.op("pool", lambda e, i=i: e.memset(S16[i][:, :], 0.0), writes=[("r_S16", i)])
    qrs = [P.sb([128, 2, RB], F32, f"r_qr{i}") for i in range(2)]
    krs = [P.sb([128, 2, RB], F32, f"r_kr{i}") for i in range(2)]
    ktrs = [P.sb([128, NCH, 256], F32, f"r_ktr{i}") for i in range(2)]
    vrs = [P.sb([128, NCH, 256], F32, f"r_vr{i}") for i in range(2)]
    css = [P.sb([128, RB], F32, f"r_cs{i}") for i in range(2)]
    sns = [P.sb([128, RB], F32, f"r_sn{i}") for i in range(2)]
    cstms = [P.sb([128, NCH, 128], F32, f"r_cstm{i}") for i in range(2)]
    sntms = [P.sb([128, NCH, 128], F32, f"r_sntm{i}") for i in range(2)]
    ta = P.sb([128, RB], F32, "r_ta")
    tb = P.sb([128, RB], F32, "r_tb")
    tc_ = P.sb([128, RB], F32, "r_tc")
    q16 = P.sb([128, 2, RB], BF16, "r_q16")
    qd16 = P.sb([128, 2, RB], BF16, "r_qd16")
    k16 = P.sb([128, 2, RB], BF16, "r_k16")
    kt16 = P.sb([128, NCH, 256], BF16, "r_kt16")
    v16 = P.sb([128, NCH, 256], BF16, "r_v16")
    vd16 = P.sb([128, NCH, 256], BF16, "r_vd16")
    sm16 = [P.sb([128, 128], BF16, f"r_sm16_{i}") for i in range(2)]
    yb = [P.sb([128, NCH, 256], F32, f"r_yb{i}") for i in range(2)]
    toks = []
    KS = 256 ** -0.5
    ta3 = ta[:, :].rearrange("p (n d) -> p n d", d=128)
    tb3 = tb[:, :].rearrange("p (n d) -> p n d", d=128)
    tc3 = tc_[:, :].rearrange("p (n d) -> p n d", d=128)

    def rot_fm(src, skey, cs, sn, kcs, ksn, outs, scale):
        t1, t2 = src[:, 0, :], src[:, 1, :]
        P.op("dve", lambda e: e.tensor_tensor(out=ta[:, :], in0=t1, in1=cs[:, :], op=ALU.mult), reads=[skey, kcs], writes=["r_ta"])
        P.op("pool", lambda e: e.tensor_tensor(out=tb[:, :], in0=t2, in1=sn[:, :], op=ALU.mult), reads=[skey, ksn], writes=["r_tb"])
        P.op("dve", lambda e: e.tensor_tensor(out=ta[:, :], in0=ta[:, :], in1=tb[:, :], op=ALU.subtract), reads=["r_ta", "r_tb"], writes=["r_ta"])
        P.op("pool", lambda e: e.tensor_tensor(out=tb[:, :], in0=t1, in1=sn[:, :], op=ALU.mult), reads=[skey, ksn, "r_ta"], writes=["r_tb"])
        P.op("dve", lambda e: e.tensor_tensor(out=tc_[:, :], in0=t2, in1=cs[:, :], op=ALU.mult), reads=[skey, kcs], writes=["r_tc"])
        P.op("dve", lambda e: e.tensor_tensor(out=tb[:, :], in0=tb[:, :], in1=tc_[:, :], op=ALU.add), reads=["r_tb", "r_tc"], writes=["r_tb"])
        for (o, ok, extra) in outs:
            if extra is None:
                P.op("act", lambda e, o=o: e.mul(out=o[:, 0, :], in_=ta[:, :], mul=scale), reads=["r_ta"], writes=[ok + "0"])
                P.op("act", lambda e, o=o: e.mul(out=o[:, 1, :], in_=tb[:, :], mul=scale), reads=["r_tb"], writes=[ok + "1"])
            else:
                P.op("dve", lambda e, o=o: e.tensor_tensor(out=o[:, 0, :], in0=ta[:, :], in1=qdec[:, :], op=ALU.mult), reads=["r_ta"] + qdk, writes=[ok + "0"])
                P.op("pool", lambda e, o=o: e.tensor_tensor(out=o[:, 1, :], in0=tb[:, :], in1=qdec[:, :], op=ALU.mult), reads=["r_tb"] + qdk, writes=[ok + "1"])

    def rot_tm(ktr, kktr, cst_, snt, kcst, ksnt):
        t1, t2 = ktr[:, :, 0:128], ktr[:, :, 128:256]
        P.op("dve", lambda e: e.tensor_tensor(out=ta3, in0=t1, in1=cst_[:, :, :], op=ALU.mult), reads=[kktr, kcst], writes=["r_ta"])
        P.op("pool", lambda e: e.tensor_tensor(out=tb3, in0=t2, in1=snt[:, :, :], op=ALU.mult), reads=[kktr, ksnt], writes=["r_tb"])
        P.op("dve", lambda e: e.tensor_tensor(out=ta3, in0=ta3, in1=tb3, op=ALU.subtract), reads=["r_ta", "r_tb"], writes=["r_ta"])
        P.op("act", lambda e: e.mul(out=kt16[:, :, 0:128], in_=ta3, mul=KS), reads=["r_ta"], writes=["r_kt16a"])
        P.op("pool", lambda e: e.tensor_tensor(out=tb3, in0=t1, in1=snt[:, :, :], op=ALU.mult), reads=[kktr, ksnt, "r_ta"], writes=["r_tb"])
        P.op("dve", lambda e: e.tensor_tensor(out=tc3, in0=t2, in1=cst_[:, :, :], op=ALU.mult), reads=[kktr, kcst], writes=["r_tc"])
        P.op("dve", lambda e: e.tensor_tensor(out=tb3, in0=tb3, in1=tc3, op=ALU.add), reads=["r_tb", "r_tc"], writes=["r_tb"])
        P.op("act", lambda e: e.mul(out=kt16[:, :, 128:256], in_=tb3, mul=KS), reads=["r_tb"], writes=["r_kt16b"])

    def load_block(b):
        u = b % 2
        t0 = b * RB
        P.dma("sp", qrs[u][:, :, :], qT[:, t0:t0 + RB].rearrange("(j p) t -> p j t", p=128), writes=[("r_qr", u)])
        P.dma("sp", krs[u][:, :, :], kT[:, t0:t0 + RB].rearrange("(j p) t -> p j t", p=128), writes=[("r_kr", u)])
        P.dma("sp", ktrs[u][:, :, :], ktm[t0:t0 + RB, :].rearrange("(n p) d -> p n d", p=128), writes=[("r_ktr", u)])
        P.dma("sp", vrs[u][:, :, :], v[t0:t0 + RB, :].rearrange("(n p) d -> p n d", p=128), writes=[("r_vr", u)])
        P.dma("sp", css[u][:, :], cosT[:, t0:t0 + RB], writes=[("r_cs", u)])
        P.dma("sp", sns[u][:, :], sinT[:, t0:t0 + RB], writes=[("r_sn", u)])
        P.dma("sp", cstms[u][:, :, :], costm[t0:t0 + RB, :].rearrange("(n p) d -> p n d", p=128), writes=[("r_cstm", u)])
        P.dma("sp", sntms[u][:, :, :], sintm[t0:t0 + RB, :].rearrange("(n p) d -> p n d", p=128), writes=[("r_sntm", u)])
    load_block(0)
    for b in range(SEQ // RB):
        t0 = b * RB
        u = b % 2
        if b + 1 < SEQ // RB:
            load_block(b + 1)
        rot_fm(qrs[u], ("r_qr", u), css[u], sns[u], ("r_cs", u), ("r_sn", u), [(q16, "r_q16", None), (qd16, "r_qd16", True)], 1.0)
        rot_fm(krs[u], ("r_kr", u), css[u], sns[u], ("r_cs", u), ("r_sn", u), [(k16, "r_k16", None)], KS)
        rot_tm(ktrs[u], ("r_ktr", u), cstms[u], sntms[u], ("r_cstm", u), ("r_sntm", u))
        P.op("pool", lambda e, vr=vrs[u]: e.tensor_copy(out=v16[:, :, :], in_=vr[:, :, :]), reads=[("r_vr", u)], writes=["r_v16"])
        P.op("act", lambda e, vr=vrs[u]: e.activation(out=vd16[:, :, :], in_=vr[:, :, :], func=AF.Copy, scale=cst[:, 4:5]),
             reads=[("r_vr", u), "r_kdec"], writes=["r_vd16"])
        ybuf = yb[b % 2]
        ybk = ("r_yb", b % 2)

        def stage1(n):
            cs_ = slice(n * 128, (n + 1) * 128)
            bs, bsk = PS.next()

            def mm_s(e, cs_=cs_, bs=bs):
                for j in range(2):
                    ins = e.matmul(bs[:, 0:128], lhsT=k16[:, j, cs_], rhs=q16[:, j, cs_], start=(j == 0), stop=(j == 1))
                return ins
            P.op("pe", mm_s, reads=["r_k160", "r_k161", "r_q160", "r_q161"], writes=[bsk])
            sm = sm16[n % 2]
            smk = ("r_sm", n % 2)
            P.op("dve", lambda e, sm=sm, bs=bs: e.tensor_tensor(out=sm[:, :], in0=bs[:, 0:128], in1=maskT[:, :], op=ALU.mult),
                 reads=[bsk, "r_maskT"], writes=[smk])
            bkv, bkvk = PS.next()

            def mm_kv(e, bkv=bkv, n=n):
                for j in range(2):
                    ins = e.matmul(bkv[:, j * 256:(j + 1) * 256], lhsT=kt16[:, n, j * 128:(j + 1) * 128], rhs=vd16[:, n, :],
                                   start=True, stop=True)
                return ins
            P.op("pe", mm_kv, reads=["r_kt16a", "r_kt16b", "r_vd16"], writes=[bkvk])
            return sm, smk, bkv, bkvk

        def stage2(n, st):
            sm, smk, bkv, bkvk = st
            cs_ = slice(n * 128, (n + 1) * 128)
            g = b * NCH + n
            Sin, Sink = S16[g % 2], ("r_S16", g % 2)
            Sout, Soutk = S16[(g + 1) % 2], ("r_S16", (g + 1) % 2)
            P.op("dve", lambda e, bkv=bkv: e.scalar_tensor_tensor(out=S32[:, :], in0=S32[:, :], scalar=cst[:, 5:6], in1=bkv[:, :],
                                                                 op0=ALU.mult, op1=ALU.add),
                 reads=["r_S32", "r_cdec", bkvk], writes=["r_S32"])
            P.op("pool", lambda e, Sout=Sout: e.tensor_copy(out=Sout[:, :], in_=S32[:, :]), reads=["r_S32"], writes=[Soutk])
            by, byk = PS.next()

            def mm_y(e, cs_=cs_, by=by, sm=sm, n=n, Sin=Sin):
                e.matmul(by[:, 0:256], lhsT=sm[:, :], rhs=v16[:, n, :], start=True, stop=False)
                for j in range(2):
                    ins = e.matmul(by[:, 0:256], lhsT=qd16[:, j, cs_], rhs=Sin[:, j * 256:(j + 1) * 256], start=False, stop=(j == 1))
                return ins
            P.op("pe", mm_y, reads=[smk, "r_v16", "r_qd160", "r_qd161", Sink], writes=[byk])
            P.op("act", lambda e, by=by, n=n, ybuf=ybuf: e.copy(out=ybuf[:, n, :], in_=by[:, 0:256]), reads=[byk], writes=[ybk + (n,)])
        st = stage1(0)
        for n in range(NCH):
            nxt = stage1(n + 1) if n + 1 < NCH else None
            stage2(n, st)
            st = nxt
            yield
        toks.append(P.dma("sp", y[t0:t0 + RB, :].rearrange("(n p) e -> p n e", p=128), ybuf[:, :, :],
                          reads=[ybk + (n,) for n in range(NCH)]))
    return toks


def _na_geom():
    nrows = SEQ // GRID_W
    kc = np.arange(64)
    q = np.arange(64)
    cstart = np.clip(q - 8, 0, 48)
    inwin = (kc[:, None] >= cstart[None, :]) & (kc[:, None] < cstart[None, :] + 16)
    dc = np.clip(kc[:, None] - q[None, :], -15, 15) + 15
    pats, classes, pairs = {}, [], []
    for pr in range(nrows // 2):
        r0 = 2 * pr
        base = min(min(max(r0 - 4, 0), nrows - 8), nrows - 10)
        pat = []
        for a_ in range(2):
            r = r0 + a_
            rs = min(max(r - 4, 0), nrows - 8)
            for i in range(10):
                kr = base + i
                ok = rs <= kr <= rs + 7
                pat.append((ok, kr - r + 7 if ok else 0))
        pat = tuple(pat)
        if pat not in pats:
            pats[pat] = len(classes)
            dr_t = np.zeros((128, 5, 128), np.int64)
            dc_t = np.zeros((128, 5, 128), np.int64)
            va_t = np.zeros((128, 5, 128), bool)
            for a_ in range(2):
                for i in range(10):
                    ok, drv = pat[a_ * 10 + i]
                    ps_, qs_ = slice((i % 2) * 64, (i % 2) * 64 + 64), slice(a_ * 64, a_ * 64 + 64)
                    dr_t[ps_, i // 2, qs_] = drv
                    dc_t[ps_, i // 2, qs_] = dc
                    va_t[ps_, i // 2, qs_] = inwin & ok
            classes.append((dr_t, dc_t, va_t))
        pairs.append((base, pats[pat]))
    return pairs, classes


def emit_na(P, PS, nc, io):
    qT, kT, v128, biasd, o = io
    pairs, classes = _na_geom()
    ncls = len(classes)
    q16 = P.sb([128, SEQ], BF16, "n_q16")
    k16 = P.sb([128, SEQ], BF16, "n_k16")
    va = P.sb([128, 64, 132], BF16, "n_va")
    bias = P.sb([128, ncls, 640], F32, "n_bias")
    for i in range(4):
        sl = slice(i * 2048, (i + 1) * 2048)
        P.dma("pool", q16[:, sl], qT[:, sl], writes=[("n_q16", i)], max_dma_last_dim=4096)
        P.dma("pool", k16[:, sl], kT[:, sl], writes=[("n_k16", i)], max_dma_last_dim=4096)
    qk_keys = [("n_q16", i) for i in range(4)] + [("n_k16", i) for i in range(4)]
    P.op("pool", lambda e: e.memset(va[:, :, 128:132], 1.0), writes=["n_va1"])
    for i in range(4):
        P.dma("pool", va[:, i * 16:(i + 1) * 16, 0:128], v128[:, i * 16:(i + 1) * 16, :], writes=[("n_va", i)])
    va_keys = ["n_va1"] + [("n_va", i) for i in range(4)]
    P.dma("sp", bias[:, :, :], biasd[:, :, :], writes=["n_bias"])
    st = [P.sb([128, 640], F32, f"n_st{i}") for i in range(2)]
    e16 = [P.sb([128, 640], BF16, f"n_e16_{i}") for i in range(2)]
    rc = [P.sb([128, 1], F32, f"n_rc{i}") for i in range(2)]
    ob = [P.sb([128, 8, 128], F32, f"n_ob{i}") for i in range(2)]
    toks = []
    scale = 128 ** -0.5
    for pr, (base, cls) in enumerate(pairs):
        t0 = (base // 2) * 128
        qs = slice(pr * 128, (pr + 1) * 128)
        bx, bxk = PS.next()
        by, byk = PS.next()

        def mm_s(e, t0=t0, qs=qs, bx=bx):
            for i in range(4):
                ins = e.matmul(bx[:, i * 128:(i + 1) * 128], lhsT=k16[:, t0 + i * 128:t0 + (i + 1) * 128], rhs=q16[:, qs], start=True, stop=True)
            return ins
        P.op("pe", mm_s, reads=qk_keys, writes=[bxk])
        P.op("pe", lambda e, t0=t0, qs=qs, by=by: e.matmul(by[:, 0:128], lhsT=k16[:, t0 + 512:t0 + 640], rhs=q16[:, qs], start=True, stop=True),
             reads=qk_keys, writes=[byk])
        s_, sk = st[pr % 2], ("n_st", pr % 2)
        P.op("dve", lambda e, s_=s_, bx=bx, cls=cls: e.scalar_tensor_tensor(out=s_[:, 0:512], in0=bx[:, :], scalar=scale, in1=bias[:, cls, 0:512],
                                                                            op0=ALU.mult, op1=ALU.add), reads=[bxk, "n_bias"], writes=[sk + ("a",)])
        P.op("dve", lambda e, s_=s_, by=by, cls=cls: e.scalar_tensor_tensor(out=s_[:, 512:640], in0=by[:, 0:128], scalar=scale, in1=bias[:, cls, 512:640],
                                                                            op0=ALU.mult, op1=ALU.add), reads=[byk, "n_bias"], writes=[sk + ("b",)])
        e_, ek = e16[pr % 2], ("n_e16", pr % 2)
        P.op("act", lambda e, s_=s_, e_=e_: e.activation(out=e_[:, :], in_=s_[:, :], func=AF.Exp), reads=[sk + ("a",), sk + ("b",)], writes=[ek])

        def mm_o(e, base=base, by=by, e_=e_):
            for i in range(5):
                ins = e.matmul(by[:, 128:257], lhsT=e_[:, i * 128:(i + 1) * 128], rhs=va[:, base // 2 + i, 0:129], start=(i == 0), stop=(i == 4))
            return ins
        P.op("pe", mm_o, reads=[ek] + va_keys, writes=[byk])
        rc_, rck = rc[pr % 2], ("n_rc", pr % 2)
        P.op("dve", lambda e, rc_=rc_, by=by: e.reciprocal(out=rc_[:, :], in_=by[:, 256:257]), reads=[byk], writes=[rck])
        obuf, obk = ob[(pr // 8) % 2], ("n_ob", (pr // 8) % 2)
        P.op("act", lambda e, obuf=obuf, by=by, rc_=rc_, pr=pr: e.activation(out=obuf[:, pr % 8, :], in_=by[:, 128:256], func=AF.Copy, scale=rc_[:, 0:1]),
             reads=[byk, rck], writes=[obk + (pr % 8,)])
        yield
        if pr % 8 == 7:
            g = pr // 8
            toks.append(P.dma("sp", o[g * 1024:(g + 1) * 1024, :].rearrange("(j p) d -> p j d", p=128), obuf[:, :, :],
                              reads=[obk + (i,) for i in range(8)]))
    return toks


def emit_lru(P, PS, nc, io):
    xpf, xpb, wtap, bconv, wad, wid, gb, lam, hout = io
    tap = P.sb([128, 8], F32, "l_tap")
    bc = P.sb([128, 1], F32, "l_bc")
    gbt = P.sb([128, 4], F32, "l_gb")
    lm = P.sb([128, 8], F32, "l_lm")
    wa16 = P.sb([128, 2, 128], BF16, "l_wa16")
    wi16 = P.sb([128, 2, 128], BF16, "l_wi16")
    P.dma("sp", tap[:, :], wtap[:, :], writes=["l_tap"])
    P.dma("sp", bc[:, :], bconv[:, :], writes=["l_bc"])
    P.dma("sp", gbt[:, :], gb[:, :], writes=["l_gb"])
    P.dma("sp", lm[:, 0:2], lam[:, :], writes=["l_lm0"])
    P.dma("pool", wa16[:, :, :], wad[:, :, :], writes=["l_wa16"])
    P.dma("pool", wi16[:, :, :], wid[:, :, :], writes=["l_wi16"])
    P.op("act", lambda e: e.activation(out=lm[:, 2:4], in_=lm[:, 0:2], func=AF.Exp, scale=-1.0), reads=["l_lm0"], writes=["l_lm1"])
    P.op("act", lambda e: e.activation(out=lm[:, 4:6], in_=lm[:, 2:4], func=AF.Ln, bias=1.0), reads=["l_lm1"], writes=["l_lm2"])
    P.op("act", lambda e: e.mul(out=lm[:, 6:8], in_=lm[:, 4:6], mul=-8.0), reads=["l_lm2"], writes=["l_lm3"])
    xps = [P.sb([128, LB + 3], F32, f"l_xp{i}") for i in range(2)]
    xcs = [P.sb([128, LB], F32, f"l_xc{i}") for i in range(2)]
    xc16s = [P.sb([128, LB], BF16, f"l_xc16_{i}") for i in range(2)]
    rgs = [P.sb([128, LB], F32, f"l_rg{i}") for i in range(2)]
    igs = [P.sb([128, LB], F32, f"l_ig{i}") for i in range(2)]
    hb = [P.sb([128, LB], F32, f"l_h{i}") for i in range(2)]
    toks = []
    it = 0
    NS = LB // 512
    for d in range(2):
        src = xpf if d == 0 else xpb
        for b in range(SEQ // LB):
            t0 = b * LB
            u = it % 2
            xp, xc, xc16, rg, ig = xps[u], xcs[u], xc16s[u], rgs[u], igs[u]
            kxp, kxc, kxc16 = ("l_xp", u), ("l_xc", u), ("l_xc16", u)
            rgk = [("l_rg", u, s) for s in range(NS)]
            igk = [("l_ig", u, s) for s in range(NS)]
            P.dma("sp", xp[:, :], src[:, t0:t0 + LB + 3], writes=[kxp])
            P.op("dve", lambda e, d=d, xc=xc, xp=xp: e.tensor_scalar(out=xc[:, :], in0=xp[:, 0:LB], scalar1=tap[:, 4 * d:4 * d + 1],
                                                                     scalar2=bc[:, 0:1], op0=ALU.mult, op1=ALU.add),
                 reads=[kxp, "l_tap", "l_bc"], writes=[kxc])
            for j in range(1, 4):
                P.op("dve", lambda e, d=d, j=j, xc=xc, xp=xp: e.scalar_tensor_tensor(out=xc[:, :], in0=xp[:, j:j + LB],
                                                                                     scalar=tap[:, 4 * d + j:4 * d + j + 1],
                                                                                     in1=xc[:, :], op0=ALU.mult, op1=ALU.add),
                     reads=[kxp, "l_tap", kxc], writes=[kxc])
            P.op("pool", lambda e, xc=xc, xc16=xc16: e.tensor_copy(out=xc16[:, :], in_=xc[:, :]), reads=[kxc], writes=[kxc16])
            for s_ in range(NS):
                sl = slice(s_ * 512, (s_ + 1) * 512)
                br, brk = PS.next()
                bi_, bik = PS.next()
                P.op("pe", lambda e, d=d, sl=sl, br=br, xc16=xc16: e.matmul(br[:, :], lhsT=wa16[:, d, :], rhs=xc16[:, sl], start=True, stop=True),
                     reads=["l_wa16", kxc16], writes=[brk])
                P.op("pe", lambda e, d=d, sl=sl, bi_=bi_, xc16=xc16: e.matmul(bi_[:, :], lhsT=wi16[:, d, :], rhs=xc16[:, sl], start=True, stop=True),
                     reads=["l_wi16", kxc16], writes=[bik])
                P.op("act", lambda e, d=d, sl=sl, br=br, rg=rg: e.activation(out=rg[:, sl], in_=br[:, :], func=AF.Sigmoid, bias=gbt[:, 2 * d:2 * d + 1]),
                     reads=[brk, "l_gb"], writes=[rgk[s_]])
                P.op("act", lambda e, d=d, sl=sl, bi_=bi_, ig=ig: e.activation(out=ig[:, sl], in_=bi_[:, :], func=AF.Sigmoid,
                                                                               bias=gbt[:, 2 * d + 1:2 * d + 2]),
                     reads=[bik, "l_gb"], writes=[igk[s_]])
                yield
            P.op("act", lambda e, d=d, rg=rg: e.activation(out=rg[:, :], in_=rg[:, :], func=AF.Exp, scale=lm[:, 6 + d:7 + d]),
                 reads=rgk + ["l_lm3"], writes=rgk)
            P.op("dve", lambda e, ig=ig, xc=xc: e.tensor_tensor(out=ig[:, :], in0=ig[:, :], in1=xc[:, :], op=ALU.mult), reads=igk + [kxc], writes=igk)
            P.op("pool", lambda e, xc=xc, rg=rg: e.tensor_tensor(out=xc[:, :], in0=rg[:, :], in1=rg[:, :], op=ALU.mult), reads=rgk + igk, writes=[kxc])
            P.op("act", lambda e, xc=xc: e.activation(out=xc[:, :], in_=xc[:, :], func=AF.Sqrt, scale=-1.0, bias=1.0), reads=[kxc], writes=[kxc])
            P.op("dve", lambda e, ig=ig, xc=xc: e.tensor_tensor(out=ig[:, :], in0=ig[:, :], in1=xc[:, :], op=ALU.mult), reads=igk + [kxc], writes=igk)
            h, hk = hb[it % 2], ("l_h", it % 2)
            hprev, hpk = hb[(it + 1) % 2], ("l_h", (it + 1) % 2)
            if b == 0:
                P.op("dve", lambda e, h=h, rg=rg, ig=ig: e.tensor_tensor_scan(out=h[:, :], data0=rg[:, :], data1=ig[:, :], initial=0.0,
                                                                              op0=ALU.mult, op1=ALU.add), reads=rgk + igk, writes=[hk])
            else:
                P.op("dve", lambda e, h=h, hprev=hprev, rg=rg, ig=ig: e.tensor_tensor_scan(out=h[:, :], data0=rg[:, :], data1=ig[:, :],
                                                                                           initial=hprev[:, LB - 1:LB], op0=ALU.mult, op1=ALU.add),
                     reads=rgk + igk + [hpk], writes=[hk])
            toks.append(P.dma("sp", hout[d, :, t0:t0 + LB], h[:, :], reads=[hk]))
            it += 1
            yield
    return toks


def build_B(parts=("ret", "na", "lru")):
    nc = bass.Bass("TRN2", target_bir_lowering=False)
    di = lambda n, s: nc.dram_tensor(n, s, F32, kind="ExternalInput").ap()
    do = lambda n, s: nc.dram_tensor(n, s, F32, kind="ExternalOutput").ap()
    P = Prog(nc)
    PS = PsumRing(P)
    toks = []
    gens = []
    if "ret" in parts:
        io = (di("r_qT", [256, SEQ]), di("r_kT", [256, SEQ]), di("r_ktm", [SEQ, 256]), di("r_v", [SEQ, 256]),
              di("r_cosT", [128, SEQ]), di("r_sinT", [128, SEQ]), di("r_costm", [SEQ, 128]), di("r_sintm", [SEQ, 128]),
              di("r_logit", [128, 1]), di("r_diffT", [128, 128]), di("r_keepT", [128, 128]), di("r_idx1", [128, 128]),
              di("r_kidx", [128, 1]), do("r_y", [SEQ, 256]))
        gens.append(emit_retention(P, PS.sub(range(0, 4)), nc, io))
    if "na" in parts:
        io = (di("n_qT", [128, SEQ]), di("n_kT", [128, SEQ]), di("n_v128", [128, 64, 128]), di("n_bias", [128, len(_na_geom()[1]), 640]),
              do("n_o", [SEQ, 128]))
        gens.append(emit_na(P, PS.sub(range(4, 6)), nc, io))
    if "lru" in parts:
        io = (di("l_xpf", [128, SEQ + 3]), di("l_xpb", [128, SEQ + 3]), di("l_wtap", [128, 8]), di("l_bconv", [128, 1]),
              di("l_wa", [128, 2, 128]), di("l_wi", [128, 2, 128]), di("l_gb", [128, 4]), di("l_lam", [128, 2]),
              do("l_h", [2, 128, SEQ]))
        gens.append(emit_lru(P, PS.sub(range(6, 8)), nc, io))
    while gens:
        for g in list(gens):
            try:
                next(g)
            except StopIteration as stop:
                toks += stop.value
                gens.remove(g)
    P.emit(toks)
    return nc


def _rot_tables():
    half = 128
    inv = (10000.0 ** (-np.arange(half, dtype=np.float32) / np.float32(half))).astype(np.float32)
    pos = np.arange(SEQ, dtype=np.float32)
    ang = (pos[:, None] * inv[None, :]).astype(np.float32)
    return np.cos(ang).astype(np.float32), np.sin(ang).astype(np.float32)


_CONST = {}


def _consts():
    if _CONST:
        return _CONST
    cos, sin = _rot_tables()
    _CONST["cos_tm"] = [cos, np.ascontiguousarray(cos[::-1])]
    _CONST["sin_tm"] = [sin, np.ascontiguousarray(sin[::-1])]
    _CONST["cos_fm"] = [np.ascontiguousarray(c.T) for c in _CONST["cos_tm"]]
    _CONST["sin_fm"] = [np.ascontiguousarray(c.T) for c in _CONST["sin_tm"]]
    idx = np.arange(128, dtype=np.float32)
    diff = idx[None, :] - idx[:, None]
    keep = [(diff >= 0), (diff > 0)]
    _CONST["diffT"] = [np.where(k, diff, 0.0).astype(np.float32) for k in keep]
    _CONST["keepT"] = [k.astype(np.float32) for k in keep]
    _CONST["idx1"] = np.ascontiguousarray(np.broadcast_to((idx + 1.0)[None, :], (128, 128))).astype(np.float32)
    _CONST["kidx"] = (127.0 - idx)[:, None].astype(np.float32)
    kc = np.arange(64)[:, None, None, None]
    cl = np.arange(8)[None, :, None, None]
    ki = np.arange(8)[None, None, :, None]
    q = np.arange(64)[None, None, None, :]
    cstart = np.clip(q - 8, 0, 48)
    inwin = (kc >= cstart) & (kc < cstart + 16)
    dr = ki - cl + 7 + 0 * kc + 0 * q
    dc = np.clip(kc - q, -15, 15) + 15 + 0 * cl + 0 * ki
    _CONST["na_classes"] = _na_geom()[1]
    _CONST["na_dr"] = np.broadcast_to(dr, (64, 8, 8, 64)).copy()
    _CONST["na_dc"] = np.broadcast_to(dc, (64, 8, 8, 64)).copy()
    _CONST["na_win"] = np.broadcast_to(inwin, (64, 8, 8, 64)).copy()
    return _CONST


def prep_B(projT, l, inp):
    C = _consts()
    ims = []
    for c in range(NCORES):
        hh, dd = c // 2, c % 2
        fl = (lambda a: a[:, ::-1]) if dd else (lambda a: a)
        m = {}
        qT = fl(projT[hh * 256:(hh + 1) * 256])
        kT = fl(projT[1024 + hh * 256:1024 + (hh + 1) * 256])
        vT = fl(projT[2048 + hh * 256:2048 + (hh + 1) * 256])
        m["r_qT"] = np.ascontiguousarray(qT)
        m["r_kT"] = np.ascontiguousarray(kT)
        m["r_ktm"] = np.ascontiguousarray(kT.T)
        m["r_v"] = np.ascontiguousarray(vT.T)
        m["r_cosT"], m["r_sinT"] = C["cos_fm"][dd], C["sin_fm"][dd]
        m["r_costm"], m["r_sintm"] = C["cos_tm"][dd], C["sin_tm"][dd]
        m["r_logit"] = np.full((128, 1), inp["ret_decay"][l, dd, hh], np.float32)
        m["r_diffT"], m["r_keepT"] = C["diffT"][dd], C["keepT"][dd]
        m["r_idx1"], m["r_kidx"] = C["idx1"], C["kidx"]
        m["n_qT"] = np.ascontiguousarray(projT[4096 + c * 128:4096 + (c + 1) * 128])
        m["n_kT"] = np.ascontiguousarray(projT[5120 + c * 128:5120 + (c + 1) * 128])
        vh = projT[6144 + c * 128:6144 + (c + 1) * 128]
        m["n_v128"] = np.ascontiguousarray(vh.T.reshape(64, 128, 128).transpose(1, 0, 2))
        rpb = inp["na_rpb"][l, c]
        tabs = []
        for dr_t, dc_t, va_t in C["na_classes"]:
            tabs.append(np.where(va_t, rpb[dr_t, dc_t], np.float32(NEG)).astype(np.float32).reshape(128, 640))
        m["n_bias"] = np.ascontiguousarray(np.stack(tabs, axis=1))
        x = projT[7168 + c * 128:7168 + (c + 1) * 128]
        xpf = np.zeros((128, SEQ + 3), np.float32)
        xpf[:, 2:2 + SEQ] = x
        xpb = np.zeros((128, SEQ + 3), np.float32)
        xpb[:, 1:1 + SEQ] = x[:, ::-1]
        m["l_xpf"], m["l_xpb"] = xpf, xpb
        wc = inp["w_conv"][l][:, c * 128:(c + 1) * 128]
        m["l_wtap"] = np.ascontiguousarray(np.concatenate([wc.T, wc[::-1].T], axis=1))
        m["l_bconv"] = np.ascontiguousarray(inp["b_conv"][l][c * 128:(c + 1) * 128, None])
        m["l_wa"] = np.ascontiguousarray(inp["lru_wa"][l][:, c].transpose(1, 0, 2))
        m["l_wi"] = np.ascontiguousarray(inp["lru_wi"][l][:, c].transpose(1, 0, 2))
        sl = slice(c * 128, (c + 1) * 128)
        m["l_gb"] = np.ascontiguousarray(np.stack([inp["lru_ba"][l][0, sl], inp["lru_bi"][l][0, sl],
                                                   inp["lru_ba"][l][1, sl], inp["lru_bi"][l][1, sl]], axis=1))
        m["l_lam"] = np.ascontiguousarray(inp["lru_lambda"][l][:, sl].T)
        ims.append(m)
    return ims


def post_B(results):
    yf = np.concatenate([results[2 * h]["r_y"] for h in range(4)], axis=1)
    yb = np.concatenate([results[2 * h + 1]["r_y"][::-1] for h in range(4)], axis=1)
    na = np.concatenate([results[c]["n_o"] for c in range(8)], axis=1)
    hf = np.concatenate([results[c]["l_h"][0] for c in range(8)], axis=0)
    hb = np.concatenate([results[c]["l_h"][1][:, ::-1] for c in range(8)], axis=0)
    return yf, yb, na, hf, hb


def fm_stats(P, PS, x, xk, KC, T, ones, scr, sk):
    nfeat = KC * 128
    P.op("act", lambda e: e.activation(out=scr[:, 0:KC, 0:T], in_=x[:, 0:KC, 0:T], func=AF.Square), reads=xk, writes=sk)
    b_sum, k_sum = PS.next()
    b_sq, k_sq = PS.next()

    def mm_sum(e):
        for k in range(KC):
            ins = e.matmul(b_sum[:, 0:T], lhsT=ones[:, :], rhs=x[:, k, 0:T], start=(k == 0), stop=(k == KC - 1))
        return ins

    def mm_sq(e):
        for k in range(KC):
            ins = e.matmul(b_sq[:, 0:T], lhsT=ones[:, :], rhs=scr[:, k, 0:T], start=(k == 0), stop=(k == KC - 1))
        return ins
    P.op("pe", mm_sum, reads=xk + ["ones"], writes=[k_sum])
    P.op("pe", mm_sq, reads=sk + ["ones"], writes=[k_sq])
    if not hasattr(P, "_ln_tmp"):
        P._ln_tmp = (P.sb([128, 512], F32, "ln_mean"), P.sb([128, 512], F32, "ln_rstd"))
    mean, rstd = P._ln_tmp
    mk, rk = "ln_mean", "ln_rstd"
    P.op("act", lambda e: e.mul(out=mean[:, 0:T], in_=b_sum[:, 0:T], mul=1.0 / nfeat), reads=[k_sum], writes=[mk])
    P.op("dve", lambda e: e.tensor_tensor(out=rstd[:, 0:T], in0=mean[:, 0:T], in1=mean[:, 0:T], op=ALU.mult), reads=[mk], writes=[rk])
    P.op("dve", lambda e: e.scalar_tensor_tensor(out=rstd[:, 0:T], in0=b_sq[:, 0:T], scalar=1.0 / nfeat, in1=rstd[:, 0:T],
                                                 op0=ALU.mult, op1=ALU.subtract), reads=[k_sq, rk], writes=[rk])
    P.op("dve", lambda e: e.tensor_scalar(out=rstd[:, 0:T], in0=rstd[:, 0:T], scalar1=EPS, scalar2=None, op0=ALU.add),
         reads=[rk], writes=[rk])
    P.op("act", lambda e: e.activation(out=rstd[:, 0:T], in_=rstd[:, 0:T], func=AF.Sqrt), reads=[rk], writes=[rk])
    P.op("dve", lambda e: e.reciprocal(out=rstd[:, 0:T], in_=rstd[:, 0:T]), reads=[rk], writes=[rk])
    return mean, rstd, mk, rk


def build_C(with_next_proj):
    T = 512
    nc = bass.Bass("TRN2", target_bir_lowering=False)
    di = lambda n, s: nc.dram_tensor(n, s, F32, kind="ExternalInput").ap()
    do = lambda n, s: nc.dram_tensor(n, s, F32, kind="ExternalOutput").ap()
    c_yf, c_yb, c_g = di("c_yf", [1024, TPC]), di("c_yb", [1024, TPC]), di("c_g", [1024, TPC])
    c_na, c_hf, c_hb, c_ly = di("c_na", [1024, TPC]), di("c_hf", [1024, TPC]), di("c_hb", [1024, TPC]), di("c_ly", [1024, TPC])
    c_gp, c_gb, c_h = di("c_gp", [6144, TPC]), di("c_gb", [128, 48]), di("c_h", [D, TPC])
    w_br, w_out = di("w_br", [3072, D]), di("w_out", [D, D])
    ln1g, ln1b, ln2g, ln2b = di("ln1g", [128, 16]), di("ln1b", [128, 16]), di("ln2g", [128, 16]), di("ln2b", [128, 16])
    w_f1, w_f2 = di("w_f1", [D, 2 * DFF]), di("w_f2", [DFF, D])
    h2T = do("h2T", [D, TPC])
    h2s = nc.dram_tensor("h2s", [D, TPC], F32).ap()
    if with_next_proj:
        w_in = di("w_in", [D, IN_COLS])
        projT = do("projT", [IN_COLS, TPC])
    P = Prog(nc)
    PS = PsumRing(P)
    ones = P.sb([128, 128], F32, "ones")
    P.op("pool", lambda e: e.memset(ones[:, :], 1.0), writes=["ones"])
    g1, b1 = load_consts(P, ln1g, "g1", 16), load_consts(P, ln1b, "b1", 16)
    g2, b2 = load_consts(P, ln2g, "g2", 16), load_consts(P, ln2b, "b2", 16)
    gbt = load_consts(P, c_gb, "gbt", 48)
    A = P.sb([128, 16, T], F32, "arenaA")
    B = P.sb([128, 16, T], F32, "arenaB")
    U = P.sb([128, 44 * T], BF16, "arenaU")
    U3 = U[:, :].rearrange("p (k t) -> p k t", t=T)
    h1_16 = P.sb([128, 16, T], BF16, "h1_16")
    wflat = [P.sb([128, 5632], BF16, f"wf{i}") for i in range(4)]
    wv = lambda kc, wc: [w[:, 0:kc * wc].rearrange("p (k c) -> p k c", c=wc) for w in wflat]
    sc = [P.sb([128, T], F32, f"sc{i}") for i in range(8)]
    sck = [("sc", i) for i in range(8)]
    gpt = [P.sb([128, T], F32, f"gpt{i}") for i in range(2)]
    Ak = [("A", k) for k in range(16)]
    Bk = [("B", k) for k in range(16)]
    Uk = [("U", k) for k in range(44)]
    toks = []
    for half in range(2):
        ts = slice(half * T, (half + 1) * T)
        for hh in range(4):
            rows = slice(hh * 256, (hh + 1) * 256)
            o8 = 8 * (hh % 2)
            y, yk = A[:, o8:o8 + 2, :], Ak[o8:o8 + 2]
            y2, y2k = A[:, o8 + 2:o8 + 4, :], Ak[o8 + 2:o8 + 4]
            gg, ggk = A[:, o8 + 4:o8 + 6, :], Ak[o8 + 4:o8 + 6]
            sq, sqk = A[:, o8 + 6:o8 + 8, :], Ak[o8 + 6:o8 + 8]
            P.dma("sp", y, c_yf[rows, ts].rearrange("(k p) t -> p k t", p=128), writes=yk)
            P.dma("act", y2, c_yb[rows, ts].rearrange("(k p) t -> p k t", p=128), writes=y2k)
            P.dma("sp", gg, c_g[rows, ts].rearrange("(k p) t -> p k t", p=128), writes=ggk)
            P.op("dve", lambda e, y=y, y2=y2: e.tensor_tensor(out=y, in0=y, in1=y2, op=ALU.add), reads=yk + y2k, writes=yk)
            mean, rstd, mk, rk = fm_stats(P, PS, y, yk, 2, T, ones, sq, sqk)
            P.op("act", lambda e, gg=gg: e.activation(out=gg, in_=gg, func=AF.Silu), reads=ggk, writes=ggk)
            for j in range(2):
                P.op("dve", lambda e, j=j, o8=o8: e.tensor_tensor(out=A[:, o8 + j, :], in0=A[:, o8 + j, :], in1=mean[:, 0:T], op=ALU.subtract),
                     reads=[Ak[o8 + j], mk], writes=[Ak[o8 + j]])
                P.op("dve", lambda e, j=j, o8=o8: e.tensor_tensor(out=A[:, o8 + j, :], in0=A[:, o8 + j, :], in1=rstd[:, 0:T], op=ALU.mult),
                     reads=[Ak[o8 + j], rk], writes=[Ak[o8 + j]])
                P.op("dve", lambda e, j=j, hh=hh, o8=o8: e.tensor_tensor(out=U3[:, 2 * hh + j, :], in0=A[:, o8 + j, :], in1=A[:, o8 + 4 + j, :], op=ALU.mult),
                     reads=[Ak[o8 + j], Ak[o8 + 4 + j]], writes=[Uk[2 * hh + j]])
        P.dma("pool", U3[:, 8:16, :], c_na[:, ts].rearrange("(k p) t -> p k t", p=128), writes=Uk[8:16])
        for k in range(8):
            rows = slice(k * 128, (k + 1) * 128)
            o4 = 4 * (k % 2)
            hf_, hb_, ly_, t_ = sc[o4], sc[o4 + 1], sc[o4 + 2], sc[o4 + 3]
            k0, k1, k2, k3 = sck[o4], sck[o4 + 1], sck[o4 + 2], sck[o4 + 3]
            P.dma("sp", hf_[:, :], c_hf[rows, ts], writes=[k0])
            P.dma("act", hb_[:, :], c_hb[rows, ts], writes=[k1])
            P.dma("sp", ly_[:, :], c_ly[rows, ts], writes=[k2])
            P.op("dve", lambda e, hf_=hf_, hb_=hb_: e.tensor_tensor(out=hf_[:, :], in0=hf_[:, :], in1=hb_[:, :], op=ALU.add), reads=[k0, k1], writes=[k0])
            P.op("dve", lambda e, t_=t_, ly_=ly_: e.tensor_tensor(out=t_[:, :], in0=ly_[:, :], in1=ly_[:, :], op=ALU.mult), reads=[k2], writes=[k3])
            P.op("dve", lambda e, t_=t_: e.tensor_scalar(out=t_[:, :], in0=t_[:, :], scalar1=0.044715, scalar2=1.0, op0=ALU.mult, op1=ALU.add),
                 reads=[k3], writes=[k3])
            P.op("dve", lambda e, t_=t_, ly_=ly_: e.tensor_tensor(out=t_[:, :], in0=t_[:, :], in1=ly_[:, :], op=ALU.mult), reads=[k3, k2], writes=[k3])
            P.op("act", lambda e, t_=t_: e.activation(out=t_[:, :], in_=t_[:, :], func=AF.Sigmoid, scale=1.5957691216057308), reads=[k3], writes=[k3])
            P.op("dve", lambda e, t_=t_, ly_=ly_: e.tensor_tensor(out=t_[:, :], in0=t_[:, :], in1=ly_[:, :], op=ALU.mult), reads=[k3, k2], writes=[k3])
            P.op("dve", lambda e, k=k, t_=t_, hf_=hf_: e.tensor_tensor(out=U3[:, 16 + k, :], in0=t_[:, :], in1=hf_[:, :], op=ALU.mult),
                 reads=[k3, k0], writes=[Uk[16 + k]])
        for b in range(3):
            def evac(ci, hf, bank, bk, b=b):
                gp_, gpk = gpt[ci % 2], ("gpt", ci % 2)
                rows = slice(b * 2048 + ci * 128, b * 2048 + (ci + 1) * 128)
                P.dma("sp", gp_[:, :], c_gp[rows, ts], writes=[gpk])
                P.op("act", lambda e: e.activation(out=gp_[:, :], in_=gp_[:, :], func=AF.Sigmoid, bias=gbt[:, b * 16 + ci:b * 16 + ci + 1]),
                     reads=[gpk, "gbt"], writes=[gpk])
                if b == 0:
                    P.op("dve", lambda e: e.tensor_tensor(out=A[:, ci, :], in0=bank[:, 0:T], in1=gp_[:, :], op=ALU.mult),
                         reads=[bk, gpk], writes=[Ak[ci]])
                else:
                    P.op("dve", lambda e: e.tensor_tensor(out=gp_[:, :], in0=bank[:, 0:T], in1=gp_[:, :], op=ALU.mult),
                         reads=[bk, gpk], writes=[gpk])
                    if b == 1:
                        P.op("dve", lambda e: e.tensor_tensor(out=A[:, ci, :], in0=A[:, ci, :], in1=gp_[:, :], op=ALU.add),
                             reads=[Ak[ci], gpk], writes=[Ak[ci]])
                    else:
                        P.op("dve", lambda e: e.tensor_tensor(out=U3[:, 24 + ci, :], in0=A[:, ci, :], in1=gp_[:, :], op=ALU.add),
                             reads=[Ak[ci], gpk], writes=[Uk[24 + ci]])
            stream_matmul_fm(P, PS, w_br[b * 1024:(b + 1) * 1024, :], 0, D, 8, U3[:, 8 * b:8 * b + 8, :], Uk[8 * b:8 * b + 8], T, evac,
                             wv(8, 512), "W", WC=512)
        P.dma("sp", B[:, :, :], c_h[:, ts].rearrange("(k p) t -> p k t", p=128), writes=Bk)

        def evac_o(ci, hf, bank, bk):
            P.op("dve", lambda e: e.scalar_tensor_tensor(out=B[:, ci, :], in0=B[:, ci, :], scalar=ALPHA, in1=bank[:, 0:T],
                                                         op0=ALU.mult, op1=ALU.add), reads=[bk, Bk[ci]], writes=[Bk[ci]])
        stream_matmul_fm(P, PS, w_out, 0, D, 16, U3[:, 24:40, :], Uk[24:40], T, evac_o, wv(16, 256), "W", WC=256)
        layernorm_fm(P, PS, B, ("B",), T, g1, b1, ["g1", "b1"], ones, B, ("B",), h1_16, ("h1_16",), A, ("A",))
        h1k = [("h1_16", k) for k in range(16)]

        def evac_f(ci, hf, bank, bk):
            if ci < 44:
                P.op("act", lambda e: e.activation(out=U3[:, ci, :], in_=bank[:, 0:T], func=AF.Silu), reads=[bk], writes=[Uk[ci]])
            else:
                f = ci - 44
                P.op("dve", lambda e: e.tensor_tensor(out=U3[:, f, :], in0=bank[:, 0:T], in1=U3[:, f, :], op=ALU.mult),
                     reads=[bk, Uk[f]], writes=[Uk[f]])
        stream_matmul_fm(P, PS, w_f1, 0, 2 * DFF, 16, h1_16, h1k, T, evac_f, wv(16, 256), "W", WC=256)

        def evac_2(ci, hf, bank, bk):
            P.op("dve", lambda e: e.scalar_tensor_tensor(out=B[:, ci, :], in0=B[:, ci, :], scalar=ALPHA, in1=bank[:, 0:T],
                                                         op0=ALU.mult, op1=ALU.add), reads=[bk, Bk[ci]], writes=[Bk[ci]])
        stream_matmul_fm(P, PS, w_f2, 0, D, 44, U3, Uk, T, evac_2, wv(22, 256), "W", WC=256, KT=2)
        layernorm_fm(P, PS, B, ("B",), T, g2, b2, ["g2", "b2"], ones, B, ("B",), h1_16, ("h1_16",), A, ("A",))
        toks.append(P.dma("sp", h2T[:, ts].rearrange("(k p) t -> p k t", p=128), B[:, :, :], reads=Bk))
        if with_next_proj:
            P.dma("act", h2s[:, ts].rearrange("(k p) t -> p k t", p=128), B[:, :, :], reads=Bk, writes=[("h2s", half)])
    if with_next_proj:
        h16 = U[:, 0:16 * TPC].rearrange("p (k t) -> p k t", t=TPC)
        for k in range(16):
            P.dma("pool", h16[:, k, :], h2s[k * 128:(k + 1) * 128, :], reads=[("h2s", 0), ("h2s", 1)], writes=Uk[2 * k:2 * k + 2])
        obufs = [P.sb([128, 512], F32, f"ob{i}") for i in range(2)]
        toks += emit_inproj(P, PS, h16, Uk[0:32], w_in, projT, wv(16, 256), obufs, wname="W")
    P.emit(toks)
    return nc


def prep_C(l, inp, projT, hT, yf, yb, na, hf, hb, with_next):
    yfT, ybT, naT = np.ascontiguousarray(yf.T), np.ascontiguousarray(yb.T), np.ascontiguousarray(na.T)
    fm16 = lambda v: np.ascontiguousarray(v.reshape(16, 128).T)
    shared = {
        "c_gb": np.ascontiguousarray(inp["gate_b"][l].reshape(48, 128).T),
        "w_br": np.ascontiguousarray(inp["w_branch"][l].reshape(3072, D)), "w_out": inp["w_out"][l],
        "ln1g": fm16(inp["ln1_g"][l]), "ln1b": fm16(inp["ln1_b"][l]), "ln2g": fm16(inp["ln2_g"][l]), "ln2b": fm16(inp["ln2_b"][l]),
        "w_f1": inp["w_ffn_in"][l], "w_f2": inp["w_ffn_out"][l],
    }
    if with_next:
        shared["w_in"] = inp["w_in"][l + 1]
    ims = []
    for c in range(NCORES):
        ts = slice(c * TPC, (c + 1) * TPC)
        cc = lambda a: np.ascontiguousarray(a[:, ts])
        m = dict(shared)
        m.update({"c_yf": cc(yfT), "c_yb": cc(ybT), "c_g": cc(projT[3072:4096]), "c_na": cc(naT), "c_hf": cc(hf), "c_hb": cc(hb),
                  "c_ly": cc(projT[8192:9216]), "c_gp": cc(projT[9216:15360]), "c_h": cc(hT)})
        ims.append(m)
    return ims


def _run(nc, ims):
    return run_bass_kernel_spmd(nc, ims, core_ids=list(range(NCORES))).results


def kernel(**inputs):
    inp = {k: np.asarray(v) for k, v in inputs.items()}
    x = inp["x"][0]
    fm16 = lambda v: np.ascontiguousarray(v.reshape(16, 128).T)
    ims = [{"xT": np.ascontiguousarray(x[c * TPC:(c + 1) * TPC].T), "lng": fm16(inp["ln_in_g"]), "lnb": fm16(inp["ln_in_b"]),
            "w_in": inp["w_in"][0]} for c in range(NCORES)]
    res = _run(build_A0(), ims)
    hT = np.concatenate([r["hT"] for r in res], axis=1)
    projT = np.concatenate([r["projT"] for r in res], axis=1)
    for l in range(DEPTH):
        resB = _run(build_B(), prep_B(projT, l, inp))
        yf, yb, na, hf, hb = post_B(resB)
        del resB
        nxt = l + 1 < DEPTH
        res = _run(build_C(nxt), prep_C(l, inp, projT, hT, yf, yb, na, hf, hb, nxt))
        del yf, yb, na, hf, hb
        hT = np.concatenate([r["h2T"] for r in res], axis=1)
        if nxt:
            projT = np.concatenate([r["projT"] for r in res], axis=1)
        del res
    return np.ascontiguousarray(hT.T)[None].astype(np.float32)
```
